# Optimizing a Trainium2 kernel written in Bass

```python
import jax, jax.numpy as jnp
from jax import lax
import numpy as np

D_MODEL = 1024
BATCH = 8
SEQ = 2048
DEPTH = 2
DEC_BATCH = 128
DEC_SEQ = 8
PAST_LEN = 16384
PAGE_SIZE = 128

MIX = D_MODEL
G_A = MIX // 4
G_B = MIX // 4
G_C = MIX // 4
G_D = MIX // 4
N_MIXERS = 4
HEAD_DIM = 64
A_HEADS = G_A // HEAD_DIM
CHUNK = 128
POOL_WINDOWS = (2, 4, 8, 16)
POOL_GROUPS = len(POOL_WINDOWS)
POOL_GDIM = G_B // POOL_GROUPS
POOL_BUF = max(POOL_WINDOWS) - 1
SCONV_W = 3
DCONV_W = 31
D_FF = 128 * ((8 * D_MODEL // 3 + 127) // 128)
N_ADA = 9
ALPHA = (2 * DEPTH) ** 0.25
BETA = (8 * DEPTH) ** -0.25
SPLIT_SIZES = (G_A, G_A, G_B, G_C, G_C, G_C, G_D, G_D)
IN_COLS = sum(SPLIT_SIZES)
SPLIT_POINTS = tuple(int(s) for s in np.cumsum(SPLIT_SIZES)[:-1])

kernel_name = "hymba_style_conv_pool_sgu_decoder_step"


def layer_norm(x, g, b, eps=1e-5):
    xf = x.astype(jnp.float32)
    mu = jnp.mean(xf, axis=-1, keepdims=True)
    var = jnp.mean(jnp.square(xf - mu), axis=-1, keepdims=True)
    return ((xf - mu) * lax.rsqrt(var + eps) * g + b).astype(x.dtype)


def rms_norm_groups(x, g, n_groups, eps=1e-6):
    b_, t, c = x.shape
    xf = x.astype(jnp.float32).reshape(b_, t, n_groups, c // n_groups)
    xf = xf * lax.rsqrt(jnp.mean(xf * xf, axis=-1, keepdims=True) + eps)
    return (xf.reshape(b_, t, c) * g).astype(x.dtype)


def modulate(x, shift, scale):
    return x * (1.0 + scale[:, None, :]) + shift[:, None, :]


def swiglu(x, w_gate, w_up, w_down):
    return (jax.nn.silu(x @ w_gate) * (x @ w_up)) @ w_down


def causal_dwconv(x_ext, w):
    c = w.shape[1]
    return lax.conv_general_dilated(x_ext, w[:, None, :], window_strides=(1,), padding='VALID',
                                    dimension_numbers=('NWC', 'WIO', 'NWC'), feature_group_count=c)


def sgu_mixer(u, v, ln_g, ln_b, w_s, b_s):
    b_, t, _ = u.shape
    l = CHUNK if t % CHUNK == 0 else t
    u = jax.nn.gelu(u, approximate=False)
    v = layer_norm(jax.nn.gelu(v, approximate=False), ln_g, ln_b)
    vc = v.reshape(b_, t // l, l, A_HEADS, HEAD_DIM)
    mask = jnp.tril(jnp.ones((l, l), dtype=bool))
    w = jnp.where(mask[None], w_s[:, :l, :l], 0.0)
    mixed = jnp.einsum('hts,bcshd->bcthd', w, vc) + jnp.transpose(b_s[:, :l])[:, :, None]
    y = u * mixed.reshape(b_, t, G_A)
    return y, vc[:, -1].reshape(b_, l, G_A)


def pool_mixer(xb, prefix, start_pos, pool_w, pool_scale):
    b_, t, _ = xb.shape
    ext = jnp.concatenate([prefix, xb], axis=1)
    cs = jnp.cumsum(ext.astype(jnp.float32), axis=1)
    cs = jnp.pad(cs, ((0, 0), (1, 0), (0, 0)))
    pos = start_pos + jnp.arange(t)
    outs = []
    for g, win in enumerate(POOL_WINDOWS):
        sl = slice(g * POOL_GDIM, (g + 1) * POOL_GDIM)
        s = cs[:, POOL_BUF + 1:, sl] - cs[:, POOL_BUF + 1 - win:POOL_BUF + 1 - win + t, sl]
        cnt = jnp.minimum(win, pos + 1).astype(jnp.float32)
        outs.append(s / cnt[None, :, None])
    pooled = jnp.concatenate(outs, axis=-1).astype(xb.dtype) - xb
    pooled = pooled.reshape(b_, t, POOL_GROUPS, POOL_GDIM)
    y = jnp.einsum('btgc,gcd->btgd', pooled, pool_w).reshape(b_, t, G_B) * pool_scale
    return y, ext[:, -POOL_BUF:]


def short_conv_mixer(gate_b, gate_c, h, prefix, w):
    z = gate_c * h
    ext = jnp.concatenate([prefix, z], axis=1)
    y = gate_b * causal_dwconv(ext, w)
    return y, ext[:, -(SCONV_W - 1):]


def conformer_conv_mixer(a, gt, prefix, w, b, ln_g, ln_b):
    z = a * jax.nn.sigmoid(gt)
    ext = jnp.concatenate([prefix, z], axis=1)
    h = causal_dwconv(ext, w) + b
    h = jax.nn.silu(layer_norm(h, ln_g, ln_b))
    return h, ext[:, -(DCONV_W - 1):]


def run_trunk(x, c, start_pos, pool_prefix, sconv_prefix, dconv_prefix, params):
    (ln_in_g, ln_in_b, w_ada, b_ada, ffn1_w_gate, ffn1_w_up, ffn1_w_down, w_in,
     sgu_ln_g, sgu_ln_b, sgu_w, sgu_b, pool_w, pool_scale, sconv_w, dconv_w, dconv_b,
     conv_ln_g, conv_ln_b, out_norm_g, w_out, ffn2_w_gate, ffn2_w_up, ffn2_w_down,
     post_ln_g, post_ln_b) = params
    b_ = x.shape[0]
    x = layer_norm(x, ln_in_g, ln_in_b)
    sgu_states, pool_states, sconv_states, dconv_states = [], [], [], []
    for l in range(DEPTH):
        ada = (jax.nn.silu(c) @ w_ada[l] + b_ada[l]).reshape(b_, N_ADA, D_MODEL)
        sh1, sc1, g1, sh2, sc2, g2, sh3, sc3, g3 = [ada[:, i] for i in range(N_ADA)]
        h = swiglu(modulate(x, sh1, sc1), ffn1_w_gate[l], ffn1_w_up[l], ffn1_w_down[l])
        x = layer_norm(ALPHA * x + 0.5 * (1.0 + g1)[:, None, :] * h, post_ln_g[l, 0], post_ln_b[l, 0])
        proj = modulate(x, sh2, sc2) @ w_in[l]
        u, v, xb, gate_b, gate_c, hc, glu_a, glu_g = jnp.split(proj, SPLIT_POINTS, axis=-1)
        ya, sgu_v = sgu_mixer(u, v, sgu_ln_g[l], sgu_ln_b[l], sgu_w[l], sgu_b[l])
        yb, pool_new = pool_mixer(xb, pool_prefix[l], start_pos, pool_w[l], pool_scale[l])
        yc, sconv_new = short_conv_mixer(gate_b, gate_c, hc, sconv_prefix[l], sconv_w[l])
        yd, dconv_new = conformer_conv_mixer(glu_a, glu_g, dconv_prefix[l], dconv_w[l], dconv_b[l],
                                             conv_ln_g[l], conv_ln_b[l])
        mix = jnp.concatenate([ya, yb, yc, yd], axis=-1)
        mix = rms_norm_groups(mix, out_norm_g[l], N_MIXERS) @ w_out[l]
        x = layer_norm(ALPHA * x + (1.0 + g2)[:, None, :] * mix, post_ln_g[l, 1], post_ln_b[l, 1])
        h = swiglu(modulate(x, sh3, sc3), ffn2_w_gate[l], ffn2_w_up[l], ffn2_w_down[l])
        x = layer_norm(ALPHA * x + 0.5 * (1.0 + g3)[:, None, :] * h, post_ln_g[l, 2], post_ln_b[l, 2])
        sgu_states.append(sgu_v)
        pool_states.append(pool_new)
        sconv_states.append(sconv_new)
        dconv_states.append(dconv_new)
    return (x, jnp.stack(sgu_states, 0), jnp.stack(pool_states, 0),
            jnp.stack(sconv_states, 0), jnp.stack(dconv_states, 0))


def setup_inputs(seed: int = 0) -> dict:
    key = jax.random.key(seed)
    ks = iter(jax.random.split(key, 48))

    def nrm(shape, scale=1.0):
        return jax.random.normal(next(ks), shape, jnp.float32) * scale

    def gain(shape):
        return 1.0 + nrm(shape, 0.05)

    return {
        "x_prompt": nrm((BATCH, SEQ, D_MODEL)),
        "x_sample": nrm((DEC_BATCH, DEC_SEQ, D_MODEL)),
        "state_pool": nrm((DEPTH, DEC_BATCH, POOL_BUF, G_B)),
        "state_sconv": nrm((DEPTH, DEC_BATCH, SCONV_W - 1, G_C)),
        "state_dconv": nrm((DEPTH, DEC_BATCH, DCONV_W - 1, G_D), 0.5),
        "c_prompt": nrm((BATCH, D_MODEL)),
        "c_sample": nrm((DEC_BATCH, D_MODEL)),
        "ln_in_g": gain((D_MODEL,)),
        "ln_in_b": nrm((D_MODEL,), 0.02),
        "w_ada": nrm((DEPTH, D_MODEL, N_ADA * D_MODEL), 0.1 * D_MODEL ** -0.5),
        "b_ada": nrm((DEPTH, N_ADA * D_MODEL), 0.02),
        "ffn1_w_gate": nrm((DEPTH, D_MODEL, D_FF), BETA * D_MODEL ** -0.5),
        "ffn1_w_up": nrm((DEPTH, D_MODEL, D_FF), BETA * D_MODEL ** -0.5),
        "ffn1_w_down": nrm((DEPTH, D_FF, D_MODEL), BETA * D_FF ** -0.5),
        "w_in": nrm((DEPTH, D_MODEL, IN_COLS), D_MODEL ** -0.5),
        "sgu_ln_g": gain((DEPTH, G_A)),
        "sgu_ln_b": nrm((DEPTH, G_A), 0.02),
        "sgu_w": nrm((DEPTH, A_HEADS, CHUNK, CHUNK), 0.5 * CHUNK ** -0.5),
        "sgu_b": gain((DEPTH, A_HEADS, CHUNK)),
        "pool_w": nrm((DEPTH, POOL_GROUPS, POOL_GDIM, POOL_GDIM), POOL_GDIM ** -0.5),
        "pool_scale": gain((DEPTH, G_B)),
        "sconv_w": nrm((DEPTH, SCONV_W, G_C), SCONV_W ** -0.5),
        "dconv_w": nrm((DEPTH, DCONV_W, G_D), DCONV_W ** -0.5),
        "dconv_b": nrm((DEPTH, G_D), 0.02),
        "conv_ln_g": gain((DEPTH, G_D)),
        "conv_ln_b": nrm((DEPTH, G_D), 0.02),
        "out_norm_g": gain((DEPTH, MIX)),
        "w_out": nrm((DEPTH, MIX, D_MODEL), BETA * MIX ** -0.5),
        "ffn2_w_gate": nrm((DEPTH, D_MODEL, D_FF), BETA * D_MODEL ** -0.5),
        "ffn2_w_up": nrm((DEPTH, D_MODEL, D_FF), BETA * D_MODEL ** -0.5),
        "ffn2_w_down": nrm((DEPTH, D_FF, D_MODEL), BETA * D_FF ** -0.5),
        "post_ln_g": gain((DEPTH, 3, D_MODEL)),
        "post_ln_b": nrm((DEPTH, 3, D_MODEL), 0.02),
    }


def reference(x_prompt, x_sample, state_pool, state_sconv, state_dconv, c_prompt, c_sample,
              ln_in_g, ln_in_b, w_ada, b_ada, ffn1_w_gate, ffn1_w_up, ffn1_w_down, w_in,
              sgu_ln_g, sgu_ln_b, sgu_w, sgu_b, pool_w, pool_scale, sconv_w, dconv_w, dconv_b,
              conv_ln_g, conv_ln_b, out_norm_g, w_out, ffn2_w_gate, ffn2_w_up, ffn2_w_down,
              post_ln_g, post_ln_b):
    params = (ln_in_g, ln_in_b, w_ada, b_ada, ffn1_w_gate, ffn1_w_up, ffn1_w_down, w_in,
              sgu_ln_g, sgu_ln_b, sgu_w, sgu_b, pool_w, pool_scale, sconv_w, dconv_w, dconv_b,
              conv_ln_g, conv_ln_b, out_norm_g, w_out, ffn2_w_gate, ffn2_w_up, ffn2_w_down,
              post_ln_g, post_ln_b)
    dt = x_prompt.dtype
    pool_zero = jnp.zeros((DEPTH, BATCH, POOL_BUF, G_B), dt)
    sconv_zero = jnp.zeros((DEPTH, BATCH, SCONV_W - 1, G_C), dt)
    dconv_zero = jnp.zeros((DEPTH, BATCH, DCONV_W - 1, G_D), dt)
    y_prompt, sgu_v_prompt, pool_prompt, sconv_prompt, dconv_prompt = run_trunk(
        x_prompt, c_prompt, 0, pool_zero, sconv_zero, dconv_zero, params)
    y_sample, sgu_v_sample, pool_sample, sconv_sample, dconv_sample = run_trunk(
        x_sample, c_sample, PAST_LEN, state_pool, state_sconv, state_dconv, params)
    return (y_prompt, y_sample, sgu_v_prompt, sgu_v_sample, pool_prompt, pool_sample,
            sconv_prompt, sconv_sample, dconv_prompt, dconv_sample)
```

```python
from concourse.bass_utils import run_bass_kernel_spmd
import os
import heapq
import numpy as np
import concourse.bass as bass
import concourse.mybir as mybir

F32 = mybir.dt.float32
BF16 = mybir.dt.bfloat16
ALU = mybir.AluOpType
AF = mybir.ActivationFunctionType

CENG = ['pe', 'act', 'dve', 'pool']
DEF_LAT = '0.5'; DEF_LOOK = '12'; DEF_WINDOW = '400'
ALLENG = ['pe', 'act', 'dve', 'pool', 'sp']
_DSZ = {F32: 4, BF16: 2}


def _region(ap):
    name = ap.tensor.name
    steps = ap.ap
    esz = _DSZ.get(ap.dtype, 4)
    sp = str(ap.space)
    if 'DRAM' in sp.upper() or 'HBM' in sp.upper():
        lo = ap.offset
        hi = lo + sum((c - 1) * abs(s) for s, c in steps) + 1
        return (name, 0, 1, lo * esz, hi * esz, None)
    if 'PSUM' in sp.upper():
        return (name, 0, 128, 0, 1 << 20, None)
    pstep, pcnt = steps[0]
    if pstep == 0:
        pstep = 1 << 40
    p0 = ap.offset // pstep if pstep < (1 << 40) else 0
    f0 = ap.offset - p0 * pstep if pstep < (1 << 40) else ap.offset
    free = steps[1:]
    ext = sum((c - 1) * abs(s) for s, c in free) + 1
    rows = None
    if len(free) >= 2:
        s0, c0 = free[0]
        ein = sum((c - 1) * abs(s) for s, c in free[1:]) + 1
        if 1 < c0 <= 16 and s0 > ein:
            rows = tuple(((f0 + r * s0) * esz, (f0 + r * s0 + ein) * esz) for r in range(c0))
    return (name, p0, p0 + pcnt, f0 * esz, (f0 + ext) * esz, rows)


_TSET = {AF.Gelu: 'gelu', AF.Tanh: 'gelu', AF.Silu: 'silu', AF.Sigmoid: 'sigm', AF.Sqrt: 'sqrt', AF.Ln: 'lnexp', AF.Exp: 'lnexp'}


def _nfree(ap):
    n = 1
    for s, c in ap.ap[1:]:
        n *= c
    return n


def _ovl(a, b):
    if not (a[1] < b[2] and b[1] < a[2] and a[3] < b[4] and b[3] < a[4]):
        return False
    ra, rb = a[5], b[5]
    if ra is None and rb is None:
        return True
    ia = ra if ra is not None else ((a[3], a[4]),)
    ib = rb if rb is not None else ((b[3], b[4]),)
    for (x0, x1) in ia:
        for (y0, y1) in ib:
            if x0 < y1 and y0 < x1:
                return True
    return False


def _covers(a, b):
    if not (a[1] <= b[1] and a[2] >= b[2] and a[3] <= b[3] and a[4] >= b[4]):
        return False
    ra, rb = a[5], b[5]
    if ra is None:
        return True
    ib = rb if rb is not None else ((b[3], b[4]),)
    for (y0, y1) in ib:
        if not any(x0 <= y0 and y1 <= x1 for (x0, x1) in ra):
            return False
    return True


class Op:
    __slots__ = ('eng', 'fn', 'pos', 'is_dma', 'dsem', 'dval', 'waits', 'sig', 'vc', 'prewait', 'tag',
                 'idx', 'deps', 'cost', 'xfer', 'nsucc', 'succ', 'start', 'finish', 'raw', 'tset')


class Sched:
    def __init__(self, nc, kdma=8):
        self.nc = nc
        self.ops = []
        self.wr = {}
        self.rd = {}
        self.kdma = kdma

    def _add(self, eng, fn, ins, outs, is_dma=False, cost=0.3, xfer=0.0):
        op = Op()
        op.eng = eng; op.fn = fn; op.is_dma = is_dma; op.sig = False; op.waits = []; op.prewait = None
        op.idx = len(self.ops); op.cost = cost; op.xfer = xfer; op.tset = None
        deps = {}
        raw = set()
        for ap in ins:
            r = _region(ap)
            for (wr_, o) in self.wr.get(r[0], ()):
                if _ovl(r, wr_):
                    deps[o.idx] = o; raw.add(o.idx)
        for ap in outs:
            r = _region(ap)
            for (wr_, o) in self.wr.get(r[0], ()):
                if _ovl(r, wr_):
                    deps[o.idx] = o
            for (rr, o) in self.rd.get(r[0], ()):
                if _ovl(r, rr):
                    deps[o.idx] = o
        for ap in ins:
            r = _region(ap)
            lst = self.rd.setdefault(r[0], [])
            if not is_dma:
                keep = []
                for (rr, o) in lst:
                    if o.eng == eng and (not o.is_dma) and _covers(r, rr):
                        deps[o.idx] = o
                    else:
                        keep.append((rr, o))
                lst[:] = keep
            lst.append((r, op))
        for ap in outs:
            r = _region(ap)
            wl = self.wr.setdefault(r[0], [])
            wl[:] = [(wr_, o) for (wr_, o) in wl if not _covers(r, wr_)]
            wl.append((r, op))
            rl = self.rd.get(r[0])
            if rl:
                rl[:] = [(rr, o) for (rr, o) in rl if not _covers(r, rr)]
        deps.pop(op.idx, None)
        op.deps = list(deps.values())
        op.raw = raw
        self.ops.append(op)
        return op

    def mm(self, out, lhsT, rhs, start=True, stop=True, **kw):
        n = _nfree(rhs)
        cost = max(n, 64) / 2400.0 + 0.012
        if 'tile_position' in kw:
            cost *= 0.27
        return self._add('pe', lambda e: e.matmul(out, lhsT, rhs, start=start, stop=stop, **kw),
                         [lhsT, rhs] + ([] if start else [out]), [out], cost=cost)

    def act(self, out, in_, func, bias=None, scale=None):
        ins = [in_]
        kw = {}
        if bias is not None:
            kw['bias'] = bias
            if not isinstance(bias, (int, float)):
                ins.append(bias)
        if scale is not None:
            kw['scale'] = scale
            if not isinstance(scale, (int, float)):
                ins.append(scale)
        o = self._add('act', lambda e: e.activation(out, in_, func, **kw), ins, [out],
                      cost=0.2 + _nfree(out) / 1200.0)
        o.tset = _TSET.get(func)
        return o

    def _vcost(self, eng, out, mult=1.0):
        n = _nfree(out)
        if eng == 'pool':
            return 0.3 + n * 2.2 / 1200.0
        return 0.12 + mult * n / 960.0

    def tt(self, eng, out, in0, in1, op):
        return self._add(eng, lambda e: e.tensor_tensor(out, in0, in1, op), [in0, in1], [out], cost=self._vcost(eng, out))

    def ts(self, eng, out, in0, s1, s2, op0, op1=None):
        ins = [in0] + [s for s in (s1, s2) if s is not None and not isinstance(s, (int, float))]
        if op1 is None:
            return self._add(eng, lambda e: e.tensor_single_scalar(out, in0, s1, op0), ins, [out], cost=self._vcost(eng, out))
        return self._add(eng, lambda e: e.tensor_scalar(out, in0, s1, s2, op0, op1), ins, [out], cost=self._vcost(eng, out))

    def stt(self, eng, out, in0, scalar, in1, op0, op1):
        ins = [in0, in1] + ([] if isinstance(scalar, (int, float)) else [scalar])
        return self._add(eng, lambda e: e.scalar_tensor_tensor(out, in0, scalar, in1, op0, op1), ins, [out],
                         cost=self._vcost(eng, out))

    def copy(self, eng, out, in_):
        if eng == 'act':
            return self._add('act', lambda e: e.activation(out, in_, AF.Copy), [in_], [out], cost=0.2 + _nfree(out) / 1200.0)
        return self._add(eng, lambda e: e.tensor_copy(out, in_), [in_], [out], cost=self._vcost(eng, out))

    def memset(self, eng, out, val):
        return self._add(eng, lambda e: e.memset(out, val), [], [out], cost=self._vcost(eng, out))

    def recip(self, out, in_):
        return self._add('dve', lambda e: e.reciprocal(out, in_), [in_], [out], cost=self._vcost('dve', out, 4.0))

    def generic(self, eng, fn, ins, outs, cost=0.4):
        return self._add(eng, fn, ins, outs, cost=cost)

    def dma(self, eng, out, in_, **kw):
        nbytes = 1
        for d_ in out.shape:
            nbytes *= d_
        nbytes *= max(_DSZ.get(out.dtype, 4), _DSZ.get(in_.dtype, 4))
        return self._add(eng, lambda e, sem, val: e.dma_start(out=out, in_=in_, **kw).then_inc(sem, 16),
                         [in_], [out], is_dma=True, cost=(1.2 if eng == 'pool' else 0.1), xfer=2.0 + nbytes / 150e3)

    def _schedule(self):
        ops = self.ops
        LAT = float(os.environ.get('SCHED_LAT', DEF_LAT))
        LOOK = int(os.environ.get('SCHED_LOOK', DEF_LOOK))
        if os.environ.get('NOSCHED'):
            t = 0.0
            for op in ops:
                op.start = t; op.finish = t + 1e-3; t += 1e-3
            return list(ops)
        for op in ops:
            op.nsucc = len(op.deps); op.succ = []
        for op in ops:
            for d in op.deps:
                d.succ.append(op)
        cand = {e: [] for e in ALLENG}
        rtime = {}
        for op in ops:
            if op.nsucc == 0:
                heapq.heappush(cand[op.eng], (op.idx, op)); rtime[op.idx] = 0.0
        free = {e: 0.0 for e in ALLENG}
        dma_free = 0.0
        cur_tset = [None]
        order = []
        nleft = len(ops)
        WINDOW = int(os.environ.get('SCHED_WINDOW', DEF_WINDOW))
        while nleft:
            best = None
            for e in ALLENG:
                h = cand[e]
                if not h:
                    continue
                tfree = free[e]
                pick = None; pick_t = None
                look = heapq.nsmallest(LOOK + 4 if e == 'act' else LOOK, h)
                lo_idx = look[0][0]
                for (idx, o) in look:
                    if idx - lo_idx > WINDOW:
                        break
                    rt = rtime[idx]
                    st = rt if rt > tfree else tfree
                    if e == 'act' and o.tset is not None and o.tset != cur_tset[0]:
                        st += 1.3
                    if pick is None or st < pick_t - 1e-9:
                        pick = o; pick_t = st
                    if e != 'act' and rt <= tfree:
                        break
                if best is None or pick_t < best[0] - 1e-9 or (abs(pick_t - best[0]) <= 1e-9 and pick.idx < best[1].idx):
                    best = (pick_t, pick, e)
            st, op, e = best
            h = cand[e]
            h.remove((op.idx, op)); heapq.heapify(h)
            op.start = st
            if op.is_dma:
                free[e] = st + op.cost
                xs = max(st + op.cost, dma_free)
                dma_free = xs + (op.xfer - 2.0) * 0.5
                op.finish = xs + op.xfer
            else:
                if e == 'act' and op.tset is not None:
                    cur_tset[0] = op.tset
                op.finish = st + op.cost
                free[e] = op.finish
            order.append(op)
            nleft -= 1
            for s in op.succ:
                s.nsucc -= 1
                lat = (0.0 if op.eng == 'pe' else 0.15) if (s.eng == op.eng and not op.is_dma) else LAT
                rt = op.finish + lat
                if rtime.get(s.idx, 0.0) < rt:
                    rtime[s.idx] = rt
                if s.nsucc == 0:
                    heapq.heappush(cand[s.eng], (s.idx, s))
        order.sort(key=lambda o: (o.start, o.idx))
        self.sim_end = max(o.finish for o in order)
        return order

    def _waits(self, order):
        self.streams = {e: [] for e in ALLENG}
        self.cops = {e: [] for e in CENG}
        cpos = {e: 0 for e in CENG}
        vcs = {e: {c: 0 for c in CENG} for e in ALLENG}
        dknown = {e: {} for e in ALLENG}
        ndma = {e: 0 for e in ALLENG}
        for op in order:
            eng = op.eng
            vc = vcs[eng]; dk = dknown[eng]
            best = {}; dmab = {}
            for d in op.deps:
                if d.is_dma:
                    if d.dsem not in dmab or d.dval > dmab[d.dsem].dval:
                        dmab[d.dsem] = d
                else:
                    a = d.eng
                    if a == eng and eng == 'pe':
                        continue
                    if a not in best or d.pos > best[a].pos:
                        best[a] = d
            for key, d in dmab.items():
                if dk.get(key, 0) >= d.dval:
                    continue
                op.waits.append(('dma', key, d.dval))
                dk[key] = d.dval
                for c in CENG:
                    if d.vc[c] > vc[c]:
                        vc[c] = d.vc[c]
            for a, d in best.items():
                if vc[a] >= d.pos:
                    continue
                op.waits.append(('eng', a, d.pos)); d.sig = True
                for c in CENG:
                    if d.vc[c] > vc[c]:
                        vc[c] = d.vc[c]
                if d.pos > vc[a]:
                    vc[a] = d.pos
            if op.is_dma:
                i = ndma[eng]; ndma[eng] += 1
                op.dsem = (eng, i % self.kdma)
                op.dval = 16 * (i // self.kdma + 1)
                if i >= self.kdma:
                    op.prewait = (op.dsem, op.dval - 16)
                    if dk.get(op.dsem, 0) < op.dval - 16:
                        dk[op.dsem] = op.dval - 16
                op.pos = None
                op.vc = dict(vc)
            else:
                cpos[eng] += 1
                op.pos = cpos[eng]
                self.cops[eng].append(op)
                op.vc = dict(vc)
                op.vc[eng] = op.pos
            self.streams[eng].append(op)
        self.ndma = ndma

    def emit(self, es):
        nc = self.nc
        order = self._schedule()
        self._waits(order)
        csem = {e: es.enter_context(nc.semaphore('s_' + e)) for e in CENG}
        dsem = {}
        for e in ALLENG:
            for k in range(min(self.kdma, self.ndma[e])):
                dsem[(e, k)] = es.enter_context(nc.semaphore('d_%s%d' % (e, k)))
        count = {}
        for e in CENG:
            n = 0
            for op in self.cops[e]:
                if op.sig:
                    n += 1
                count[(e, op.pos)] = n
        self.count = count
        streams = self.streams
        kd = self.kdma
        nd = self.ndma

        def run(engname, eng):
            for op in streams[engname]:
                if op.prewait is not None:
                    eng.wait_ge(dsem[op.prewait[0]], op.prewait[1])
                for w in op.waits:
                    if w[0] == 'dma':
                        eng.wait_ge(dsem[w[1]], w[2])
                    else:
                        eng.wait_ge(csem[w[1]], count[(w[1], w[2])])
                if op.is_dma:
                    op.fn(eng, dsem[op.dsem], op.dval)
                else:
                    ins = op.fn(eng)
                    if op.sig:
                        ins.then_inc(csem[engname], 1)
            n = nd[engname]
            for k in range(min(kd, n)):
                cnt = (n - k + kd - 1) // kd
                eng.wait_ge(dsem[(engname, k)], 16 * cnt)

        block = es.enter_context(nc.Block())

        @block.tensor
        def _(e):
            run('pe', e)

        @block.scalar
        def _(e):
            run('act', e)

        @block.vector
        def _(e):
            run('dve', e)

        @block.gpsimd
        def _(e):
            run('pool', e)

        @block.sync
        def _(e):
            run('sp', e)

import contextlib

D = 1024; KC = 8; FF = 2816; NFC = 22; DEPTH = 2
NT = 2176; NPR = 2048; NSM = 128
ALPHA = (2 * DEPTH) ** 0.25
GROUPS = [(0, 4), (4, 4), (8, 2), (10, 4), (14, 4), (18, 4)]
FFN_BLOCKS = [(0, 512), (512, 512), (1024, 512), (1536, 512), (2048, 128)]
MIX_BLOCKS = [(i * 256, 256) for i in range(8)] + [(2048, 128)]
SLOT = 12288
PIECE = 4096

DEFAULT_FLAGS = 'castdve,rescdve,subdve,lngdve,bdve'

def _const_layout():
    off = {}
    n = 0
    def add(name, k):
        nonlocal n
        off[name] = n; n += k
    add('ln_in_g', 8); add('ln_in_b', 8)
    add('post_g', 48); add('post_b', 48)
    add('out_norm_g', 16); add('pool_scale', 4)
    add('sconv_w', 12); add('dconv_w', 124); add('dconv_b', 4)
    add('conv_ln_g', 4); add('conv_ln_b', 4)
    add('b_ada', 144)
    add('invcnt', 32)
    add('idn32', 32)
    return off, n
COFF, NCONST = _const_layout()


def build_program(nlayers=DEPTH, stop=None):
    import os as _os2
    FL = set(_os2.environ.get('KFLAGS', DEFAULT_FLAGS).split(','))
    nc = bass.Bass("TRN2", target_bir_lowering=False)
    dt_in = lambda name, shape: nc.dram_tensor(name, shape, F32, kind="ExternalInput").ap()
    dt_out = lambda name, shape: nc.dram_tensor(name, shape, F32, kind="ExternalOutput").ap()
    xT = dt_in("xT", [8, 128, NT])
    cT = dt_in("cT", [128, 8, 17])
    consts = dt_in("consts", [128, NCONST])
    w_ada = dt_in("w_ada", [DEPTH, D, 9 * D])
    wg = [dt_in("ffn1_w_gate", [DEPTH, D, FF]), dt_in("ffn2_w_gate", [DEPTH, D, FF])]
    wu = [dt_in("ffn1_w_up", [DEPTH, D, FF]), dt_in("ffn2_w_up", [DEPTH, D, FF])]
    wd = [dt_in("ffn1_w_down", [DEPTH, FF, D]), dt_in("ffn2_w_down", [DEPTH, FF, D])]
    w_in = dt_in("w_in", [DEPTH, D, 2048])
    w_out = dt_in("w_out", [DEPTH, D, D])
    sgu_wT = dt_in("sgu_wT", [DEPTH, 4, 128, 128])
    sgu_b = dt_in("sgu_b", [DEPTH, 4, 128])
    sgu_ln = dt_in("sgu_ln", [DEPTH, 2, 256])
    pool_w = dt_in("pool_w", [DEPTH, 4, 64, 64])
    masks = dt_in("masks", [2, 128, 128])
    st_pool = dt_in("st_pool", [DEPTH, 128, 2, 16, 15])
    st_sconv = dt_in("st_sconv", [DEPTH, 128, 2, 16, 2])
    st_dconv = dt_in("st_dconv", [DEPTH, 128, 2, 16, 30])

    yT = dt_out("yT", [8, 128, NT])
    o_sgu_p = dt_out("o_sgu_p", [DEPTH, 128, 256])
    o_sgu_s = dt_out("o_sgu_s", [DEPTH, 128, 256])
    o_pool_p = dt_out("o_pool_p", [DEPTH, 128, 2, 15])
    o_pool_s = dt_out("o_pool_s", [DEPTH, 128, 2, 16, 15])
    o_sconv_p = dt_out("o_sconv_p", [DEPTH, 128, 2, 2])
    o_sconv_s = dt_out("o_sconv_s", [DEPTH, 128, 2, 16, 2])
    o_dconv_p = dt_out("o_dconv_p", [DEPTH, 128, 2, 30])
    o_dconv_s = dt_out("o_dconv_s", [DEPTH, 128, 2, 16, 30])

    o_ada = dt_out("o_ada", [128, DEPTH * 72 * 17]) if stop is not None else None
    es = contextlib.ExitStack()
    with es:
        S = Sched(nc)
        sb = lambda name, shape, dt=F32: es.enter_context(nc.sbuf_tensor(name, shape, dt))
        X = sb("X", [128, 8, NT])
        XM = sb("XM", [128, 8, NT], BF16)
        W = sb("W", [128, 2 * SLOT], BF16)
        SQ = sb("SQ", [128, 8, 256], BF16)
        MR = sb("MR", [128, 2, 256])
        ADA = sb("ADA", [128, DEPTH, 72, 17])
        CST = sb("CST", [128, NCONST])
        BC = sb("BC", [128, DEPTH * 3 + 1, 8])
        AB = sb("AB", [128, DEPTH * 3 + 1, 8])
        EPS = sb("EPS", [128, 2])
        CSB = sb("CSB", [128, 8, 17], BF16)
        ONESA = sb("ONESA", [128, 128], BF16)
        ONESB = sb("ONESB", [128, 128], BF16)
        WTP = sb("WTP", [128, 4, 128], BF16)
        WTS = sb("WTS", [128, 4, 128], BF16)
        BTP = sb("BTP", [128, 2, 128])
        LNG = sb("LNG", [128, 2, 256])
        PWB = sb("PWB", [128, 2, 128], BF16)
        XS = sb("XS", [128, 128])
        DG = sb("DG", [128, 62, 32], BF16)
        NSCR = 6352
        SCR = sb("SCR", [128, NSCR])

        def carve(w0, shape, dt=F32):
            nel = 1
            for d_ in shape:
                nel *= d_
            nw = nel if dt == F32 else nel // 2
            a = SCR[:, w0:w0 + nw]
            if dt != F32:
                a = a.bitcast(dt)
            if len(shape) == 1:
                return a
            if len(shape) == 2:
                return a.rearrange("p (a b) -> p a b", b=shape[1])
            return a.rearrange("p (a b c) -> p a b c", b=shape[1], c=shape[2])
        Y = carve(0, [8, 256]); U = carve(2048, [2, 256]); H = carve(2560, [2, 256])
        VG0 = carve(3072, [256]); VB0 = carve(3328, [256], BF16); PB = carve(3456, [2, 256], BF16)
        SA = carve(3712, [368]); SBB = carve(4080, [368]); EB = carve(4448, [2, 368])
        EZ = carve(5184, [2, 272]); ED = carve(5728, [2, 608], BF16); BST0 = carve(6336, [16])
        SG = carve(0, [2, 512], BF16); ACTB = carve(512, [2, 4, 512], BF16)
        STG = carve(0, [512]); MSK = carve(512, [2, 128])
        HB = SQ[:, 0:4, :].rearrange("p (a c) n -> p a c n", c=2)
        ps = [es.enter_context(nc.psum_tensor("ps%d" % i, [128, 512], F32)) for i in range(8)]
        hb = [ps[i // 2][:, (i % 2) * 256:(i % 2) * 256 + 256] for i in range(16)]

        cc = lambda name, j: CST[:, COFF[name] + j:COFF[name] + j + 1]

        for c in range(8):
            S.dma('sp', X[:, c, :], xT[c])
        S.dma('sp', CST[:, :], consts)
        S.dma('sp', STG[:, 0:136], cT.rearrange("p k s -> p (k s)"))
        S.memset('dve', EPS[:, 0:1], 1e-5)
        S.memset('dve', EPS[:, 1:2], 1e-6)
        S.memset('dve', ONESA[:, :], 1.0 / 1024)
        S.memset('dve', ONESB[:, :], 1.0 / 256)
        S.act(CSB[:, :, :].rearrange("p k s -> p (k s)"), STG[:, 0:136], AF.Silu)

        WA = sb("WA", [128, 2, 8, 128], BF16)
        pcnt = [0]

        def ln_cols(k):
            if k == 0:
                return COFF['ln_in_g'], COFF['ln_in_b']
            return COFF['post_g'] + (k - 1) * 8, COFF['post_b'] + (k - 1) * 8
        nln = 3 * nlayers

        def ada_post(l, i):
            sl = ADA[:, l, (3 * i + 1) * 8:(3 * i + 2) * 8, :]
            S.ts('dve', sl, sl, 1.0, None, ALU.add)
            gl = ADA[:, l, (3 * i + 2) * 8:(3 * i + 3) * 8, :]
            if i == 1:
                S.ts('dve', gl, gl, 1.0, None, ALU.add)
            else:
                S.ts('dve', gl, gl, 1.0, 0.5, ALU.add, ALU.mult)
            k = 3 * l + i
            gcol, bcol = ln_cols(k)
            S.tt('dve', BC[:, k, :], CST[:, bcol:bcol + 8], ADA[:, l, (3 * i + 1) * 8:(3 * i + 2) * 8, 0], ALU.mult)
            S.tt('dve', BC[:, k, :], BC[:, k, :], ADA[:, l, (3 * i) * 8:(3 * i + 1) * 8, 0], ALU.add)

        def ada_chunk(l, j, Wt, jj, banks):
            pt = ps[banks[pcnt[0] % len(banks)]][:, 0:17]; pcnt[0] += 1
            for k in range(8):
                S.mm(pt, Wt[:, k, jj * 128:(jj + 1) * 128], CSB[:, k, :], start=(k == 0), stop=(k == 7))
            S.act(ADA[:, l, j, :], pt, AF.Identity, bias=cc('b_ada', l * 72 + j))

        if stop != 'setup':
            wv0 = w_ada[0].rearrange("(k p) n -> p k n", p=128)
            for n in range(6):
                reg = (n % 6) * PIECE
                Wt = W[:, reg:reg + PIECE].rearrange("p (k n) -> p k n", n=512)
                S.dma('pool', Wt, wv0[:, :, n * 512:(n + 1) * 512])
                for jj in range(4):
                    ada_chunk(0, n * 4 + jj, Wt, jj, list(range(8)))
            for k in range(nln + 1):
                gcol, bcol = ln_cols(k)
                S.ts('dve', AB[:, k, :], CST[:, bcol:bcol + 8], ALPHA, None, ALU.mult)
            ada_post(0, 0)

        ada_todo = [(0, j) for j in range(24, 72)]
        for l in range(1, nlayers):
            for j in range(72):
                ada_todo.append((l, j))
        ada_state = {'i': 0}

        def ada_pump(nmax):
            for _ in range(nmax):
                if ada_state['i'] >= len(ada_todo):
                    return
                l, j = ada_todo[ada_state['i']]
                slot = ada_state['i'] % 2
                ada_state['i'] += 1
                wvl = w_ada[l].rearrange("(k p) n -> p k n", p=128)
                S.dma('pool', WA[:, slot, :, :], wvl[:, :, j * 128:(j + 1) * 128])
                ada_chunk(l, j, WA[:, slot, :, :], 0, [4, 5, 6, 7])
                if j % 24 == 23:
                    ada_post(l, j // 24)

        lnp = [0]

        import os as _os
        _lnstep = int(_os.environ.get('LNSTEP', '9'))

        def ln_block(k, t0, n):
            gcol, bcol = ln_cols(k)
            final = (k == nln)
            sample = (t0 >= NPR)
            tok = slice(t0, t0 + n)
            merge = 'lnmerge' in FL
            Xb = X[:, :, tok]
            if merge:
                S.act(SQ[:, :, 0:n], Xb, AF.Square)
                S.copy('dve', XM[:, :, tok], Xb)
            else:
                for c in range(8):
                    S.act(SQ[:, c, 0:n], X[:, c, tok], AF.Square)
                    S.copy('dve', XM[:, c, tok], X[:, c, tok])
            pm = ps[6][:, 0:n]
            pq = ps[7][:, 0:n]
            for c in range(8):
                S.mm(pm, ONESA[:, :], XM[:, c, tok], start=(c == 0), stop=(c == 7))
            for c in range(8):
                S.mm(pq, ONESA[:, :], SQ[:, c, 0:n], start=(c == 0), stop=(c == 7))
            mean = MR[:, 0, 0:n]; rstd = MR[:, 1, 0:n]
            S.copy('act', mean, pm)
            S.act(rstd, pm, AF.Square)
            S.tt('dve', rstd, pq, rstd, ALU.subtract)
            S.act(rstd, rstd, AF.Ln, bias=EPS[:, 0:1], scale=1.0)
            S.act(rstd, rstd, AF.Exp, scale=-0.5)
            if merge:
                S.tt('dve', Xb, Xb, mean.unsqueeze(1).to_broadcast([128, 8, n]), ALU.subtract)
            for c in range(8):
                xc = X[:, c, tok]
                if not merge:
                    S.tt('dve', xc, xc, mean, ALU.subtract)
                S.stt('dve', xc, xc, CST[:, gcol + c:gcol + c + 1], rstd, ALU.mult, ALU.mult)
                if final:
                    if not merge:
                        S.act(xc, xc, AF.Identity, bias=CST[:, bcol + c:bcol + c + 1], scale=1.0)
                    continue
                l, i = divmod(k, 3)
                if not sample:
                    S.act(XM[:, c, tok], xc, AF.Identity, bias=BC[:, k, c:c + 1],
                          scale=ADA[:, l, (3 * i + 1) * 8 + c, 0:1])
                else:
                    S.act(XS[:, 0:n], xc, AF.Identity, bias=CST[:, bcol + c:bcol + c + 1], scale=1.0)
                    xs3 = XS[:, 0:n].rearrange("p (s t) -> p s t", t=8)
                    scb = ADA[:, l, (3 * i + 1) * 8 + c, 1:17].unsqueeze(2).to_broadcast([128, 16, 8])
                    shb = ADA[:, l, (3 * i) * 8 + c, 1:17].unsqueeze(2).to_broadcast([128, 16, 8])
                    S.tt('dve', xs3, xs3, scb, ALU.mult)
                    S.tt('dve', XM[:, c, tok].rearrange("p (s t) -> p s t", t=8), xs3, shb, ALU.add)
                if not merge:
                    S.ts('dve', xc, xc, ALPHA, AB[:, k, c:c + 1], ALU.mult, ALU.add)
            if merge:
                if final:
                    S.tt('dve', Xb, Xb, CST[:, bcol:bcol + 8].unsqueeze(2).to_broadcast([128, 8, n]), ALU.add)
                else:
                    S.stt('dve', Xb, Xb, ALPHA, AB[:, k, :].unsqueeze(2).to_broadcast([128, 8, n]), ALU.mult, ALU.add)

        def accumulate(pt, l, i, oc, t0, n):
            xc = X[:, oc, t0:t0 + n]
            j = (3 * i + 2) * 8 + oc
            if t0 < NPR:
                S.stt('dve', xc, pt, ADA[:, l, j, 0:1], xc, ALU.mult, ALU.add)
            else:
                gbc = ADA[:, l, j, 1:17].unsqueeze(2).to_broadcast([128, 16, 8])
                xs3 = XS[:, 0:n].rearrange("p (s t) -> p s t", t=8)
                S.tt('dve', xs3, pt.rearrange("p (s t) -> p s t", t=8), gbc, ALU.mult)
                S.tt('dve', xc, xc, XS[:, 0:n], ALU.add)

        slot_ctr = [0]
        gu_ctr = [0]
        dn_ctr = [0]

        def ffn(l, which):
            i = 0 if which == 0 else 2
            k_next = 3 * l + i + 1
            wgl = wg[which][l].rearrange("(k p) n -> p k n", p=128)
            wul = wu[which][l].rearrange("(k p) n -> p k n", p=128)
            wdl = wd[which][l].rearrange("(j p) n -> p j n", p=128)
            items = []
            slots = {}

            def load(gi):
                f0, G = GROUPS[gi]
                s = slot_ctr[0] % 2; slot_ctr[0] += 1
                base = s * SLOT
                Wg_ = W[:, base:base + 4096].rearrange("p (k n) -> p k n", n=512)
                Wu_ = W[:, base + 4096:base + 8192].rearrange("p (k n) -> p k n", n=512)
                Wd_ = W[:, base + 8192:base + 12288].rearrange("p (j n) -> p j n", n=1024)
                S.dma('pool', Wg_[:, :, 0:G * 128], wgl[:, :, f0 * 128:(f0 + G) * 128])
                S.dma('pool', Wu_[:, :, 0:G * 128], wul[:, :, f0 * 128:(f0 + G) * 128])
                S.dma('pool', Wd_[:, 0:G, :], wdl[:, f0:f0 + G, :])
                slots[gi] = (Wg_, Wu_, Wd_)

            def gu(gi, bi, par):
                f0, G = GROUPS[gi]
                t0, n = FFN_BLOCKS[bi]
                Wg_, Wu_, Wd_ = slots[gi]
                for j in range(G):
                    q = gu_ctr[0] % 2; gu_ctr[0] += 1
                    pg = ps[2 * q][:, 0:n]; pu = ps[2 * q + 1][:, 0:n]
                    for k in range(8):
                        S.mm(pg, Wg_[:, k, j * 128:(j + 1) * 128], XM[:, k, t0:t0 + n], start=(k == 0), stop=(k == 7))
                    for k in range(8):
                        S.mm(pu, Wu_[:, k, j * 128:(j + 1) * 128], XM[:, k, t0:t0 + n], start=(k == 0), stop=(k == 7))
                    S.act(SG[:, q, 0:n], pg, AF.Silu)
                    S.tt('dve', ACTB[:, par, j, 0:n], SG[:, q, 0:n], pu, ALU.mult)

            def down(gi, bi, par):
                f0, G = GROUPS[gi]
                t0, n = FFN_BLOCKS[bi]
                Wg_, Wu_, Wd_ = slots[gi]
                for oc in range(8):
                    pd = ps[4 + dn_ctr[0] % 4][:, 0:n]; dn_ctr[0] += 1
                    for j in range(G):
                        S.mm(pd, Wd_[:, j, oc * 128:(oc + 1) * 128], ACTB[:, par, j, 0:n], start=(j == 0), stop=(j == G - 1))
                    accumulate(pd, l, i, oc, t0, n)
                if gi == len(GROUPS) - 1:
                    for (m0, mn) in MIX_BLOCKS:
                        if t0 <= m0 < t0 + n:
                            ln_block(k_next, m0, mn)

            load(0)
            prev = None
            par = 0
            for gi in range(len(GROUPS)):
                if gi + 1 < len(GROUPS):
                    pass
                for bi in range(len(FFN_BLOCKS)):
                    if bi == 0 and gi + 1 < len(GROUPS):
                        pending_load = gi + 1
                    else:
                        pending_load = None
                    gu(gi, bi, par)
                    if prev is not None:
                        down(*prev)
                    if pending_load is not None:
                        load(pending_load)
                    prev = (gi, bi, par)
                    par ^= 1
                    if l == 0:
                        ada_pump(2 if which == 0 else 3)
            down(*prev)
            if l == 0 and which == 1:
                ada_pump(1000)

        hbp = [0]

        def next_hb():
            r = ps[hbp[0] % 6][:, 0:256]; hbp[0] += 1
            return r

        def mixer(l):
            k_next = 3 * l + 2
            S.dma('sp', MSK[:, 0, :], masks[0])
            S.dma('sp', MSK[:, 1, :], masks[1])
            S.dma('sp', STG[:, :].rearrange("p (h t) -> p h t", t=128), sgu_wT[l].rearrange("h s t -> s h t"))
            for h in range(4):
                S.tt('dve', WTP[:, h, :], STG[:, h * 128:(h + 1) * 128], MSK[:, 0, :], ALU.mult)
            for h in range(4):
                for sq in range(16):
                    src = sgu_wT[l, h, 0:8, 0:8].unsqueeze(1).to_broadcast([8, 16, 8])
                    S.dma('sp', STG[sq * 8:(sq + 1) * 8, h * 128:(h + 1) * 128].rearrange("p (a b) -> p a b", b=8), src)
            for h in range(4):
                S.tt('dve', WTS[:, h, :], STG[:, h * 128:(h + 1) * 128], MSK[:, 1, :], ALU.mult)
            for c in range(2):
                for hh in range(2):
                    S.dma('sp', BTP[hh * 64:(hh + 1) * 64, c, :], sgu_b[l, 2 * c + hh:2 * c + hh + 1, :].to_broadcast([64, 128]))
            S.dma('sp', LNG[:, 0, :], sgu_ln[l, 0:1, :].to_broadcast([128, 256]))
            S.dma('sp', LNG[:, 1, :], sgu_ln[l, 1:2, :].to_broadcast([128, 256]))
            S.memset('dve', STG[:, 0:256], 0.0)
            for g in range(4):
                c, hh = divmod(g, 2)
                S.dma('sp', STG[hh * 64:(hh + 1) * 64, c * 128 + hh * 64:c * 128 + hh * 64 + 64], pool_w[l, g])
            S.copy('dve', PWB[:, :, :].rearrange("p c n -> p (c n)"), STG[:, 0:256])
            for c in range(2):
                for kk in range(31):
                    S.act(DG[:, c * 31 + kk, :], CST[:, COFF['idn32']:COFF['idn32'] + 32], AF.Identity,
                          scale=cc('dconv_w', (l * 2 + c) * 31 + kk), bias=0.0)
            wi = w_in[l].rearrange("(k p) n -> p k n", p=128)
            wo = w_out[l].rearrange("(k p) n -> p k n", p=128)
            Wp = [W[:, q * PIECE:(q + 1) * PIECE].rearrange("p (k n) -> p k n", n=512) for q in range(6)]
            for q in range(4):
                S.dma('pool', Wp[q], wi[:, :, q * 512:(q + 1) * 512])
            for q in range(2):
                S.dma('pool', Wp[4 + q], wo[:, :, q * 512:(q + 1) * 512])

            def wcol(oc):
                q, r = divmod(oc, 4)
                return lambda k: Wp[q][:, k, r * 128:(r + 1) * 128]

            def proj(oc, t0, n):
                pt = next_hb()[:, 0:n]
                wc = wcol(oc)
                for k in range(8):
                    S.mm(pt, wc(k), XM[:, k, t0:t0 + n], start=(k == 0), stop=(k == 7))
                return pt

            def front(bi, t0, n):
                sample = t0 >= NPR
                NS, T = (16, 8) if sample else (1, n)
                v3 = lambda ap: ap.rearrange("p (s t) -> p s t", t=T)

                def ext(buf, c, P):
                    L = P + T
                    return buf[:, c, 0:NS * L].rearrange("p (s t) -> p s t", t=L)
                if sample:
                    for c in range(2):
                        S.dma('pool', ext(ED, c, 30)[:, :, 0:30], st_dconv[l, :, c])
                        S.dma('sp', o_dconv_s[l, :, c, :, 0:22], st_dconv[l, :, c, :, 8:30])
                elif bi == 0:
                    for c in range(2):
                        S.memset('pool', ED[:, c, 0:30], 0.0)
                for c in range(2):
                    E = ext(ED, c, 30)
                    pt = proj(14 + c, t0, n)
                    S.act(H[:, c, 0:n], pt, AF.Sigmoid)
                    pt = proj(12 + c, t0, n)
                    S.tt('dve', E[:, :, 30:30 + T], v3(H[:, c, 0:n]), v3(pt), ALU.mult)
                    if sample:
                        S.tt('dve', XS[:, 0:n], H[:, c, 0:n], pt, ALU.mult)
                        S.dma('sp', o_dconv_s[l, :, c, :, 22:30], XS[:, 0:n].rearrange("p (s t) -> p s t", t=8))
                    elif bi == 7:
                        S.tt('dve', XS[:, 64:94], H[:, c, n - 30:n], pt[:, n - 30:n], ALU.mult)
                        S.dma('sp', o_dconv_p[l, :, c], XS[:, 64:94])
                for c in range(2):
                    E = ext(ED, c, 30)
                    pc = next_hb()[:, 0:n]
                    for kk in range(31):
                        for g in range(4):
                            S.mm(v3(pc[32 * g:32 * g + 32, :]), DG[32 * g:32 * g + 32, c * 31 + kk, :],
                                 E[32 * g:32 * g + 32, :, kk:kk + T], start=(kk == 0), stop=(kk == 30),
                                 tile_position=(32 * g, 32 * g))
                    S.act(H[:, c, 0:n], pc, AF.Identity, bias=cc('dconv_b', l * 2 + c), scale=1.0)
                BE = 'dve' if 'bdve' in FL else 'pool'
                if sample:
                    for c in range(2):
                        S.dma('sp', ext(EB, c, 15)[:, :, 0:15], st_pool[l, :, c])
                elif bi == 0:
                    for c in range(2):
                        S.memset(BE, EB[:, c, 0:15], 0.0)
                for c in range(2):
                    pt = proj(4 + c, t0, n)
                    S.copy('act', ext(EB, c, 15)[:, :, 15:15 + T], v3(pt))
                L = 15 + T
                for c in range(2):
                    E = ext(EB, c, 15)
                    A = SA[:, 0:NS * L].rearrange("p (s t) -> p s t", t=L)
                    Bv = SBB[:, 0:NS * L].rearrange("p (s t) -> p s t", t=L)
                    xb_new = E[:, :, 15:L]
                    pbv = v3(PB[:, c, 0:n])
                    S.tt(BE, A[:, :, 1:L], E[:, :, 1:L], E[:, :, 0:L - 1], ALU.add)
                    if c == 0:
                        S.tt(BE, Bv[64:128, :, 3:L], A[64:128, :, 3:L], A[64:128, :, 1:L - 2], ALU.add)
                        S.stt('dve', pbv[0:64], A[0:64, :, 15:L], 0.5, xb_new[0:64], ALU.mult, ALU.subtract)
                        S.stt('dve', pbv[64:128], Bv[64:128, :, 15:L], 0.25, xb_new[64:128], ALU.mult, ALU.subtract)
                    else:
                        S.tt(BE, Bv[:, :, 3:L], A[:, :, 3:L], A[:, :, 1:L - 2], ALU.add)
                        S.tt(BE, A[:, :, 7:L], Bv[:, :, 7:L], Bv[:, :, 3:L - 4], ALU.add)
                        S.tt(BE, Bv[64:128, :, 15:L], A[64:128, :, 15:L], A[64:128, :, 7:L - 8], ALU.add)
                        S.stt('dve', pbv[0:64], A[0:64, :, 15:L], 0.125, xb_new[0:64], ALU.mult, ALU.subtract)
                        S.stt('dve', pbv[64:128], Bv[64:128, :, 15:L], 0.0625, xb_new[64:128], ALU.mult, ALU.subtract)
                    if (not sample) and bi == 0:
                        ic = CST[:, COFF['invcnt'] + c * 16:COFF['invcnt'] + (c + 1) * 16]
                        tq = XS[:, 64:80]
                        for (p0, p1, src) in ((0, 64, A), (64, 128, Bv)):
                            S.tt(BE, tq[p0:p1], src[p0:p1, 0, 15:31], ic[p0:p1], ALU.mult)
                            S.tt(BE, PB[p0:p1, c, 0:16], tq[p0:p1], E[p0:p1, 0, 15:31], ALU.subtract)
                    pt = next_hb()[:, 0:n]
                    S.mm(pt, PWB[:, c, :], PB[:, c, 0:n], start=True, stop=True)
                    S.act(Y[:, 2 + c, 0:n], pt, AF.Identity, scale=cc('pool_scale', l * 2 + c), bias=0.0)
                if sample:
                    for c in range(2):
                        S.dma('sp', ext(EZ, c, 2)[:, :, 0:2], st_sconv[l, :, c])
                elif bi == 0:
                    for c in range(2):
                        S.memset('pool', EZ[:, c, 0:2], 0.0)
                for c in range(2):
                    E = ext(EZ, c, 2)
                    pt = proj(8 + c, t0, n)
                    S.copy('act', E[:, :, 2:2 + T], v3(pt))
                    pt = proj(10 + c, t0, n)
                    S.tt('dve', E[:, :, 2:2 + T], E[:, :, 2:2 + T], v3(pt), ALU.mult)
                    acc = v3(Y[:, 4 + c, 0:n])
                    wcol_ = lambda kk: cc('sconv_w', (l * 2 + c) * 3 + kk)
                    S.ts('dve', acc, E[:, :, 2:2 + T], wcol_(2), None, ALU.mult)
                    S.stt('dve', acc, E[:, :, 1:1 + T], wcol_(1), acc, ALU.mult, ALU.add)
                    S.stt('dve', acc, E[:, :, 0:T], wcol_(0), acc, ALU.mult, ALU.add)
                    pt = proj(6 + c, t0, n)
                    S.tt('dve', Y[:, 4 + c, 0:n], Y[:, 4 + c, 0:n], pt, ALU.mult)
                for c in range(2):
                    pt = proj(c, t0, n)
                    S.act(U[:, c, 0:n], pt, AF.Gelu)
                for ti in range(n // 128):
                    tt0 = t0 + ti * 128
                    if ti == 0:
                        VG, VB, BST = VG0, VB0, BST0
                    else:
                        VG = Y[:, 7, :]
                        VB = Y[:, 6, 0:128].bitcast(BF16)
                        BST = Y[:, 6, 128:144]
                    pv = next_hb()
                    for k in range(8):
                        S.mm(pv, XM[:, k, tt0:tt0 + 128], Wp[0][:, k, 256:512], start=(k == 0), stop=(k == 7))
                    S.act(VG[:, :], pv, AF.Gelu)
                    S.generic('dve', lambda e, b_=BST, v_=VG: e.bn_stats(b_[:, 0:6], v_[:, :]), [VG[:, :]], [BST[:, 0:6]])
                    S.generic('dve', lambda e, b_=BST: e.bn_aggr(b_[:, 8:10], b_[:, 0:6]), [BST[:, 0:6]], [BST[:, 8:10]])
                    S.act(BST[:, 9:10], BST[:, 9:10], AF.Sqrt, bias=EPS[:, 0:1], scale=1.0)
                    S.recip(BST[:, 9:10], BST[:, 9:10])
                    S.stt('dve', BST[:, 10:11], BST[:, 8:9], -1.0, BST[:, 9:10], ALU.mult, ALU.mult)
                    S.act(VG[:, :], VG[:, :], AF.Identity, bias=BST[:, 10:11], scale=BST[:, 9:10])
                    S.tt('dve' if 'lngdve' in FL else 'pool', VG[:, :], VG[:, :], LNG[:, 0, :], ALU.mult)
                    S.tt('dve', VG[:, :], VG[:, :], LNG[:, 1, :], ALU.add)
                    S.copy('act', VB[:, :], VG[:, :])
                    if tt0 == NPR - 128:
                        S.dma('sp', o_sgu_p[l], VG[:, :])
                    if sample:
                        S.dma('sp', o_sgu_s[l], VG[:, :])
                    pm = next_hb()
                    WT = WTS if sample else WTP
                    for c in range(2):
                        for hh in range(2):
                            h = 2 * c + hh
                            S.mm(pm[hh * 64:(hh + 1) * 64, c * 128:(c + 1) * 128], VB[:, h * 64:(h + 1) * 64], WT[:, h, :],
                                 start=True, stop=True)
                    tmp = Y[:, 0:2, ti * 128:(ti + 1) * 128]
                    if not sample:
                        S.tt('dve', tmp, pm.rearrange("p (c t) -> p c t", t=128), BTP[:, :, :], ALU.add)
                    else:
                        for c in range(2):
                            S.tt('dve', tmp[:, c, :].rearrange("p (s t) -> p s t", t=8),
                                 pm[:, c * 128:(c + 1) * 128].rearrange("p (s t) -> p s t", t=8),
                                 BTP[:, c, 0:8].unsqueeze(1).to_broadcast([128, 16, 8]), ALU.add)
                    S.tt('dve', Y[:, 0:2, ti * 128:(ti + 1) * 128], tmp, U[:, 0:2, ti * 128:(ti + 1) * 128], ALU.mult)
                for c in range(2):
                    S.copy('act', HB[:, 0, c, 0:n], H[:, c, 0:n])
                    S.act(HB[:, 1, c, 0:n], H[:, c, 0:n], AF.Square)
                pm_ = next_hb()[:, 0:n]
                pq_ = next_hb()[:, 0:n]
                for c in range(2):
                    S.mm(pm_, ONESB[:, :], HB[:, 0, c, 0:n], start=(c == 0), stop=(c == 1))
                for c in range(2):
                    S.mm(pq_, ONESB[:, :], HB[:, 1, c, 0:n], start=(c == 0), stop=(c == 1))
                mean = MR[:, 0, 0:n]; rstd = MR[:, 1, 0:n]
                S.copy('act', mean, pm_)
                S.act(rstd, pm_, AF.Square)
                S.tt('dve', rstd, pq_, rstd, ALU.subtract)
                S.act(rstd, rstd, AF.Ln, bias=EPS[:, 0:1], scale=1.0)
                S.act(rstd, rstd, AF.Exp, scale=-0.5)
                for c in range(2):
                    hc_ = H[:, c, 0:n]
                    S.tt('dve', hc_, hc_, mean, ALU.subtract)
                    S.tt('dve', hc_, hc_, rstd, ALU.mult)
                    S.act(Y[:, 6 + c, 0:n], hc_, AF.Silu, bias=cc('conv_ln_b', l * 2 + c), scale=cc('conv_ln_g', l * 2 + c))
                for c in range(2):
                    if sample:
                        S.dma('sp', o_pool_s[l, :, c], ext(EB, c, 15)[:, :, T:T + 15])
                        S.dma('sp', o_sconv_s[l, :, c], ext(EZ, c, 2)[:, :, T:T + 2])
                    elif bi == 7:
                        S.dma('sp', o_pool_p[l, :, c], EB[:, c, T:T + 15])
                        S.dma('sp', o_sconv_p[l, :, c], EZ[:, c, T:T + 2])
                    else:
                        S.copy('pool' if 'carrypool' in FL else 'act', XS[:, 0:15], EB[:, c, T:T + 15])
                        S.copy('pool' if 'carrypool' in FL else 'act', EB[:, c, 0:15], XS[:, 0:15])
                        S.copy('pool' if 'carrypool' in FL else 'act', XS[:, 16:18], EZ[:, c, T:T + 2])
                        S.copy('pool' if 'carrypool' in FL else 'act', EZ[:, c, 0:2], XS[:, 16:18])
                        S.copy('pool' if 'carrypool' in FL else 'act', XS[:, 32:62], ED[:, c, T:T + 30])
                        S.copy('pool' if 'carrypool' in FL else 'act', ED[:, c, 0:30], XS[:, 32:62])

            def rms(bi, t0, n):
                for c in range(8):
                    if 'rmsdve' in FL and c % 2 == 1:
                        S.tt('dve', SQ[:, c, 0:n], Y[:, c, 0:n], Y[:, c, 0:n], ALU.mult)
                    else:
                        S.act(SQ[:, c, 0:n], Y[:, c, 0:n], AF.Square)
                for g in range(4):
                    pr = next_hb()[:, 0:n]
                    for c in range(2):
                        S.mm(pr, ONESB[:, :], SQ[:, 2 * g + c, 0:n], start=(c == 0), stop=(c == 1))
                    rg = MR[:, g % 2, 0:n]
                    S.act(rg, pr, AF.Ln, bias=EPS[:, 1:2], scale=1.0)
                    S.act(rg, rg, AF.Exp, scale=-0.5)
                    for c in (2 * g, 2 * g + 1):
                        S.stt('dve', XM[:, c, t0:t0 + n], Y[:, c, 0:n], cc('out_norm_g', l * 8 + c), rg,
                              ALU.mult, ALU.mult)

            def back(bi, t0, n):
                for oc in range(8):
                    pt = next_hb()[:, 0:n]
                    q, r = divmod(oc, 4)
                    for k in range(8):
                        S.mm(pt, Wp[4 + q][:, k, r * 128:(r + 1) * 128], XM[:, k, t0:t0 + n], start=(k == 0), stop=(k == 7))
                    accumulate(pt, l, 1, oc, t0, n)
                ln_block(k_next, t0, n)

            nb = len(MIX_BLOCKS)
            for bi, (t0, n) in enumerate(MIX_BLOCKS):
                front(bi, t0, n)
                if bi > 0:
                    back(bi - 1, *MIX_BLOCKS[bi - 1])
                rms(bi, t0, n)
            back(nb - 1, *MIX_BLOCKS[nb - 1])

        def finish():
            for (t0_, n_) in MIX_BLOCKS:
                for c in range(8):
                    S.dma('sp', yT[c][:, t0_:t0_ + n_], X[:, c, t0_:t0_ + n_])
            if stop is not None:
                S.dma('sp', o_ada, ADA[:, :, :, :].rearrange("p l j s -> p (l j s)"))
            S.emit(es)

        stages = []
        if stop in ('setup', 'ada'):
            finish()
            return nc
        import os
        _sel = os.environ.get('LNBLK')
        _blks = MIX_BLOCKS if _sel is None else [MIX_BLOCKS[int(q)] for q in _sel.split(',')]
        stages.append(('ln0', lambda: [ln_block(0, t0, n) for (t0, n) in _blks]))
        for l in range(nlayers):
            stages.append(('ffn1_%d' % l, lambda l=l: ffn(l, 0)))
            stages.append(('mixer_%d' % l, lambda l=l: mixer(l)))
            stages.append(('ffn2_%d' % l, lambda l=l: ffn(l, 1)))
        for name, fn in stages:
            fn()
            if stop == name:
                break
        finish()
    return nc

_PROG_CACHE = {}


def _fm(a, rows):
    return np.ascontiguousarray(np.asarray(a, np.float32).reshape(rows, 128).T)


def _const_tables():
    inv = np.zeros((128, 2, 16), np.float32)
    wins = (2, 4, 8, 16)
    for c in range(2):
        for p in range(128):
            w = wins[2 * c + p // 64]
            for pos in range(16):
                inv[p, c, pos] = 1.0 / min(w, pos + 1)
    m0 = np.triu(np.ones((128, 128), np.float32))
    m1 = np.kron(np.eye(16, dtype=np.float32), np.triu(np.ones((8, 8), np.float32)))
    return inv.reshape(128, 32), np.stack([m0, m1]).astype(np.float32)


def _make_in_maps(x_prompt, x_sample, state_pool, state_sconv, state_dconv, c_prompt, c_sample,
           ln_in_g, ln_in_b, w_ada, b_ada, ffn1_w_gate, ffn1_w_up, ffn1_w_down, w_in,
           sgu_ln_g, sgu_ln_b, sgu_w, sgu_b, pool_w, pool_scale, sconv_w, dconv_w, dconv_b,
           conv_ln_g, conv_ln_b, out_norm_g, w_out, ffn2_w_gate, ffn2_w_up, ffn2_w_down,
           post_ln_g, post_ln_b):
    f32 = lambda a: np.ascontiguousarray(np.asarray(a, dtype=np.float32))
    x_prompt = f32(x_prompt); x_sample = f32(x_sample)
    inv, masks = _const_tables()
    consts = np.zeros((128, NCONST), np.float32)

    def put(name, arr):
        consts[:, COFF[name]:COFF[name] + arr.shape[1]] = arr
    put('ln_in_g', _fm(ln_in_g, 8)); put('ln_in_b', _fm(ln_in_b, 8))
    put('post_g', _fm(post_ln_g, 48)); put('post_b', _fm(post_ln_b, 48))
    put('out_norm_g', _fm(out_norm_g, 16)); put('pool_scale', _fm(pool_scale, 4))
    put('sconv_w', f32(sconv_w).reshape(2, 3, 2, 128).transpose(3, 0, 2, 1).reshape(128, 12))
    put('dconv_w', f32(dconv_w).reshape(2, 31, 2, 128).transpose(3, 0, 2, 1).reshape(128, 124))
    put('dconv_b', _fm(dconv_b, 4)); put('conv_ln_g', _fm(conv_ln_g, 4)); put('conv_ln_b', _fm(conv_ln_b, 4))
    put('b_ada', _fm(b_ada, 144))
    put('invcnt', inv)
    put('idn32', np.tile(np.eye(32, dtype=np.float32), (4, 1)))

    shared = {
        "consts": consts,
        "w_ada": f32(w_ada),
        "ffn1_w_gate": f32(ffn1_w_gate), "ffn1_w_up": f32(ffn1_w_up), "ffn1_w_down": f32(ffn1_w_down),
        "ffn2_w_gate": f32(ffn2_w_gate), "ffn2_w_up": f32(ffn2_w_up), "ffn2_w_down": f32(ffn2_w_down),
        "w_in": f32(w_in), "w_out": f32(w_out),
        "sgu_wT": f32(np.asarray(sgu_w).transpose(0, 1, 3, 2)),
        "sgu_b": f32(sgu_b),
        "sgu_ln": f32(np.stack([np.asarray(sgu_ln_g), np.asarray(sgu_ln_b)], axis=1)),
        "pool_w": f32(pool_w),
        "masks": masks,
    }
    state_pool = np.asarray(state_pool); state_sconv = np.asarray(state_sconv); state_dconv = np.asarray(state_dconv)
    c_prompt = np.asarray(c_prompt); c_sample = np.asarray(c_sample)

    def st_fm(st, i):
        a = st[:, 16 * i:16 * i + 16]
        L_, _, R, _ = a.shape
        a = a.reshape(L_, 16, R, 2, 128).transpose(0, 4, 3, 1, 2)
        return f32(a)

    in_maps = []
    for i in range(8):
        xall = np.concatenate([x_prompt[i], x_sample[16 * i:16 * i + 16].reshape(128, 1024)], axis=0)
        call = np.concatenate([c_prompt[i:i + 1], c_sample[16 * i:16 * i + 16]], axis=0)
        m = dict(shared)
        m["xT"] = f32(xall.T.reshape(8, 128, NT))
        m["cT"] = f32(call.reshape(17, 8, 128).transpose(2, 1, 0))
        m["st_pool"] = st_fm(state_pool, i)
        m["st_sconv"] = st_fm(state_sconv, i)
        m["st_dconv"] = st_fm(state_dconv, i)
        in_maps.append(m)

    return in_maps


def _gather(R):
    y_prompt = np.zeros((8, 2048, 1024), np.float32)
    y_sample = np.zeros((128, 8, 1024), np.float32)
    sgu_p = np.zeros((DEPTH, 8, 128, 256), np.float32)
    sgu_s = np.zeros((DEPTH, 128, 8, 256), np.float32)
    pool_p = np.zeros((DEPTH, 8, 15, 256), np.float32)
    pool_s = np.zeros((DEPTH, 128, 15, 256), np.float32)
    sconv_p = np.zeros((DEPTH, 8, 2, 256), np.float32)
    sconv_s = np.zeros((DEPTH, 128, 2, 256), np.float32)
    dconv_p = np.zeros((DEPTH, 8, 30, 256), np.float32)
    dconv_s = np.zeros((DEPTH, 128, 30, 256), np.float32)
    for i in range(8):
        r = R[i]
        yt = np.asarray(r["yT"]).reshape(1024, NT)
        y_prompt[i] = yt[:, :2048].T
        y_sample[16 * i:16 * i + 16] = yt[:, 2048:].T.reshape(16, 8, 1024)
        sgu_p[:, i] = np.asarray(r["o_sgu_p"])
        sgu_s[:, 16 * i:16 * i + 16] = np.asarray(r["o_sgu_s"]).reshape(DEPTH, 16, 8, 256)
        for (op_, os_, dp, ds, rows) in (("o_pool_p", "o_pool_s", pool_p, pool_s, 15),
                                         ("o_sconv_p", "o_sconv_s", sconv_p, sconv_s, 2),
                                         ("o_dconv_p", "o_dconv_s", dconv_p, dconv_s, 30)):
            a = np.asarray(r[op_])
            dp[:, i] = a.transpose(0, 3, 2, 1).reshape(DEPTH, rows, 256)
            b = np.asarray(r[os_])
            ds[:, 16 * i:16 * i + 16] = b.transpose(0, 3, 4, 2, 1).reshape(DEPTH, 16, rows, 256)
    return (y_prompt, y_sample, sgu_p, sgu_s, pool_p, pool_s, sconv_p, sconv_s, dconv_p, dconv_s)


def kernel(x_prompt, x_sample, state_pool, state_sconv, state_dconv, c_prompt, c_sample,
           ln_in_g, ln_in_b, w_ada, b_ada, ffn1_w_gate, ffn1_w_up, ffn1_w_down, w_in,
           sgu_ln_g, sgu_ln_b, sgu_w, sgu_b, pool_w, pool_scale, sconv_w, dconv_w, dconv_b,
           conv_ln_g, conv_ln_b, out_norm_g, w_out, ffn2_w_gate, ffn2_w_up, ffn2_w_down,
           post_ln_g, post_ln_b):
    in_maps = _make_in_maps(x_prompt, x_sample, state_pool, state_sconv, state_dconv, c_prompt, c_sample,
           ln_in_g, ln_in_b, w_ada, b_ada, ffn1_w_gate, ffn1_w_up, ffn1_w_down, w_in,
           sgu_ln_g, sgu_ln_b, sgu_w, sgu_b, pool_w, pool_scale, sconv_w, dconv_w, dconv_b,
           conv_ln_g, conv_ln_b, out_norm_g, w_out, ffn2_w_gate, ffn2_w_up, ffn2_w_down,
           post_ln_g, post_ln_b)
    if "nc" not in _PROG_CACHE:
        _PROG_CACHE["nc"] = build_program()
    nc = _PROG_CACHE["nc"]
    res = run_bass_kernel_spmd(nc, in_maps, core_ids=list(range(8)))
    return _gather(res.results)
```

```python
from concourse.bass_utils import run_bass_kernel_spmd
import os
import heapq
import numpy as np
import concourse.bass as bass
import concourse.mybir as mybir

F32 = mybir.dt.float32
BF16 = mybir.dt.bfloat16
ALU = mybir.AluOpType
AF = mybir.ActivationFunctionType

CENG = ['pe', 'act', 'dve', 'pool']
DEF_LAT = '0.6'; DEF_LOOK = '12'; DEF_WINDOW = '400'; DEF_SLAT = '0.3'; DEF_TPEN = '1.3'
ALLENG = ['pe', 'act', 'dve', 'pool', 'sp']
_DSZ = {F32: 4, BF16: 2}


def _region(ap):
    name = ap.tensor.name
    steps = ap.ap
    esz = _DSZ.get(ap.dtype, 4)
    sp = str(ap.space)
    if 'DRAM' in sp.upper() or 'HBM' in sp.upper():
        lo = ap.offset
        hi = lo + sum((c - 1) * abs(s) for s, c in steps) + 1
        return (name, 0, 1, lo * esz, hi * esz, None)
    if 'PSUM' in sp.upper():
        return (name, 0, 128, 0, 1 << 20, None)
    pstep, pcnt = steps[0]
    if pstep == 0:
        pstep = 1 << 40
    p0 = ap.offset // pstep if pstep < (1 << 40) else 0
    f0 = ap.offset - p0 * pstep if pstep < (1 << 40) else ap.offset
    free = steps[1:]
    ext = sum((c - 1) * abs(s) for s, c in free) + 1
    rows = None
    if len(free) >= 2:
        s0, c0 = free[0]
        ein = sum((c - 1) * abs(s) for s, c in free[1:]) + 1
        if 1 < c0 <= 16 and s0 > ein:
            rows = tuple(((f0 + r * s0) * esz, (f0 + r * s0 + ein) * esz) for r in range(c0))
    return (name, p0, p0 + pcnt, f0 * esz, (f0 + ext) * esz, rows)


_TSET = {AF.Gelu: 'gelu', AF.Tanh: 'gelu', AF.Silu: 'silu', AF.Sigmoid: 'sigm', AF.Sqrt: 'sqrt', AF.Ln: 'lnexp', AF.Exp: 'lnexp'}


def _nfree(ap):
    n = 1
    for s, c in ap.ap[1:]:
        n *= c
    return n


def _ovl(a, b):
    if not (a[1] < b[2] and b[1] < a[2] and a[3] < b[4] and b[3] < a[4]):
        return False
    ra, rb = a[5], b[5]
    if ra is None and rb is None:
        return True
    ia = ra if ra is not None else ((a[3], a[4]),)
    ib = rb if rb is not None else ((b[3], b[4]),)
    for (x0, x1) in ia:
        for (y0, y1) in ib:
            if x0 < y1 and y0 < x1:
                return True
    return False


def _covers(a, b):
    if not (a[1] <= b[1] and a[2] >= b[2] and a[3] <= b[3] and a[4] >= b[4]):
        return False
    ra, rb = a[5], b[5]
    if ra is None:
        return True
    ib = rb if rb is not None else ((b[3], b[4]),)
    for (y0, y1) in ib:
        if not any(x0 <= y0 and y1 <= x1 for (x0, x1) in ra):
            return False
    return True


class Op:
    __slots__ = ('eng', 'fn', 'pos', 'is_dma', 'dsem', 'dval', 'waits', 'sig', 'vc', 'prewait', 'tag',
                 'idx', 'deps', 'cost', 'xfer', 'nsucc', 'succ', 'start', 'finish', 'raw', 'tset')


class Sched:
    def __init__(self, nc, kdma=8):
        self.nc = nc
        self.ops = []
        self.wr = {}
        self.rd = {}
        self.kdma = kdma

    def _add(self, eng, fn, ins, outs, is_dma=False, cost=0.3, xfer=0.0):
        op = Op()
        op.eng = eng; op.fn = fn; op.is_dma = is_dma; op.sig = False; op.waits = []; op.prewait = None
        op.idx = len(self.ops); op.cost = cost; op.xfer = xfer; op.tset = None
        deps = {}
        raw = set()
        for ap in ins:
            r = _region(ap)
            for (wr_, o) in self.wr.get(r[0], ()):
                if _ovl(r, wr_):
                    deps[o.idx] = o; raw.add(o.idx)
        for ap in outs:
            r = _region(ap)
            for (wr_, o) in self.wr.get(r[0], ()):
                if _ovl(r, wr_):
                    deps[o.idx] = o
            for (rr, o) in self.rd.get(r[0], ()):
                if _ovl(r, rr):
                    deps[o.idx] = o
        for ap in ins:
            r = _region(ap)
            lst = self.rd.setdefault(r[0], [])
            if not is_dma:
                keep = []
                for (rr, o) in lst:
                    if o.eng == eng and (not o.is_dma) and _covers(r, rr):
                        deps[o.idx] = o
                    else:
                        keep.append((rr, o))
                lst[:] = keep
            lst.append((r, op))
        for ap in outs:
            r = _region(ap)
            wl = self.wr.setdefault(r[0], [])
            wl[:] = [(wr_, o) for (wr_, o) in wl if not _covers(r, wr_)]
            wl.append((r, op))
            rl = self.rd.get(r[0])
            if rl:
                rl[:] = [(rr, o) for (rr, o) in rl if not _covers(r, rr)]
        deps.pop(op.idx, None)
        op.deps = list(deps.values())
        op.raw = raw
        self.ops.append(op)
        return op

    def mm(self, out, lhsT, rhs, start=True, stop=True, **kw):
        n = _nfree(rhs)
        cost = max(n, 64) / 2400.0 + 0.012
        if rhs.dtype == F32:
            cost *= 4.0
        if 'tile_position' in kw:
            cost *= 0.27
        return self._add('pe', lambda e: e.matmul(out, lhsT, rhs, start=start, stop=stop, **kw),
                         [lhsT, rhs] + ([] if start else [out]), [out], cost=cost)

    def act(self, out, in_, func, bias=None, scale=None):
        ins = [in_]
        kw = {}
        if bias is not None:
            kw['bias'] = bias
            if not isinstance(bias, (int, float)):
                ins.append(bias)
        if scale is not None:
            kw['scale'] = scale
            if not isinstance(scale, (int, float)):
                ins.append(scale)
        o = self._add('act', lambda e: e.activation(out, in_, func, **kw), ins, [out],
                      cost=0.2 + _nfree(out) / 1200.0)
        o.tset = _TSET.get(func)
        return o

    def _vcost(self, eng, out, mult=1.0):
        n = _nfree(out)
        if eng == 'pool':
            return 0.3 + n * 2.2 / 1200.0
        return 0.12 + mult * n / 960.0

    def tt(self, eng, out, in0, in1, op):
        return self._add(eng, lambda e: e.tensor_tensor(out, in0, in1, op), [in0, in1], [out], cost=self._vcost(eng, out))

    def ts(self, eng, out, in0, s1, s2, op0, op1=None):
        ins = [in0] + [s for s in (s1, s2) if s is not None and not isinstance(s, (int, float))]
        if op1 is None:
            return self._add(eng, lambda e: e.tensor_single_scalar(out, in0, s1, op0), ins, [out], cost=self._vcost(eng, out))
        return self._add(eng, lambda e: e.tensor_scalar(out, in0, s1, s2, op0, op1), ins, [out], cost=self._vcost(eng, out))

    def stt(self, eng, out, in0, scalar, in1, op0, op1):
        ins = [in0, in1] + ([] if isinstance(scalar, (int, float)) else [scalar])
        return self._add(eng, lambda e: e.scalar_tensor_tensor(out, in0, scalar, in1, op0, op1), ins, [out],
                         cost=self._vcost(eng, out))

    def copy(self, eng, out, in_):
        if eng == 'act':
            return self._add('act', lambda e: e.activation(out, in_, AF.Copy), [in_], [out], cost=0.2 + _nfree(out) / 1200.0)
        return self._add(eng, lambda e: e.tensor_copy(out, in_), [in_], [out], cost=self._vcost(eng, out))

    def memset(self, eng, out, val):
        return self._add(eng, lambda e: e.memset(out, val), [], [out], cost=self._vcost(eng, out))

    def recip(self, out, in_):
        return self._add('dve', lambda e: e.reciprocal(out, in_), [in_], [out], cost=self._vcost('dve', out, 4.0))

    def generic(self, eng, fn, ins, outs, cost=0.4):
        return self._add(eng, fn, ins, outs, cost=cost)

    def dma(self, eng, out, in_, **kw):
        nbytes = 1
        for d_ in out.shape:
            nbytes *= d_
        nbytes *= max(_DSZ.get(out.dtype, 4), _DSZ.get(in_.dtype, 4))
        return self._add(eng, lambda e, sem, val: e.dma_start(out=out, in_=in_, **kw).then_inc(sem, 16),
                         [in_], [out], is_dma=True, cost=(1.2 if eng == 'pool' else 0.1), xfer=2.0 + nbytes / 150e3)

    def _schedule(self):
        ops = self.ops
        LAT = float(os.environ.get('SCHED_LAT', DEF_LAT))
        LOOK = int(os.environ.get('SCHED_LOOK', DEF_LOOK))
        SLAT = float(os.environ.get('SCHED_SLAT', DEF_SLAT))
        TPEN = float(os.environ.get('SCHED_TPEN', DEF_TPEN))
        if os.environ.get('NOSCHED'):
            t = 0.0
            for op in ops:
                op.start = t; op.finish = t + 1e-3; t += 1e-3
            return list(ops)
        for op in ops:
            op.nsucc = len(op.deps); op.succ = []
        for op in ops:
            for d in op.deps:
                d.succ.append(op)
        cand = {e: [] for e in ALLENG}
        rtime = {}
        for op in ops:
            if op.nsucc == 0:
                heapq.heappush(cand[op.eng], (op.idx, op)); rtime[op.idx] = 0.0
        free = {e: 0.0 for e in ALLENG}
        dma_free = 0.0
        cur_tset = [None]
        order = []
        nleft = len(ops)
        WINDOW = int(os.environ.get('SCHED_WINDOW', DEF_WINDOW))
        while nleft:
            best = None
            for e in ALLENG:
                h = cand[e]
                if not h:
                    continue
                tfree = free[e]
                pick = None; pick_t = None
                look = heapq.nsmallest(LOOK + 4 if e == 'act' else LOOK, h)
                lo_idx = look[0][0]
                for (idx, o) in look:
                    if idx - lo_idx > WINDOW:
                        break
                    rt = rtime[idx]
                    st = rt if rt > tfree else tfree
                    if e == 'act' and o.tset is not None and o.tset != cur_tset[0]:
                        st += TPEN
                    if pick is None or st < pick_t - 1e-9:
                        pick = o; pick_t = st
                    if e != 'act' and rt <= tfree:
                        break
                if best is None or pick_t < best[0] - 1e-9 or (abs(pick_t - best[0]) <= 1e-9 and pick.idx < best[1].idx):
                    best = (pick_t, pick, e)
            st, op, e = best
            h = cand[e]
            h.remove((op.idx, op)); heapq.heapify(h)
            op.start = st
            if op.is_dma:
                free[e] = st + op.cost
                xs = max(st + op.cost, dma_free)
                dma_free = xs + (op.xfer - 2.0) * 0.5
                op.finish = xs + op.xfer
            else:
                if e == 'act' and op.tset is not None:
                    cur_tset[0] = op.tset
                op.finish = st + op.cost
                free[e] = op.finish
            order.append(op)
            nleft -= 1
            for s in op.succ:
                s.nsucc -= 1
                lat = (0.0 if op.eng == 'pe' else SLAT) if (s.eng == op.eng and not op.is_dma) else LAT
                rt = op.finish + lat
                if rtime.get(s.idx, 0.0) < rt:
                    rtime[s.idx] = rt
                if s.nsucc == 0:
                    heapq.heappush(cand[s.eng], (s.idx, s))
        order.sort(key=lambda o: (o.start, o.idx))
        self.sim_end = max(o.finish for o in order)
        return order

    def _waits(self, order):
        self.streams = {e: [] for e in ALLENG}
        self.cops = {e: [] for e in CENG}
        cpos = {e: 0 for e in CENG}
        vcs = {e: {c: 0 for c in CENG} for e in ALLENG}
        dknown = {e: {} for e in ALLENG}
        ndma = {e: 0 for e in ALLENG}
        for op in order:
            eng = op.eng
            vc = vcs[eng]; dk = dknown[eng]
            best = {}; dmab = {}
            for d in op.deps:
                if d.is_dma:
                    if d.dsem not in dmab or d.dval > dmab[d.dsem].dval:
                        dmab[d.dsem] = d
                else:
                    a = d.eng
                    if a == eng and eng == 'pe':
                        continue
                    if a not in best or d.pos > best[a].pos:
                        best[a] = d
            for key, d in dmab.items():
                if dk.get(key, 0) >= d.dval:
                    continue
                op.waits.append(('dma', key, d.dval))
                dk[key] = d.dval
                for c in CENG:
                    if d.vc[c] > vc[c]:
                        vc[c] = d.vc[c]
            for a, d in best.items():
                if vc[a] >= d.pos:
                    continue
                op.waits.append(('eng', a, d.pos)); d.sig = True
                for c in CENG:
                    if d.vc[c] > vc[c]:
                        vc[c] = d.vc[c]
                if d.pos > vc[a]:
                    vc[a] = d.pos
            if op.is_dma:
                i = ndma[eng]; ndma[eng] += 1
                op.dsem = (eng, i % self.kdma)
                op.dval = 16 * (i // self.kdma + 1)
                if i >= self.kdma:
                    op.prewait = (op.dsem, op.dval - 16)
                    if dk.get(op.dsem, 0) < op.dval - 16:
                        dk[op.dsem] = op.dval - 16
                op.pos = None
                op.vc = dict(vc)
            else:
                cpos[eng] += 1
                op.pos = cpos[eng]
                self.cops[eng].append(op)
                op.vc = dict(vc)
                op.vc[eng] = op.pos
            self.streams[eng].append(op)
        self.ndma = ndma

    def emit(self, es):
        nc = self.nc
        order = self._schedule()
        self._waits(order)
        csem = {e: es.enter_context(nc.semaphore('s_' + e)) for e in CENG}
        dsem = {}
        for e in ALLENG:
            for k in range(min(self.kdma, self.ndma[e])):
                dsem[(e, k)] = es.enter_context(nc.semaphore('d_%s%d' % (e, k)))
        count = {}
        for e in CENG:
            n = 0
            for op in self.cops[e]:
                if op.sig:
                    n += 1
                count[(e, op.pos)] = n
        self.count = count
        streams = self.streams
        kd = self.kdma
        nd = self.ndma

        def run(engname, eng):
            for op in streams[engname]:
                if op.prewait is not None:
                    eng.wait_ge(dsem[op.prewait[0]], op.prewait[1])
                for w in op.waits:
                    if w[0] == 'dma':
                        eng.wait_ge(dsem[w[1]], w[2])
                    else:
                        eng.wait_ge(csem[w[1]], count[(w[1], w[2])])
                if op.is_dma:
                    op.fn(eng, dsem[op.dsem], op.dval)
                else:
                    ins = op.fn(eng)
                    if op.sig:
                        ins.then_inc(csem[engname], 1)
            n = nd[engname]
            for k in range(min(kd, n)):
                cnt = (n - k + kd - 1) // kd
                eng.wait_ge(dsem[(engname, k)], 16 * cnt)

        block = es.enter_context(nc.Block())

        @block.tensor
        def _(e):
            run('pe', e)

        @block.scalar
        def _(e):
            run('act', e)

        @block.vector
        def _(e):
            run('dve', e)

        @block.gpsimd
        def _(e):
            run('pool', e)

        @block.sync
        def _(e):
            run('sp', e)

import contextlib

D = 1024; KC = 8; FF = 2816; NFC = 22; DEPTH = 2
NT = 2176; NPR = 2048; NSM = 128
ALPHA = (2 * DEPTH) ** 0.25
GROUPS = [(0, 4), (4, 4), (8, 2), (10, 4), (14, 4), (18, 4)]
FFN_BLOCKS = [(0, 512), (512, 512), (1024, 512), (1536, 512), (2048, 128)]
MIX_BLOCKS = [(i * 256, 256) for i in range(8)] + [(2048, 128)]
SLOT = 12288
PIECE = 4096

DEFAULT_FLAGS = 'castdve,rescdve,subdve,lngdve,bdve'

def _const_layout():
    off = {}
    n = 0
    def add(name, k):
        nonlocal n
        off[name] = n; n += k
    add('ln_in_g', 8); add('ln_in_b', 8)
    add('post_g', 48); add('post_b', 48)
    add('out_norm_g', 16); add('pool_scale', 4)
    add('sconv_w', 12); add('dconv_w', 124); add('dconv_b', 4)
    add('conv_ln_g', 4); add('conv_ln_b', 4)
    add('b_ada', 144)
    add('invcnt', 32)
    add('idn32', 32)
    return off, n
COFF, NCONST = _const_layout()


def build_program(nlayers=DEPTH, stop=None):
    import os as _os2
    FL = set(_os2.environ.get('KFLAGS', DEFAULT_FLAGS).split(','))
    nc = bass.Bass("TRN2", target_bir_lowering=False)
    dt_in = lambda name, shape: nc.dram_tensor(name, shape, F32, kind="ExternalInput").ap()
    dt_out = lambda name, shape: nc.dram_tensor(name, shape, F32, kind="ExternalOutput").ap()
    xT = dt_in("xT", [8, 128, NT])
    cT = dt_in("cT", [128, 8, 17])
    consts = dt_in("consts", [128, NCONST])
    w_ada = dt_in("w_ada", [DEPTH, D, 9 * D])
    wg = [dt_in("ffn1_w_gate", [DEPTH, D, FF]), dt_in("ffn2_w_gate", [DEPTH, D, FF])]
    wu = [dt_in("ffn1_w_up", [DEPTH, D, FF]), dt_in("ffn2_w_up", [DEPTH, D, FF])]
    wd = [dt_in("ffn1_w_down", [DEPTH, FF, D]), dt_in("ffn2_w_down", [DEPTH, FF, D])]
    w_in = dt_in("w_in", [DEPTH, D, 2048])
    w_out = dt_in("w_out", [DEPTH, D, D])
    sgu_wT = dt_in("sgu_wT", [DEPTH, 4, 128, 128])
    sgu_b = dt_in("sgu_b", [DEPTH, 4, 128])
    sgu_ln = dt_in("sgu_ln", [DEPTH, 2, 256])
    pool_w = dt_in("pool_w", [DEPTH, 4, 64, 64])
    masks = dt_in("masks", [2, 128, 128])
    st_pool = dt_in("st_pool", [DEPTH, 128, 2, 16, 15])
    st_sconv = dt_in("st_sconv", [DEPTH, 128, 2, 16, 2])
    st_dconv = dt_in("st_dconv", [DEPTH, 128, 2, 16, 30])

    yT = dt_out("yT", [8, 128, NT])
    o_sgu_p = dt_out("o_sgu_p", [DEPTH, 128, 256])
    o_sgu_s = dt_out("o_sgu_s", [DEPTH, 128, 256])
    o_pool_p = dt_out("o_pool_p", [DEPTH, 128, 2, 15])
    o_pool_s = dt_out("o_pool_s", [DEPTH, 128, 2, 16, 15])
    o_sconv_p = dt_out("o_sconv_p", [DEPTH, 128, 2, 2])
    o_sconv_s = dt_out("o_sconv_s", [DEPTH, 128, 2, 16, 2])
    o_dconv_p = dt_out("o_dconv_p", [DEPTH, 128, 2, 30])
    o_dconv_s = dt_out("o_dconv_s", [DEPTH, 128, 2, 16, 30])

    o_ada = dt_out("o_ada", [128, DEPTH * 72 * 17]) if stop is not None else None
    es = contextlib.ExitStack()
    with es:
        S = Sched(nc)
        sb = lambda name, shape, dt=F32: es.enter_context(nc.sbuf_tensor(name, shape, dt))
        X = sb("X", [128, 8, NT])
        XM = sb("XM", [128, 8, NT], BF16)
        W = sb("W", [128, 2 * SLOT], BF16)
        SQ = sb("SQ", [128, 8, 256], BF16)
        MR = sb("MR", [128, 2, 256])
        ADA = sb("ADA", [128, DEPTH, 72, 17])
        CST = sb("CST", [128, NCONST])
        BC = sb("BC", [128, DEPTH * 3 + 1, 8])
        AB = sb("AB", [128, DEPTH * 3 + 1, 8])
        EPS = sb("EPS", [128, 2])
        CSB = sb("CSB", [128, 8, 17], BF16)
        ONESA = sb("ONESA", [128, 128], BF16)
        ONESB = sb("ONESB", [128, 128], BF16)
        WTP = sb("WTP", [128, 4, 128], BF16)
        WTS = sb("WTS", [128, 4, 128], BF16)
        BTP = sb("BTP", [128, 2, 128])
        LNG = sb("LNG", [128, 2, 256])
        PWB = sb("PWB", [128, 2, 128], BF16)
        XS = sb("XS", [128, 128])
        DG = sb("DG", [128, 62, 32], BF16)
        NSCR = 6352
        SCR = sb("SCR", [128, NSCR])

        def carve(w0, shape, dt=F32):
            nel = 1
            for d_ in shape:
                nel *= d_
            nw = nel if dt == F32 else nel // 2
            a = SCR[:, w0:w0 + nw]
            if dt != F32:
                a = a.bitcast(dt)
            if len(shape) == 1:
                return a
            if len(shape) == 2:
                return a.rearrange("p (a b) -> p a b", b=shape[1])
            return a.rearrange("p (a b c) -> p a b c", b=shape[1], c=shape[2])
        Y = carve(0, [8, 256]); U = carve(2048, [2, 256]); H = carve(2560, [2, 256])
        VG0 = carve(3072, [256]); VB0 = carve(3328, [256], BF16); PB = carve(3456, [2, 256], BF16)
        SA = carve(3712, [368]); SBB = carve(4080, [368]); EB = carve(4448, [2, 368])
        EZ = carve(5184, [2, 272]); ED = carve(5728, [2, 608], BF16); BST0 = carve(6336, [16])
        SG = carve(0, [2, 512], BF16); ACTB = carve(512, [2, 4, 512], BF16)
        STG = carve(0, [512]); MSK = carve(512, [2, 128])
        HB = SQ[:, 0:4, :].rearrange("p (a c) n -> p a c n", c=2)
        ps = [es.enter_context(nc.psum_tensor("ps%d" % i, [128, 512], F32)) for i in range(8)]
        hb = [ps[i // 2][:, (i % 2) * 256:(i % 2) * 256 + 256] for i in range(16)]

        cc = lambda name, j: CST[:, COFF[name] + j:COFF[name] + j + 1]

        for c in range(8):
            S.dma('sp', X[:, c, :], xT[c])
        S.dma('sp', CST[:, :], consts)
        S.dma('sp', STG[:, 0:136], cT.rearrange("p k s -> p (k s)"))
        S.memset('dve', EPS[:, 0:1], 1e-5)
        S.memset('dve', EPS[:, 1:2], 1e-6)
        S.memset('dve', ONESA[:, :], 1.0 / 1024)
        S.memset('dve', ONESB[:, :], 1.0 / 256)
        S.act(CSB[:, :, :].rearrange("p k s -> p (k s)"), STG[:, 0:136], AF.Silu)

        WA = sb("WA", [128, 2, 8, 128], BF16)
        pcnt = [0]

        def ln_cols(k):
            if k == 0:
                return COFF['ln_in_g'], COFF['ln_in_b']
            return COFF['post_g'] + (k - 1) * 8, COFF['post_b'] + (k - 1) * 8
        nln = 3 * nlayers

        def ada_post(l, i):
            sl = ADA[:, l, (3 * i + 1) * 8:(3 * i + 2) * 8, :]
            S.ts('dve', sl, sl, 1.0, None, ALU.add)
            gl = ADA[:, l, (3 * i + 2) * 8:(3 * i + 3) * 8, :]
            if i == 1:
                S.ts('dve', gl, gl, 1.0, None, ALU.add)
            else:
                S.ts('dve', gl, gl, 1.0, 0.5, ALU.add, ALU.mult)
            k = 3 * l + i
            gcol, bcol = ln_cols(k)
            S.tt('dve', BC[:, k, :], CST[:, bcol:bcol + 8], ADA[:, l, (3 * i + 1) * 8:(3 * i + 2) * 8, 0], ALU.mult)
            S.tt('dve', BC[:, k, :], BC[:, k, :], ADA[:, l, (3 * i) * 8:(3 * i + 1) * 8, 0], ALU.add)

        def ada_chunk(l, j, Wt, jj, banks):
            pt = ps[banks[pcnt[0] % len(banks)]][:, 0:17]; pcnt[0] += 1
            for k in range(8):
                S.mm(pt, Wt[:, k, jj * 128:(jj + 1) * 128], CSB[:, k, :], start=(k == 0), stop=(k == 7))
            S.act(ADA[:, l, j, :], pt, AF.Identity, bias=cc('b_ada', l * 72 + j))

        if stop != 'setup':
            wv0 = w_ada[0].rearrange("(k p) n -> p k n", p=128)
            for n in range(6):
                reg = (n % 6) * PIECE
                Wt = W[:, reg:reg + PIECE].rearrange("p (k n) -> p k n", n=512)
                S.dma('pool', Wt, wv0[:, :, n * 512:(n + 1) * 512])
                for jj in range(4):
                    ada_chunk(0, n * 4 + jj, Wt, jj, list(range(8)))
            for k in range(nln + 1):
                gcol, bcol = ln_cols(k)
                S.ts('dve', AB[:, k, :], CST[:, bcol:bcol + 8], ALPHA, None, ALU.mult)
            ada_post(0, 0)

        ada_todo = [(0, j) for j in range(24, 72)]
        for l in range(1, nlayers):
            for j in range(72):
                ada_todo.append((l, j))
        ada_state = {'i': 0}

        def ada_pump(nmax):
            for _ in range(nmax):
                if ada_state['i'] >= len(ada_todo):
                    return
                l, j = ada_todo[ada_state['i']]
                slot = ada_state['i'] % 2
                ada_state['i'] += 1
                wvl = w_ada[l].rearrange("(k p) n -> p k n", p=128)
                S.dma('pool', WA[:, slot, :, :], wvl[:, :, j * 128:(j + 1) * 128])
                ada_chunk(l, j, WA[:, slot, :, :], 0, [4, 5, 6, 7])
                if j % 24 == 23:
                    ada_post(l, j // 24)

        lnp = [0]

        import os as _os
        _lnstep = int(_os.environ.get('LNSTEP', '9'))

        def ln_block(k, t0, n):
            gcol, bcol = ln_cols(k)
            final = (k == nln)
            sample = (t0 >= NPR)
            tok = slice(t0, t0 + n)
            merge = 'lnmerge' in FL
            Xb = X[:, :, tok]
            if merge:
                S.act(SQ[:, :, 0:n], Xb, AF.Square)
                S.copy('dve', XM[:, :, tok], Xb)
            else:
                for c in range(8):
                    S.act(SQ[:, c, 0:n], X[:, c, tok], AF.Square)
                    S.copy('dve', XM[:, c, tok], X[:, c, tok])
            pm = ps[6][:, 0:n]
            pq = ps[7][:, 0:n]
            for c in range(8):
                S.mm(pm, ONESA[:, :], XM[:, c, tok], start=(c == 0), stop=(c == 7))
            for c in range(8):
                S.mm(pq, ONESA[:, :], SQ[:, c, 0:n], start=(c == 0), stop=(c == 7))
            mean = MR[:, 0, 0:n]; rstd = MR[:, 1, 0:n]
            S.copy('act', mean, pm)
            S.act(rstd, pm, AF.Square)
            S.tt('dve', rstd, pq, rstd, ALU.subtract)
            S.act(rstd, rstd, AF.Ln, bias=EPS[:, 0:1], scale=1.0)
            S.act(rstd, rstd, AF.Exp, scale=-0.5)
            if merge:
                S.tt('dve', Xb, Xb, mean.unsqueeze(1).to_broadcast([128, 8, n]), ALU.subtract)
            for c in range(8):
                xc = X[:, c, tok]
                if not merge:
                    S.tt('dve', xc, xc, mean, ALU.subtract)
                S.stt('dve', xc, xc, CST[:, gcol + c:gcol + c + 1], rstd, ALU.mult, ALU.mult)
                if final:
                    if not merge:
                        S.act(xc, xc, AF.Identity, bias=CST[:, bcol + c:bcol + c + 1], scale=1.0)
                    continue
                l, i = divmod(k, 3)
                if not sample:
                    S.act(XM[:, c, tok], xc, AF.Identity, bias=BC[:, k, c:c + 1],
                          scale=ADA[:, l, (3 * i + 1) * 8 + c, 0:1])
                else:
                    S.act(XS[:, 0:n], xc, AF.Identity, bias=CST[:, bcol + c:bcol + c + 1], scale=1.0)
                    xs3 = XS[:, 0:n].rearrange("p (s t) -> p s t", t=8)
                    scb = ADA[:, l, (3 * i + 1) * 8 + c, 1:17].unsqueeze(2).to_broadcast([128, 16, 8])
                    shb = ADA[:, l, (3 * i) * 8 + c, 1:17].unsqueeze(2).to_broadcast([128, 16, 8])
                    S.tt('dve', xs3, xs3, scb, ALU.mult)
                    S.tt('dve', XM[:, c, tok].rearrange("p (s t) -> p s t", t=8), xs3, shb, ALU.add)
                if not merge:
                    S.ts('dve', xc, xc, ALPHA, AB[:, k, c:c + 1], ALU.mult, ALU.add)
            if merge:
                if final:
                    S.tt('dve', Xb, Xb, CST[:, bcol:bcol + 8].unsqueeze(2).to_broadcast([128, 8, n]), ALU.add)
                else:
                    S.stt('dve', Xb, Xb, ALPHA, AB[:, k, :].unsqueeze(2).to_broadcast([128, 8, n]), ALU.mult, ALU.add)

        def accumulate(pt, l, i, oc, t0, n):
            xc = X[:, oc, t0:t0 + n]
            j = (3 * i + 2) * 8 + oc
            if t0 < NPR:
                S.stt('dve', xc, pt, ADA[:, l, j, 0:1], xc, ALU.mult, ALU.add)
            else:
                gbc = ADA[:, l, j, 1:17].unsqueeze(2).to_broadcast([128, 16, 8])
                xs3 = XS[:, 0:n].rearrange("p (s t) -> p s t", t=8)
                S.tt('dve', xs3, pt.rearrange("p (s t) -> p s t", t=8), gbc, ALU.mult)
                S.tt('dve', xc, xc, XS[:, 0:n], ALU.add)

        slot_ctr = [0]
        gu_ctr = [0]
        dn_ctr = [0]

        def ffn(l, which):
            i = 0 if which == 0 else 2
            k_next = 3 * l + i + 1
            wgl = wg[which][l].rearrange("(k p) n -> p k n", p=128)
            wul = wu[which][l].rearrange("(k p) n -> p k n", p=128)
            wdl = wd[which][l].rearrange("(j p) n -> p j n", p=128)
            items = []
            slots = {}

            def load(gi):
                f0, G = GROUPS[gi]
                s = slot_ctr[0] % 2; slot_ctr[0] += 1
                base = s * SLOT
                Wg_ = W[:, base:base + 4096].rearrange("p (k n) -> p k n", n=512)
                Wu_ = W[:, base + 4096:base + 8192].rearrange("p (k n) -> p k n", n=512)
                Wd_ = W[:, base + 8192:base + 12288].rearrange("p (j n) -> p j n", n=1024)
                S.dma('pool', Wg_[:, :, 0:G * 128], wgl[:, :, f0 * 128:(f0 + G) * 128])
                S.dma('pool', Wu_[:, :, 0:G * 128], wul[:, :, f0 * 128:(f0 + G) * 128])
                S.dma('pool', Wd_[:, 0:G, :], wdl[:, f0:f0 + G, :])
                slots[gi] = (Wg_, Wu_, Wd_)

            def gu(gi, bi, par):
                f0, G = GROUPS[gi]
                t0, n = FFN_BLOCKS[bi]
                Wg_, Wu_, Wd_ = slots[gi]
                for j in range(G):
                    q = gu_ctr[0] % 2; gu_ctr[0] += 1
                    pg = ps[2 * q][:, 0:n]; pu = ps[2 * q + 1][:, 0:n]
                    for k in range(8):
                        S.mm(pg, Wg_[:, k, j * 128:(j + 1) * 128], XM[:, k, t0:t0 + n], start=(k == 0), stop=(k == 7))
                    for k in range(8):
                        S.mm(pu, Wu_[:, k, j * 128:(j + 1) * 128], XM[:, k, t0:t0 + n], start=(k == 0), stop=(k == 7))
                    S.act(SG[:, q, 0:n], pg, AF.Silu)
                    S.tt('dve', ACTB[:, par, j, 0:n], SG[:, q, 0:n], pu, ALU.mult)

            def down(gi, bi, par):
                f0, G = GROUPS[gi]
                t0, n = FFN_BLOCKS[bi]
                Wg_, Wu_, Wd_ = slots[gi]
                for oc in range(8):
                    pd = ps[4 + dn_ctr[0] % 4][:, 0:n]; dn_ctr[0] += 1
                    for j in range(G):
                        S.mm(pd, Wd_[:, j, oc * 128:(oc + 1) * 128], ACTB[:, par, j, 0:n], start=(j == 0), stop=(j == G - 1))
                    accumulate(pd, l, i, oc, t0, n)
                if gi == len(GROUPS) - 1:
                    for (m0, mn) in MIX_BLOCKS:
                        if t0 <= m0 < t0 + n:
                            ln_block(k_next, m0, mn)

            load(0)
            prev = None
            par = 0
            for gi in range(len(GROUPS)):
                if gi + 1 < len(GROUPS):
                    pass
                for bi in range(len(FFN_BLOCKS)):
                    if bi == 0 and gi + 1 < len(GROUPS):
                        pending_load = gi + 1
                    else:
                        pending_load = None
                    gu(gi, bi, par)
                    if prev is not None:
                        down(*prev)
                    if pending_load is not None:
                        load(pending_load)
                    prev = (gi, bi, par)
                    par ^= 1
                    if l == 0:
                        ada_pump(2 if which == 0 else 3)
            down(*prev)
            if l == 0 and which == 1:
                ada_pump(1000)

        hbp = [0]

        def next_hb():
            r = ps[hbp[0] % 6][:, 0:256]; hbp[0] += 1
            return r

        def mixer(l):
            k_next = 3 * l + 2
            S.dma('sp', MSK[:, 0, :], masks[0])
            S.dma('sp', MSK[:, 1, :], masks[1])
            S.dma('sp', STG[:, :].rearrange("p (h t) -> p h t", t=128), sgu_wT[l].rearrange("h s t -> s h t"))
            for h in range(4):
                S.tt('dve', WTP[:, h, :], STG[:, h * 128:(h + 1) * 128], MSK[:, 0, :], ALU.mult)
            for h in range(4):
                for sq in range(16):
                    src = sgu_wT[l, h, 0:8, 0:8].unsqueeze(1).to_broadcast([8, 16, 8])
                    S.dma('sp', STG[sq * 8:(sq + 1) * 8, h * 128:(h + 1) * 128].rearrange("p (a b) -> p a b", b=8), src)
            for h in range(4):
                S.tt('dve', WTS[:, h, :], STG[:, h * 128:(h + 1) * 128], MSK[:, 1, :], ALU.mult)
            for c in range(2):
                for hh in range(2):
                    S.dma('sp', BTP[hh * 64:(hh + 1) * 64, c, :], sgu_b[l, 2 * c + hh:2 * c + hh + 1, :].to_broadcast([64, 128]))
            S.dma('sp', LNG[:, 0, :], sgu_ln[l, 0:1, :].to_broadcast([128, 256]))
            S.dma('sp', LNG[:, 1, :], sgu_ln[l, 1:2, :].to_broadcast([128, 256]))
            S.memset('dve', STG[:, 0:256], 0.0)
            for g in range(4):
                c, hh = divmod(g, 2)
                S.dma('sp', STG[hh * 64:(hh + 1) * 64, c * 128 + hh * 64:c * 128 + hh * 64 + 64], pool_w[l, g])
            S.copy('dve', PWB[:, :, :].rearrange("p c n -> p (c n)"), STG[:, 0:256])
            for c in range(2):
                for kk in range(31):
                    S.act(DG[:, c * 31 + kk, :], CST[:, COFF['idn32']:COFF['idn32'] + 32], AF.Identity,
                          scale=cc('dconv_w', (l * 2 + c) * 31 + kk), bias=0.0)
            wi = w_in[l].rearrange("(k p) n -> p k n", p=128)
            wo = w_out[l].rearrange("(k p) n -> p k n", p=128)
            Wp = [W[:, q * PIECE:(q + 1) * PIECE].rearrange("p (k n) -> p k n", n=512) for q in range(6)]
            for q in range(4):
                S.dma('pool', Wp[q], wi[:, :, q * 512:(q + 1) * 512])
            for q in range(2):
                S.dma('pool', Wp[4 + q], wo[:, :, q * 512:(q + 1) * 512])

            def wcol(oc):
                q, r = divmod(oc, 4)
                return lambda k: Wp[q][:, k, r * 128:(r + 1) * 128]

            def proj(oc, t0, n):
                pt = next_hb()[:, 0:n]
                wc = wcol(oc)
                for k in range(8):
                    S.mm(pt, wc(k), XM[:, k, t0:t0 + n], start=(k == 0), stop=(k == 7))
                return pt

            def front(bi, t0, n):
                sample = t0 >= NPR
                NS, T = (16, 8) if sample else (1, n)
                v3 = lambda ap: ap.rearrange("p (s t) -> p s t", t=T)

                def ext(buf, c, P):
                    L = P + T
                    return buf[:, c, 0:NS * L].rearrange("p (s t) -> p s t", t=L)
                if sample:
                    for c in range(2):
                        S.dma('pool', ext(ED, c, 30)[:, :, 0:30], st_dconv[l, :, c])
                        S.dma('sp', o_dconv_s[l, :, c, :, 0:22], st_dconv[l, :, c, :, 8:30])
                elif bi == 0:
                    for c in range(2):
                        S.memset('pool', ED[:, c, 0:30], 0.0)
                for c in range(2):
                    E = ext(ED, c, 30)
                    pt = proj(14 + c, t0, n)
                    S.act(H[:, c, 0:n], pt, AF.Sigmoid)
                    pt = proj(12 + c, t0, n)
                    S.tt('dve', E[:, :, 30:30 + T], v3(H[:, c, 0:n]), v3(pt), ALU.mult)
                    if sample:
                        S.tt('dve', XS[:, 0:n], H[:, c, 0:n], pt, ALU.mult)
                        S.dma('sp', o_dconv_s[l, :, c, :, 22:30], XS[:, 0:n].rearrange("p (s t) -> p s t", t=8))
                    elif bi == 7:
                        S.tt('dve', XS[:, 64:94], H[:, c, n - 30:n], pt[:, n - 30:n], ALU.mult)
                        S.dma('sp', o_dconv_p[l, :, c], XS[:, 64:94])
                for c in range(2):
                    E = ext(ED, c, 30)
                    pc = next_hb()[:, 0:n]
                    for kk in range(31):
                        for g in range(4):
                            S.mm(v3(pc[32 * g:32 * g + 32, :]), DG[32 * g:32 * g + 32, c * 31 + kk, :],
                                 E[32 * g:32 * g + 32, :, kk:kk + T], start=(kk == 0), stop=(kk == 30),
                                 tile_position=(32 * g, 32 * g))
                    S.act(H[:, c, 0:n], pc, AF.Identity, bias=cc('dconv_b', l * 2 + c), scale=1.0)
                BE = 'dve' if 'bdve' in FL else 'pool'
                if sample:
                    for c in range(2):
                        S.dma('sp', ext(EB, c, 15)[:, :, 0:15], st_pool[l, :, c])
                elif bi == 0:
                    for c in range(2):
                        S.memset(BE, EB[:, c, 0:15], 0.0)
                for c in range(2):
                    pt = proj(4 + c, t0, n)
                    S.copy('act', ext(EB, c, 15)[:, :, 15:15 + T], v3(pt))
                L = 15 + T
                for c in range(2):
                    E = ext(EB, c, 15)
                    A = SA[:, 0:NS * L].rearrange("p (s t) -> p s t", t=L)
                    Bv = SBB[:, 0:NS * L].rearrange("p (s t) -> p s t", t=L)
                    xb_new = E[:, :, 15:L]
                    pbv = v3(PB[:, c, 0:n])
                    S.tt(BE, A[:, :, 1:L], E[:, :, 1:L], E[:, :, 0:L - 1], ALU.add)
                    if c == 0:
                        S.tt(BE, Bv[64:128, :, 3:L], A[64:128, :, 3:L], A[64:128, :, 1:L - 2], ALU.add)
                        S.stt('dve', pbv[0:64], A[0:64, :, 15:L], 0.5, xb_new[0:64], ALU.mult, ALU.subtract)
                        S.stt('dve', pbv[64:128], Bv[64:128, :, 15:L], 0.25, xb_new[64:128], ALU.mult, ALU.subtract)
                    else:
                        S.tt(BE, Bv[:, :, 3:L], A[:, :, 3:L], A[:, :, 1:L - 2], ALU.add)
                        S.tt(BE, A[:, :, 7:L], Bv[:, :, 7:L], Bv[:, :, 3:L - 4], ALU.add)
                        S.tt(BE, Bv[64:128, :, 15:L], A[64:128, :, 15:L], A[64:128, :, 7:L - 8], ALU.add)
                        S.stt('dve', pbv[0:64], A[0:64, :, 15:L], 0.125, xb_new[0:64], ALU.mult, ALU.subtract)
                        S.stt('dve', pbv[64:128], Bv[64:128, :, 15:L], 0.0625, xb_new[64:128], ALU.mult, ALU.subtract)
                    if (not sample) and bi == 0:
                        ic = CST[:, COFF['invcnt'] + c * 16:COFF['invcnt'] + (c + 1) * 16]
                        tq = XS[:, 64:80]
                        for (p0, p1, src) in ((0, 64, A), (64, 128, Bv)):
                            S.tt(BE, tq[p0:p1], src[p0:p1, 0, 15:31], ic[p0:p1], ALU.mult)
                            S.tt(BE, PB[p0:p1, c, 0:16], tq[p0:p1], E[p0:p1, 0, 15:31], ALU.subtract)
                    pt = next_hb()[:, 0:n]
                    S.mm(pt, PWB[:, c, :], PB[:, c, 0:n], start=True, stop=True)
                    S.act(Y[:, 2 + c, 0:n], pt, AF.Identity, scale=cc('pool_scale', l * 2 + c), bias=0.0)
                if sample:
                    for c in range(2):
                        S.dma('sp', ext(EZ, c, 2)[:, :, 0:2], st_sconv[l, :, c])
                elif bi == 0:
                    for c in range(2):
                        S.memset('pool', EZ[:, c, 0:2], 0.0)
                for c in range(2):
                    E = ext(EZ, c, 2)
                    pt = proj(8 + c, t0, n)
                    S.copy('act', E[:, :, 2:2 + T], v3(pt))
                    pt = proj(10 + c, t0, n)
                    S.tt('dve', E[:, :, 2:2 + T], E[:, :, 2:2 + T], v3(pt), ALU.mult)
                    acc = v3(Y[:, 4 + c, 0:n])
                    wcol_ = lambda kk: cc('sconv_w', (l * 2 + c) * 3 + kk)
                    S.ts('dve', acc, E[:, :, 2:2 + T], wcol_(2), None, ALU.mult)
                    S.stt('dve', acc, E[:, :, 1:1 + T], wcol_(1), acc, ALU.mult, ALU.add)
                    S.stt('dve', acc, E[:, :, 0:T], wcol_(0), acc, ALU.mult, ALU.add)
                    pt = proj(6 + c, t0, n)
                    S.tt('dve', Y[:, 4 + c, 0:n], Y[:, 4 + c, 0:n], pt, ALU.mult)
                for c in range(2):
                    pt = proj(c, t0, n)
                    S.act(U[:, c, 0:n], pt, AF.Gelu)
                for ti in range(n // 128):
                    tt0 = t0 + ti * 128
                    if ti == 0:
                        VG, VB, BST = VG0, VB0, BST0
                    else:
                        VG = Y[:, 7, :]
                        VB = Y[:, 6, 0:128].bitcast(BF16)
                        BST = Y[:, 6, 128:144]
                    pv = next_hb()
                    for k in range(8):
                        S.mm(pv, XM[:, k, tt0:tt0 + 128], Wp[0][:, k, 256:512], start=(k == 0), stop=(k == 7))
                    S.act(VG[:, :], pv, AF.Gelu)
                    S.generic('dve', lambda e, b_=BST, v_=VG: e.bn_stats(b_[:, 0:6], v_[:, :]), [VG[:, :]], [BST[:, 0:6]])
                    S.generic('dve', lambda e, b_=BST: e.bn_aggr(b_[:, 8:10], b_[:, 0:6]), [BST[:, 0:6]], [BST[:, 8:10]])
                    S.act(BST[:, 9:10], BST[:, 9:10], AF.Sqrt, bias=EPS[:, 0:1], scale=1.0)
                    S.recip(BST[:, 9:10], BST[:, 9:10])
                    S.stt('dve', BST[:, 10:11], BST[:, 8:9], -1.0, BST[:, 9:10], ALU.mult, ALU.mult)
                    S.act(VG[:, :], VG[:, :], AF.Identity, bias=BST[:, 10:11], scale=BST[:, 9:10])
                    S.tt('dve' if 'lngdve' in FL else 'pool', VG[:, :], VG[:, :], LNG[:, 0, :], ALU.mult)
                    S.tt('dve', VG[:, :], VG[:, :], LNG[:, 1, :], ALU.add)
                    S.copy('act', VB[:, :], VG[:, :])
                    if tt0 == NPR - 128:
                        S.dma('sp', o_sgu_p[l], VG[:, :])
                    if sample:
                        S.dma('sp', o_sgu_s[l], VG[:, :])
                    pm = next_hb()
                    WT = WTS if sample else WTP
                    for c in range(2):
                        for hh in range(2):
                            h = 2 * c + hh
                            S.mm(pm[hh * 64:(hh + 1) * 64, c * 128:(c + 1) * 128], VB[:, h * 64:(h + 1) * 64], WT[:, h, :],
                                 start=True, stop=True)
                    tmp = Y[:, 0:2, ti * 128:(ti + 1) * 128]
                    if not sample:
                        S.tt('dve', tmp, pm.rearrange("p (c t) -> p c t", t=128), BTP[:, :, :], ALU.add)
                    else:
                        for c in range(2):
                            S.tt('dve', tmp[:, c, :].rearrange("p (s t) -> p s t", t=8),
                                 pm[:, c * 128:(c + 1) * 128].rearrange("p (s t) -> p s t", t=8),
                                 BTP[:, c, 0:8].unsqueeze(1).to_broadcast([128, 16, 8]), ALU.add)
                    S.tt('dve', Y[:, 0:2, ti * 128:(ti + 1) * 128], tmp, U[:, 0:2, ti * 128:(ti + 1) * 128], ALU.mult)
                for c in range(2):
                    S.copy('act', HB[:, 0, c, 0:n], H[:, c, 0:n])
                    S.act(HB[:, 1, c, 0:n], H[:, c, 0:n], AF.Square)
                pm_ = next_hb()[:, 0:n]
                pq_ = next_hb()[:, 0:n]
                for c in range(2):
                    S.mm(pm_, ONESB[:, :], HB[:, 0, c, 0:n], start=(c == 0), stop=(c == 1))
                for c in range(2):
                    S.mm(pq_, ONESB[:, :], HB[:, 1, c, 0:n], start=(c == 0), stop=(c == 1))
                mean = MR[:, 0, 0:n]; rstd = MR[:, 1, 0:n]
                S.copy('act', mean, pm_)
                S.act(rstd, pm_, AF.Square)
                S.tt('dve', rstd, pq_, rstd, ALU.subtract)
                S.act(rstd, rstd, AF.Ln, bias=EPS[:, 0:1], scale=1.0)
                S.act(rstd, rstd, AF.Exp, scale=-0.5)
                for c in range(2):
                    hc_ = H[:, c, 0:n]
                    S.tt('dve', hc_, hc_, mean, ALU.subtract)
                    S.tt('dve', hc_, hc_, rstd, ALU.mult)
                    S.act(Y[:, 6 + c, 0:n], hc_, AF.Silu, bias=cc('conv_ln_b', l * 2 + c), scale=cc('conv_ln_g', l * 2 + c))
                for c in range(2):
                    if sample:
                        S.dma('sp', o_pool_s[l, :, c], ext(EB, c, 15)[:, :, T:T + 15])
                        S.dma('sp', o_sconv_s[l, :, c], ext(EZ, c, 2)[:, :, T:T + 2])
                    elif bi == 7:
                        S.dma('sp', o_pool_p[l, :, c], EB[:, c, T:T + 15])
                        S.dma('sp', o_sconv_p[l, :, c], EZ[:, c, T:T + 2])
                    else:
                        S.copy('pool' if 'carrypool' in FL else 'act', XS[:, 0:15], EB[:, c, T:T + 15])
                        S.copy('pool' if 'carrypool' in FL else 'act', EB[:, c, 0:15], XS[:, 0:15])
                        S.copy('pool' if 'carrypool' in FL else 'act', XS[:, 16:18], EZ[:, c, T:T + 2])
                        S.copy('pool' if 'carrypool' in FL else 'act', EZ[:, c, 0:2], XS[:, 16:18])
                        S.copy('pool' if 'carrypool' in FL else 'act', XS[:, 32:62], ED[:, c, T:T + 30])
                        S.copy('pool' if 'carrypool' in FL else 'act', ED[:, c, 0:30], XS[:, 32:62])

            def rms(bi, t0, n):
                for c in range(8):
                    if 'rmsdve' in FL and c % 2 == 1:
                        S.tt('dve', SQ[:, c, 0:n], Y[:, c, 0:n], Y[:, c, 0:n], ALU.mult)
                    else:
                        S.act(SQ[:, c, 0:n], Y[:, c, 0:n], AF.Square)
                for g in range(4):
                    pr = next_hb()[:, 0:n]
                    for c in range(2):
                        S.mm(pr, ONESB[:, :], SQ[:, 2 * g + c, 0:n], start=(c == 0), stop=(c == 1))
                    rg = MR[:, g % 2, 0:n]
                    S.act(rg, pr, AF.Ln, bias=EPS[:, 1:2], scale=1.0)
                    S.act(rg, rg, AF.Exp, scale=-0.5)
                    for c in (2 * g, 2 * g + 1):
                        S.stt('dve', XM[:, c, t0:t0 + n], Y[:, c, 0:n], cc('out_norm_g', l * 8 + c), rg,
                              ALU.mult, ALU.mult)

            def back(bi, t0, n):
                for oc in range(8):
                    pt = next_hb()[:, 0:n]
                    q, r = divmod(oc, 4)
                    for k in range(8):
                        S.mm(pt, Wp[4 + q][:, k, r * 128:(r + 1) * 128], XM[:, k, t0:t0 + n], start=(k == 0), stop=(k == 7))
                    accumulate(pt, l, 1, oc, t0, n)
                ln_block(k_next, t0, n)

            nb = len(MIX_BLOCKS)
            for bi, (t0, n) in enumerate(MIX_BLOCKS):
                front(bi, t0, n)
                if bi > 0:
                    back(bi - 1, *MIX_BLOCKS[bi - 1])
                rms(bi, t0, n)
            back(nb - 1, *MIX_BLOCKS[nb - 1])

        def finish():
            for (t0_, n_) in MIX_BLOCKS:
                for c in range(8):
                    S.dma('sp', yT[c][:, t0_:t0_ + n_], X[:, c, t0_:t0_ + n_])
            if stop is not None:
                S.dma('sp', o_ada, ADA[:, :, :, :].rearrange("p l j s -> p (l j s)"))
            S.emit(es)

        stages = []
        if stop in ('setup', 'ada'):
            finish()
            return nc
        import os
        _sel = os.environ.get('LNBLK')
        _blks = MIX_BLOCKS if _sel is None else [MIX_BLOCKS[int(q)] for q in _sel.split(',')]
        stages.append(('ln0', lambda: [ln_block(0, t0, n) for (t0, n) in _blks]))
        for l in range(nlayers):
            stages.append(('ffn1_%d' % l, lambda l=l: ffn(l, 0)))
            stages.append(('mixer_%d' % l, lambda l=l: mixer(l)))
            stages.append(('ffn2_%d' % l, lambda l=l: ffn(l, 1)))
        for name, fn in stages:
            fn()
            if stop == name:
                break
        finish()
    return nc

_PROG_CACHE = {}


def _fm(a, rows):
    return np.ascontiguousarray(np.asarray(a, np.float32).reshape(rows, 128).T)


def _const_tables():
    inv = np.zeros((128, 2, 16), np.float32)
    wins = (2, 4, 8, 16)
    for c in range(2):
        for p in range(128):
            w = wins[2 * c + p // 64]
            for pos in range(16):
                inv[p, c, pos] = 1.0 / min(w, pos + 1)
    m0 = np.triu(np.ones((128, 128), np.float32))
    m1 = np.kron(np.eye(16, dtype=np.float32), np.triu(np.ones((8, 8), np.float32)))
    return inv.reshape(128, 32), np.stack([m0, m1]).astype(np.float32)


def _make_in_maps(x_prompt, x_sample, state_pool, state_sconv, state_dconv, c_prompt, c_sample,
           ln_in_g, ln_in_b, w_ada, b_ada, ffn1_w_gate, ffn1_w_up, ffn1_w_down, w_in,
           sgu_ln_g, sgu_ln_b, sgu_w, sgu_b, pool_w, pool_scale, sconv_w, dconv_w, dconv_b,
           conv_ln_g, conv_ln_b, out_norm_g, w_out, ffn2_w_gate, ffn2_w_up, ffn2_w_down,
           post_ln_g, post_ln_b):
    f32 = lambda a: np.ascontiguousarray(np.asarray(a, dtype=np.float32))
    x_prompt = f32(x_prompt); x_sample = f32(x_sample)
    inv, masks = _const_tables()
    consts = np.zeros((128, NCONST), np.float32)

    def put(name, arr):
        consts[:, COFF[name]:COFF[name] + arr.shape[1]] = arr
    put('ln_in_g', _fm(ln_in_g, 8)); put('ln_in_b', _fm(ln_in_b, 8))
    put('post_g', _fm(post_ln_g, 48)); put('post_b', _fm(post_ln_b, 48))
    put('out_norm_g', _fm(out_norm_g, 16)); put('pool_scale', _fm(pool_scale, 4))
    put('sconv_w', f32(sconv_w).reshape(2, 3, 2, 128).transpose(3, 0, 2, 1).reshape(128, 12))
    put('dconv_w', f32(dconv_w).reshape(2, 31, 2, 128).transpose(3, 0, 2, 1).reshape(128, 124))
    put('dconv_b', _fm(dconv_b, 4)); put('conv_ln_g', _fm(conv_ln_g, 4)); put('conv_ln_b', _fm(conv_ln_b, 4))
    put('b_ada', _fm(b_ada, 144))
    put('invcnt', inv)
    put('idn32', np.tile(np.eye(32, dtype=np.float32), (4, 1)))

    shared = {
        "consts": consts,
        "w_ada": f32(w_ada),
        "ffn1_w_gate": f32(ffn1_w_gate), "ffn1_w_up": f32(ffn1_w_up), "ffn1_w_down": f32(ffn1_w_down),
        "ffn2_w_gate": f32(ffn2_w_gate), "ffn2_w_up": f32(ffn2_w_up), "ffn2_w_down": f32(ffn2_w_down),
        "w_in": f32(w_in), "w_out": f32(w_out),
        "sgu_wT": f32(np.asarray(sgu_w).transpose(0, 1, 3, 2)),
        "sgu_b": f32(sgu_b),
        "sgu_ln": f32(np.stack([np.asarray(sgu_ln_g), np.asarray(sgu_ln_b)], axis=1)),
        "pool_w": f32(pool_w),
        "masks": masks,
    }
    state_pool = np.asarray(state_pool); state_sconv = np.asarray(state_sconv); state_dconv = np.asarray(state_dconv)
    c_prompt = np.asarray(c_prompt); c_sample = np.asarray(c_sample)

    def st_fm(st, i):
        a = st[:, 16 * i:16 * i + 16]
        L_, _, R, _ = a.shape
        a = a.reshape(L_, 16, R, 2, 128).transpose(0, 4, 3, 1, 2)
        return f32(a)

    in_maps = []
    for i in range(8):
        xall = np.concatenate([x_prompt[i], x_sample[16 * i:16 * i + 16].reshape(128, 1024)], axis=0)
        call = np.concatenate([c_prompt[i:i + 1], c_sample[16 * i:16 * i + 16]], axis=0)
        m = dict(shared)
        m["xT"] = f32(xall.T.reshape(8, 128, NT))
        m["cT"] = f32(call.reshape(17, 8, 128).transpose(2, 1, 0))
        m["st_pool"] = st_fm(state_pool, i)
        m["st_sconv"] = st_fm(state_sconv, i)
        m["st_dconv"] = st_fm(state_dconv, i)
        in_maps.append(m)

    return in_maps


def _gather(R):
    y_prompt = np.zeros((8, 2048, 1024), np.float32)
    y_sample = np.zeros((128, 8, 1024), np.float32)
    sgu_p = np.zeros((DEPTH, 8, 128, 256), np.float32)
    sgu_s = np.zeros((DEPTH, 128, 8, 256), np.float32)
    pool_p = np.zeros((DEPTH, 8, 15, 256), np.float32)
    pool_s = np.zeros((DEPTH, 128, 15, 256), np.float32)
    sconv_p = np.zeros((DEPTH, 8, 2, 256), np.float32)
    sconv_s = np.zeros((DEPTH, 128, 2, 256), np.float32)
    dconv_p = np.zeros((DEPTH, 8, 30, 256), np.float32)
    dconv_s = np.zeros((DEPTH, 128, 30, 256), np.float32)
    for i in range(8):
        r = R[i]
        yt = np.asarray(r["yT"]).reshape(1024, NT)
        y_prompt[i] = yt[:, :2048].T
        y_sample[16 * i:16 * i + 16] = yt[:, 2048:].T.reshape(16, 8, 1024)
        sgu_p[:, i] = np.asarray(r["o_sgu_p"])
        sgu_s[:, 16 * i:16 * i + 16] = np.asarray(r["o_sgu_s"]).reshape(DEPTH, 16, 8, 256)
        for (op_, os_, dp, ds, rows) in (("o_pool_p", "o_pool_s", pool_p, pool_s, 15),
                                         ("o_sconv_p", "o_sconv_s", sconv_p, sconv_s, 2),
                                         ("o_dconv_p", "o_dconv_s", dconv_p, dconv_s, 30)):
            a = np.asarray(r[op_])
            dp[:, i] = a.transpose(0, 3, 2, 1).reshape(DEPTH, rows, 256)
            b = np.asarray(r[os_])
            ds[:, 16 * i:16 * i + 16] = b.transpose(0, 3, 4, 2, 1).reshape(DEPTH, 16, rows, 256)
    return (y_prompt, y_sample, sgu_p, sgu_s, pool_p, pool_s, sconv_p, sconv_s, dconv_p, dconv_s)


def kernel(x_prompt, x_sample, state_pool, state_sconv, state_dconv, c_prompt, c_sample,
           ln_in_g, ln_in_b, w_ada, b_ada, ffn1_w_gate, ffn1_w_up, ffn1_w_down, w_in,
           sgu_ln_g, sgu_ln_b, sgu_w, sgu_b, pool_w, pool_scale, sconv_w, dconv_w, dconv_b,
           conv_ln_g, conv_ln_b, out_norm_g, w_out, ffn2_w_gate, ffn2_w_up, ffn2_w_down,
           post_ln_g, post_ln_b):
    in_maps = _make_in_maps(x_prompt, x_sample, state_pool, state_sconv, state_dconv, c_prompt, c_sample,
           ln_in_g, ln_in_b, w_ada, b_ada, ffn1_w_gate, ffn1_w_up, ffn1_w_down, w_in,
           sgu_ln_g, sgu_ln_b, sgu_w, sgu_b, pool_w, pool_scale, sconv_w, dconv_w, dconv_b,
           conv_ln_g, conv_ln_b, out_norm_g, w_out, ffn2_w_gate, ffn2_w_up, ffn2_w_down,
           post_ln_g, post_ln_b)
    if "nc" not in _PROG_CACHE:
        _PROG_CACHE["nc"] = build_program()
    nc = _PROG_CACHE["nc"]
    res = run_bass_kernel_spmd(nc, in_maps, core_ids=list(range(8)))
    return _gather(res.results)
```

```python
from concourse.bass_utils import run_bass_kernel_spmd
import os
import heapq
import numpy as np
import concourse.bass as bass
import concourse.mybir as mybir

F32 = mybir.dt.float32
BF16 = mybir.dt.bfloat16
ALU = mybir.AluOpType
AF = mybir.ActivationFunctionType

CENG = ['pe', 'act', 'dve', 'pool']
DEF_LAT = '0.6'; DEF_LOOK = '12'; DEF_WINDOW = '600'; DEF_SLAT = '0.3'; DEF_TPEN = '1.3'
ALLENG = ['pe', 'act', 'dve', 'pool', 'sp']
_DSZ = {F32: 4, BF16: 2}


def _region(ap):
    name = ap.tensor.name
    steps = ap.ap
    esz = _DSZ.get(ap.dtype, 4)
    sp = str(ap.space)
    if 'DRAM' in sp.upper() or 'HBM' in sp.upper():
        lo = ap.offset
        hi = lo + sum((c - 1) * abs(s) for s, c in steps) + 1
        return (name, 0, 1, lo * esz, hi * esz, None)
    if 'PSUM' in sp.upper():
        return (name, 0, 128, 0, 1 << 20, None)
    pstep, pcnt = steps[0]
    if pstep == 0:
        pstep = 1 << 40
    p0 = ap.offset // pstep if pstep < (1 << 40) else 0
    f0 = ap.offset - p0 * pstep if pstep < (1 << 40) else ap.offset
    free = steps[1:]
    ext = sum((c - 1) * abs(s) for s, c in free) + 1
    rows = None
    if len(free) >= 2:
        s0, c0 = free[0]
        ein = sum((c - 1) * abs(s) for s, c in free[1:]) + 1
        if 1 < c0 <= 16 and s0 > ein:
            rows = tuple(((f0 + r * s0) * esz, (f0 + r * s0 + ein) * esz) for r in range(c0))
    return (name, p0, p0 + pcnt, f0 * esz, (f0 + ext) * esz, rows)


_TSET = {AF.Gelu: 'gelu', AF.Tanh: 'gelu', AF.Silu: 'silu', AF.Sigmoid: 'sigm', AF.Sqrt: 'sqrt', AF.Ln: 'lnexp', AF.Exp: 'lnexp'}


def _nfree(ap):
    n = 1
    for s, c in ap.ap[1:]:
        n *= c
    return n


def _ovl(a, b):
    if not (a[1] < b[2] and b[1] < a[2] and a[3] < b[4] and b[3] < a[4]):
        return False
    ra, rb = a[5], b[5]
    if ra is None and rb is None:
        return True
    ia = ra if ra is not None else ((a[3], a[4]),)
    ib = rb if rb is not None else ((b[3], b[4]),)
    for (x0, x1) in ia:
        for (y0, y1) in ib:
            if x0 < y1 and y0 < x1:
                return True
    return False


def _covers(a, b):
    if not (a[1] <= b[1] and a[2] >= b[2] and a[3] <= b[3] and a[4] >= b[4]):
        return False
    ra, rb = a[5], b[5]
    if ra is None:
        return True
    ib = rb if rb is not None else ((b[3], b[4]),)
    for (y0, y1) in ib:
        if not any(x0 <= y0 and y1 <= x1 for (x0, x1) in ra):
            return False
    return True


class Op:
    __slots__ = ('eng', 'fn', 'pos', 'is_dma', 'dsem', 'dval', 'waits', 'sig', 'vc', 'prewait', 'tag',
                 'idx', 'deps', 'cost', 'xfer', 'nsucc', 'succ', 'start', 'finish', 'raw', 'tset')


class Sched:
    def __init__(self, nc, kdma=8):
        self.nc = nc
        self.ops = []
        self.wr = {}
        self.rd = {}
        self.kdma = kdma

    def _add(self, eng, fn, ins, outs, is_dma=False, cost=0.3, xfer=0.0):
        op = Op()
        op.eng = eng; op.fn = fn; op.is_dma = is_dma; op.sig = False; op.waits = []; op.prewait = None
        op.idx = len(self.ops); op.cost = cost; op.xfer = xfer; op.tset = None
        deps = {}
        raw = set()
        for ap in ins:
            r = _region(ap)
            for (wr_, o) in self.wr.get(r[0], ()):
                if _ovl(r, wr_):
                    deps[o.idx] = o; raw.add(o.idx)
        for ap in outs:
            r = _region(ap)
            for (wr_, o) in self.wr.get(r[0], ()):
                if _ovl(r, wr_):
                    deps[o.idx] = o
            for (rr, o) in self.rd.get(r[0], ()):
                if _ovl(r, rr):
                    deps[o.idx] = o
        for ap in ins:
            r = _region(ap)
            lst = self.rd.setdefault(r[0], [])
            if not is_dma:
                keep = []
                for (rr, o) in lst:
                    if o.eng == eng and (not o.is_dma) and _covers(r, rr):
                        deps[o.idx] = o
                    else:
                        keep.append((rr, o))
                lst[:] = keep
            lst.append((r, op))
        for ap in outs:
            r = _region(ap)
            wl = self.wr.setdefault(r[0], [])
            wl[:] = [(wr_, o) for (wr_, o) in wl if not _covers(r, wr_)]
            wl.append((r, op))
            rl = self.rd.get(r[0])
            if rl:
                rl[:] = [(rr, o) for (rr, o) in rl if not _covers(r, rr)]
        deps.pop(op.idx, None)
        op.deps = list(deps.values())
        op.raw = raw
        self.ops.append(op)
        return op

    def mm(self, out, lhsT, rhs, start=True, stop=True, **kw):
        n = _nfree(rhs)
        cost = max(n, 64) / 2400.0 + 0.012
        if rhs.dtype == F32:
            cost *= 4.0
        if 'tile_position' in kw:
            cost *= 0.27
        return self._add('pe', lambda e: e.matmul(out, lhsT, rhs, start=start, stop=stop, **kw),
                         [lhsT, rhs] + ([] if start else [out]), [out], cost=cost)

    def act(self, out, in_, func, bias=None, scale=None):
        ins = [in_]
        kw = {}
        if bias is not None:
            kw['bias'] = bias
            if not isinstance(bias, (int, float)):
                ins.append(bias)
        if scale is not None:
            kw['scale'] = scale
            if not isinstance(scale, (int, float)):
                ins.append(scale)
        o = self._add('act', lambda e: e.activation(out, in_, func, **kw), ins, [out],
                      cost=0.2 + _nfree(out) / 1200.0)
        o.tset = _TSET.get(func)
        return o

    def _vcost(self, eng, out, mult=1.0):
        n = _nfree(out)
        if eng == 'pool':
            return 0.3 + n * 2.2 / 1200.0
        return 0.12 + mult * n / 960.0

    def tt(self, eng, out, in0, in1, op):
        return self._add(eng, lambda e: e.tensor_tensor(out, in0, in1, op), [in0, in1], [out], cost=self._vcost(eng, out))

    def ts(self, eng, out, in0, s1, s2, op0, op1=None):
        ins = [in0] + [s for s in (s1, s2) if s is not None and not isinstance(s, (int, float))]
        if op1 is None:
            return self._add(eng, lambda e: e.tensor_single_scalar(out, in0, s1, op0), ins, [out], cost=self._vcost(eng, out))
        return self._add(eng, lambda e: e.tensor_scalar(out, in0, s1, s2, op0, op1), ins, [out], cost=self._vcost(eng, out))

    def stt(self, eng, out, in0, scalar, in1, op0, op1):
        ins = [in0, in1] + ([] if isinstance(scalar, (int, float)) else [scalar])
        return self._add(eng, lambda e: e.scalar_tensor_tensor(out, in0, scalar, in1, op0, op1), ins, [out],
                         cost=self._vcost(eng, out))

    def copy(self, eng, out, in_):
        if eng == 'act':
            return self._add('act', lambda e: e.activation(out, in_, AF.Copy), [in_], [out], cost=0.2 + _nfree(out) / 1200.0)
        return self._add(eng, lambda e: e.tensor_copy(out, in_), [in_], [out], cost=self._vcost(eng, out))

    def memset(self, eng, out, val):
        return self._add(eng, lambda e: e.memset(out, val), [], [out], cost=self._vcost(eng, out))

    def recip(self, out, in_):
        return self._add('dve', lambda e: e.reciprocal(out, in_), [in_], [out], cost=self._vcost('dve', out, 4.0))

    def generic(self, eng, fn, ins, outs, cost=0.4):
        return self._add(eng, fn, ins, outs, cost=cost)

    def dma(self, eng, out, in_, **kw):
        nbytes = 1
        for d_ in out.shape:
            nbytes *= d_
        nbytes *= max(_DSZ.get(out.dtype, 4), _DSZ.get(in_.dtype, 4))
        return self._add(eng, lambda e, sem, val: e.dma_start(out=out, in_=in_, **kw).then_inc(sem, 16),
                         [in_], [out], is_dma=True, cost=(1.2 if eng == 'pool' else 0.1), xfer=2.0 + nbytes / 150e3)

    def _schedule(self):
        ops = self.ops
        LAT = float(os.environ.get('SCHED_LAT', DEF_LAT))
        LOOK = int(os.environ.get('SCHED_LOOK', DEF_LOOK))
        SLAT = float(os.environ.get('SCHED_SLAT', DEF_SLAT))
        TPEN = float(os.environ.get('SCHED_TPEN', DEF_TPEN))
        if os.environ.get('NOSCHED'):
            t = 0.0
            for op in ops:
                op.start = t; op.finish = t + 1e-3; t += 1e-3
            return list(ops)
        for op in ops:
            op.nsucc = len(op.deps); op.succ = []
        for op in ops:
            for d in op.deps:
                d.succ.append(op)
        cand = {e: [] for e in ALLENG}
        rtime = {}
        for op in ops:
            if op.nsucc == 0:
                heapq.heappush(cand[op.eng], (op.idx, op)); rtime[op.idx] = 0.0
        free = {e: 0.0 for e in ALLENG}
        dma_free = 0.0
        cur_tset = [None]
        order = []
        nleft = len(ops)
        WINDOW = int(os.environ.get('SCHED_WINDOW', DEF_WINDOW))
        while nleft:
            best = None
            for e in ALLENG:
                h = cand[e]
                if not h:
                    continue
                tfree = free[e]
                pick = None; pick_t = None
                look = heapq.nsmallest(LOOK + 4 if e == 'act' else LOOK, h)
                lo_idx = look[0][0]
                for (idx, o) in look:
                    if idx - lo_idx > WINDOW:
                        break
                    rt = rtime[idx]
                    st = rt if rt > tfree else tfree
                    if e == 'act' and o.tset is not None and o.tset != cur_tset[0]:
                        st += TPEN
                    if pick is None or st < pick_t - 1e-9:
                        pick = o; pick_t = st
                    if e != 'act' and rt <= tfree:
                        break
                if best is None or pick_t < best[0] - 1e-9 or (abs(pick_t - best[0]) <= 1e-9 and pick.idx < best[1].idx):
                    best = (pick_t, pick, e)
            st, op, e = best
            h = cand[e]
            h.remove((op.idx, op)); heapq.heapify(h)
            op.start = st
            if op.is_dma:
                free[e] = st + op.cost
                xs = max(st + op.cost, dma_free)
                dma_free = xs + (op.xfer - 2.0) * 0.5
                op.finish = xs + op.xfer
            else:
                if e == 'act' and op.tset is not None:
                    cur_tset[0] = op.tset
                op.finish = st + op.cost
                free[e] = op.finish
            order.append(op)
            nleft -= 1
            for s in op.succ:
                s.nsucc -= 1
                lat = (0.0 if op.eng == 'pe' else SLAT) if (s.eng == op.eng and not op.is_dma) else LAT
                rt = op.finish + lat
                if rtime.get(s.idx, 0.0) < rt:
                    rtime[s.idx] = rt
                if s.nsucc == 0:
                    heapq.heappush(cand[s.eng], (s.idx, s))
        order.sort(key=lambda o: (o.start, o.idx))
        self.sim_end = max(o.finish for o in order)
        return order

    def _waits(self, order):
        self.streams = {e: [] for e in ALLENG}
        self.cops = {e: [] for e in CENG}
        cpos = {e: 0 for e in CENG}
        vcs = {e: {c: 0 for c in CENG} for e in ALLENG}
        dknown = {e: {} for e in ALLENG}
        ndma = {e: 0 for e in ALLENG}
        for op in order:
            eng = op.eng
            vc = vcs[eng]; dk = dknown[eng]
            best = {}; dmab = {}
            for d in op.deps:
                if d.is_dma:
                    if d.dsem not in dmab or d.dval > dmab[d.dsem].dval:
                        dmab[d.dsem] = d
                else:
                    a = d.eng
                    if a == eng and eng == 'pe':
                        continue
                    if a not in best or d.pos > best[a].pos:
                        best[a] = d
            for key, d in dmab.items():
                if dk.get(key, 0) >= d.dval:
                    continue
                op.waits.append(('dma', key, d.dval))
                dk[key] = d.dval
                for c in CENG:
                    if d.vc[c] > vc[c]:
                        vc[c] = d.vc[c]
            for a, d in best.items():
                if vc[a] >= d.pos:
                    continue
                op.waits.append(('eng', a, d.pos)); d.sig = True
                for c in CENG:
                    if d.vc[c] > vc[c]:
                        vc[c] = d.vc[c]
                if d.pos > vc[a]:
                    vc[a] = d.pos
            if op.is_dma:
                i = ndma[eng]; ndma[eng] += 1
                op.dsem = (eng, i % self.kdma)
                op.dval = 16 * (i // self.kdma + 1)
                if i >= self.kdma:
                    op.prewait = (op.dsem, op.dval - 16)
                    if dk.get(op.dsem, 0) < op.dval - 16:
                        dk[op.dsem] = op.dval - 16
                op.pos = None
                op.vc = dict(vc)
            else:
                cpos[eng] += 1
                op.pos = cpos[eng]
                self.cops[eng].append(op)
                op.vc = dict(vc)
                op.vc[eng] = op.pos
            self.streams[eng].append(op)
        self.ndma = ndma

    def emit(self, es):
        nc = self.nc
        order = self._schedule()
        self._waits(order)
        csem = {e: es.enter_context(nc.semaphore('s_' + e)) for e in CENG}
        dsem = {}
        for e in ALLENG:
            for k in range(min(self.kdma, self.ndma[e])):
                dsem[(e, k)] = es.enter_context(nc.semaphore('d_%s%d' % (e, k)))
        count = {}
        for e in CENG:
            n = 0
            for op in self.cops[e]:
                if op.sig:
                    n += 1
                count[(e, op.pos)] = n
        self.count = count
        streams = self.streams
        kd = self.kdma
        nd = self.ndma

        def run(engname, eng):
            for op in streams[engname]:
                if op.prewait is not None:
                    eng.wait_ge(dsem[op.prewait[0]], op.prewait[1])
                for w in op.waits:
                    if w[0] == 'dma':
                        eng.wait_ge(dsem[w[1]], w[2])
                    else:
                        eng.wait_ge(csem[w[1]], count[(w[1], w[2])])
                if op.is_dma:
                    op.fn(eng, dsem[op.dsem], op.dval)
                else:
                    ins = op.fn(eng)
                    if op.sig:
                        ins.then_inc(csem[engname], 1)
            n = nd[engname]
            for k in range(min(kd, n)):
                cnt = (n - k + kd - 1) // kd
                eng.wait_ge(dsem[(engname, k)], 16 * cnt)

        block = es.enter_context(nc.Block())

        @block.tensor
        def _(e):
            run('pe', e)

        @block.scalar
        def _(e):
            run('act', e)

        @block.vector
        def _(e):
            run('dve', e)

        @block.gpsimd
        def _(e):
            run('pool', e)

        @block.sync
        def _(e):
            run('sp', e)

import contextlib

D = 1024; KC = 8; FF = 2816; NFC = 22; DEPTH = 2
NT = 2176; NPR = 2048; NSM = 128
ALPHA = (2 * DEPTH) ** 0.25
GROUPS = [(0, 4), (4, 4), (8, 2), (10, 4), (14, 4), (18, 4)]
FFN_BLOCKS = [(0, 512), (512, 512), (1024, 512), (1536, 512), (2048, 128)]
MIX_BLOCKS = [(i * 256, 256) for i in range(8)] + [(2048, 128)]
SLOT = 12288
PIECE = 4096

DEFAULT_FLAGS = 'castdve,rescdve,subdve,lngdve,bdve'

def _const_layout():
    off = {}
    n = 0
    def add(name, k):
        nonlocal n
        off[name] = n; n += k
    add('ln_in_g', 8); add('ln_in_b', 8)
    add('post_g', 48); add('post_b', 48)
    add('out_norm_g', 16); add('pool_scale', 4)
    add('sconv_w', 12); add('dconv_w', 124); add('dconv_b', 4)
    add('conv_ln_g', 4); add('conv_ln_b', 4)
    add('b_ada', 144)
    add('invcnt', 32)
    add('idn32', 32)
    return off, n
COFF, NCONST = _const_layout()


def build_program(nlayers=DEPTH, stop=None):
    import os as _os2
    FL = set(_os2.environ.get('KFLAGS', DEFAULT_FLAGS).split(','))
    nc = bass.Bass("TRN2", target_bir_lowering=False)
    dt_in = lambda name, shape: nc.dram_tensor(name, shape, F32, kind="ExternalInput").ap()
    dt_out = lambda name, shape: nc.dram_tensor(name, shape, F32, kind="ExternalOutput").ap()
    xT = dt_in("xT", [8, 128, NT])
    cT = dt_in("cT", [128, 8, 17])
    consts = dt_in("consts", [128, NCONST])
    w_ada = dt_in("w_ada", [DEPTH, D, 9 * D])
    wg = [dt_in("ffn1_w_gate", [DEPTH, D, FF]), dt_in("ffn2_w_gate", [DEPTH, D, FF])]
    wu = [dt_in("ffn1_w_up", [DEPTH, D, FF]), dt_in("ffn2_w_up", [DEPTH, D, FF])]
    wd = [dt_in("ffn1_w_down", [DEPTH, FF, D]), dt_in("ffn2_w_down", [DEPTH, FF, D])]
    w_in = dt_in("w_in", [DEPTH, D, 2048])
    w_out = dt_in("w_out", [DEPTH, D, D])
    sgu_wT = dt_in("sgu_wT", [DEPTH, 4, 128, 128])
    sgu_b = dt_in("sgu_b", [DEPTH, 4, 128])
    sgu_ln = dt_in("sgu_ln", [DEPTH, 2, 256])
    pool_w = dt_in("pool_w", [DEPTH, 4, 64, 64])
    masks = dt_in("masks", [2, 128, 128])
    st_pool = dt_in("st_pool", [DEPTH, 128, 2, 16, 15])
    st_sconv = dt_in("st_sconv", [DEPTH, 128, 2, 16, 2])
    st_dconv = dt_in("st_dconv", [DEPTH, 128, 2, 16, 30])

    yT = dt_out("yT", [8, 128, NT])
    o_sgu_p = dt_out("o_sgu_p", [DEPTH, 128, 256])
    o_sgu_s = dt_out("o_sgu_s", [DEPTH, 128, 256])
    o_pool_p = dt_out("o_pool_p", [DEPTH, 128, 2, 15])
    o_pool_s = dt_out("o_pool_s", [DEPTH, 128, 2, 16, 15])
    o_sconv_p = dt_out("o_sconv_p", [DEPTH, 128, 2, 2])
    o_sconv_s = dt_out("o_sconv_s", [DEPTH, 128, 2, 16, 2])
    o_dconv_p = dt_out("o_dconv_p", [DEPTH, 128, 2, 30])
    o_dconv_s = dt_out("o_dconv_s", [DEPTH, 128, 2, 16, 30])

    o_ada = dt_out("o_ada", [128, DEPTH * 72 * 17]) if stop is not None else None
    es = contextlib.ExitStack()
    with es:
        S = Sched(nc)
        sb = lambda name, shape, dt=F32: es.enter_context(nc.sbuf_tensor(name, shape, dt))
        X = sb("X", [128, 8, NT])
        XM = sb("XM", [128, 8, NT], BF16)
        W = sb("W", [128, 2 * SLOT], BF16)
        SQ = sb("SQ", [128, 8, 256], BF16)
        MR = sb("MR", [128, 2, 256])
        ADA = sb("ADA", [128, DEPTH, 72, 17])
        CST = sb("CST", [128, NCONST])
        BC = sb("BC", [128, DEPTH * 3 + 1, 8])
        AB = sb("AB", [128, DEPTH * 3 + 1, 8])
        EPS = sb("EPS", [128, 2])
        CSB = sb("CSB", [128, 8, 17], BF16)
        ONESA = sb("ONESA", [128, 128], BF16)
        ONESB = sb("ONESB", [128, 128], BF16)
        WTP = sb("WTP", [128, 4, 128], BF16)
        WTS = sb("WTS", [128, 4, 128], BF16)
        BTP = sb("BTP", [128, 2, 128])
        LNG = sb("LNG", [128, 2, 256])
        PWB = sb("PWB", [128, 2, 128], BF16)
        XS = sb("XS", [128, 128])
        DG = sb("DG", [128, 62, 32], BF16)
        NSCR = 6352
        SCR = sb("SCR", [128, NSCR])

        def carve(w0, shape, dt=F32):
            nel = 1
            for d_ in shape:
                nel *= d_
            nw = nel if dt == F32 else nel // 2
            a = SCR[:, w0:w0 + nw]
            if dt != F32:
                a = a.bitcast(dt)
            if len(shape) == 1:
                return a
            if len(shape) == 2:
                return a.rearrange("p (a b) -> p a b", b=shape[1])
            return a.rearrange("p (a b c) -> p a b c", b=shape[1], c=shape[2])
        Y = carve(0, [8, 256]); U = carve(2048, [2, 256]); H = carve(2560, [2, 256])
        VG0 = carve(3072, [256]); VB0 = carve(3328, [256], BF16); PB = carve(3456, [2, 256], BF16)
        SA = carve(3712, [368]); SBB = carve(4080, [368]); EB = carve(4448, [2, 368])
        EZ = carve(5184, [2, 272]); ED = carve(5728, [2, 608], BF16); BST0 = carve(6336, [16])
        SG = carve(0, [2, 512], BF16); ACTB = carve(512, [2, 4, 512], BF16)
        STG = carve(0, [512]); MSK = carve(512, [2, 128])
        HB = SQ[:, 0:4, :].rearrange("p (a c) n -> p a c n", c=2)
        ps = [es.enter_context(nc.psum_tensor("ps%d" % i, [128, 512], F32)) for i in range(8)]
        hb = [ps[i // 2][:, (i % 2) * 256:(i % 2) * 256 + 256] for i in range(16)]

        cc = lambda name, j: CST[:, COFF[name] + j:COFF[name] + j + 1]

        for c in range(8):
            S.dma('sp', X[:, c, :], xT[c])
        S.dma('sp', CST[:, :], consts)
        S.dma('sp', STG[:, 0:136], cT.rearrange("p k s -> p (k s)"))
        S.memset('dve', EPS[:, 0:1], 1e-5)
        S.memset('dve', EPS[:, 1:2], 1e-6)
        S.memset('dve', ONESA[:, :], 1.0 / 1024)
        S.memset('dve', ONESB[:, :], 1.0 / 256)
        S.act(CSB[:, :, :].rearrange("p k s -> p (k s)"), STG[:, 0:136], AF.Silu)

        WA = sb("WA", [128, 2, 8, 128], BF16)
        pcnt = [0]

        def ln_cols(k):
            if k == 0:
                return COFF['ln_in_g'], COFF['ln_in_b']
            return COFF['post_g'] + (k - 1) * 8, COFF['post_b'] + (k - 1) * 8
        nln = 3 * nlayers

        def ada_post(l, i):
            sl = ADA[:, l, (3 * i + 1) * 8:(3 * i + 2) * 8, :]
            S.ts('dve', sl, sl, 1.0, None, ALU.add)
            gl = ADA[:, l, (3 * i + 2) * 8:(3 * i + 3) * 8, :]
            if i == 1:
                S.ts('dve', gl, gl, 1.0, None, ALU.add)
            else:
                S.ts('dve', gl, gl, 1.0, 0.5, ALU.add, ALU.mult)
            k = 3 * l + i
            gcol, bcol = ln_cols(k)
            S.tt('dve', BC[:, k, :], CST[:, bcol:bcol + 8], ADA[:, l, (3 * i + 1) * 8:(3 * i + 2) * 8, 0], ALU.mult)
            S.tt('dve', BC[:, k, :], BC[:, k, :], ADA[:, l, (3 * i) * 8:(3 * i + 1) * 8, 0], ALU.add)

        def ada_chunk(l, j, Wt, jj, banks):
            pt = ps[banks[pcnt[0] % len(banks)]][:, 0:17]; pcnt[0] += 1
            for k in range(8):
                S.mm(pt, Wt[:, k, jj * 128:(jj + 1) * 128], CSB[:, k, :], start=(k == 0), stop=(k == 7))
            S.act(ADA[:, l, j, :], pt, AF.Identity, bias=cc('b_ada', l * 72 + j))

        if stop != 'setup':
            wv0 = w_ada[0].rearrange("(k p) n -> p k n", p=128)
            for n in range(6):
                reg = (n % 6) * PIECE
                Wt = W[:, reg:reg + PIECE].rearrange("p (k n) -> p k n", n=512)
                S.dma('pool', Wt, wv0[:, :, n * 512:(n + 1) * 512])
                for jj in range(4):
                    ada_chunk(0, n * 4 + jj, Wt, jj, list(range(8)))
            for k in range(nln + 1):
                gcol, bcol = ln_cols(k)
                S.ts('dve', AB[:, k, :], CST[:, bcol:bcol + 8], ALPHA, None, ALU.mult)
            ada_post(0, 0)

        ada_todo = [(0, j) for j in range(24, 72)]
        for l in range(1, nlayers):
            for j in range(72):
                ada_todo.append((l, j))
        ada_state = {'i': 0}

        def ada_pump(nmax):
            for _ in range(nmax):
                if ada_state['i'] >= len(ada_todo):
                    return
                l, j = ada_todo[ada_state['i']]
                slot = ada_state['i'] % 2
                ada_state['i'] += 1
                wvl = w_ada[l].rearrange("(k p) n -> p k n", p=128)
                S.dma('pool', WA[:, slot, :, :], wvl[:, :, j * 128:(j + 1) * 128])
                ada_chunk(l, j, WA[:, slot, :, :], 0, [4, 5, 6, 7])
                if j % 24 == 23:
                    ada_post(l, j // 24)

        lnp = [0]

        import os as _os
        _lnstep = int(_os.environ.get('LNSTEP', '9'))

        def ln_block(k, t0, n):
            gcol, bcol = ln_cols(k)
            final = (k == nln)
            sample = (t0 >= NPR)
            tok = slice(t0, t0 + n)
            merge = 'lnmerge' in FL
            Xb = X[:, :, tok]
            if merge:
                S.act(SQ[:, :, 0:n], Xb, AF.Square)
                S.copy('dve', XM[:, :, tok], Xb)
            else:
                for c in range(8):
                    S.act(SQ[:, c, 0:n], X[:, c, tok], AF.Square)
                    S.copy('dve', XM[:, c, tok], X[:, c, tok])
            pm = ps[6][:, 0:n]
            pq = ps[7][:, 0:n]
            for c in range(8):
                S.mm(pm, ONESA[:, :], XM[:, c, tok], start=(c == 0), stop=(c == 7))
            for c in range(8):
                S.mm(pq, ONESA[:, :], SQ[:, c, 0:n], start=(c == 0), stop=(c == 7))
            mean = MR[:, 0, 0:n]; rstd = MR[:, 1, 0:n]
            S.copy('act', mean, pm)
            S.act(rstd, pm, AF.Square)
            S.tt('dve', rstd, pq, rstd, ALU.subtract)
            S.act(rstd, rstd, AF.Ln, bias=EPS[:, 0:1], scale=1.0)
            S.act(rstd, rstd, AF.Exp, scale=-0.5)
            if merge:
                S.tt('dve', Xb, Xb, mean.unsqueeze(1).to_broadcast([128, 8, n]), ALU.subtract)
            for c in range(8):
                xc = X[:, c, tok]
                if not merge:
                    S.tt('dve', xc, xc, mean, ALU.subtract)
                S.stt('dve', xc, xc, CST[:, gcol + c:gcol + c + 1], rstd, ALU.mult, ALU.mult)
                if final:
                    if not merge:
                        S.act(xc, xc, AF.Identity, bias=CST[:, bcol + c:bcol + c + 1], scale=1.0)
                    continue
                l, i = divmod(k, 3)
                if not sample:
                    S.act(XM[:, c, tok], xc, AF.Identity, bias=BC[:, k, c:c + 1],
                          scale=ADA[:, l, (3 * i + 1) * 8 + c, 0:1])
                else:
                    S.act(XS[:, 0:n], xc, AF.Identity, bias=CST[:, bcol + c:bcol + c + 1], scale=1.0)
                    xs3 = XS[:, 0:n].rearrange("p (s t) -> p s t", t=8)
                    scb = ADA[:, l, (3 * i + 1) * 8 + c, 1:17].unsqueeze(2).to_broadcast([128, 16, 8])
                    shb = ADA[:, l, (3 * i) * 8 + c, 1:17].unsqueeze(2).to_broadcast([128, 16, 8])
                    S.tt('dve', xs3, xs3, scb, ALU.mult)
                    S.tt('dve', XM[:, c, tok].rearrange("p (s t) -> p s t", t=8), xs3, shb, ALU.add)
                if not merge:
                    S.ts('dve', xc, xc, ALPHA, AB[:, k, c:c + 1], ALU.mult, ALU.add)
            if merge:
                if final:
                    S.tt('dve', Xb, Xb, CST[:, bcol:bcol + 8].unsqueeze(2).to_broadcast([128, 8, n]), ALU.add)
                else:
                    S.stt('dve', Xb, Xb, ALPHA, AB[:, k, :].unsqueeze(2).to_broadcast([128, 8, n]), ALU.mult, ALU.add)

        def accumulate(pt, l, i, oc, t0, n):
            xc = X[:, oc, t0:t0 + n]
            j = (3 * i + 2) * 8 + oc
            if t0 < NPR:
                S.stt('dve', xc, pt, ADA[:, l, j, 0:1], xc, ALU.mult, ALU.add)
            else:
                gbc = ADA[:, l, j, 1:17].unsqueeze(2).to_broadcast([128, 16, 8])
                xs3 = XS[:, 0:n].rearrange("p (s t) -> p s t", t=8)
                S.tt('dve', xs3, pt.rearrange("p (s t) -> p s t", t=8), gbc, ALU.mult)
                S.tt('dve', xc, xc, XS[:, 0:n], ALU.add)

        slot_ctr = [0]
        gu_ctr = [0]
        dn_ctr = [0]

        def ffn(l, which):
            i = 0 if which == 0 else 2
            k_next = 3 * l + i + 1
            wgl = wg[which][l].rearrange("(k p) n -> p k n", p=128)
            wul = wu[which][l].rearrange("(k p) n -> p k n", p=128)
            wdl = wd[which][l].rearrange("(j p) n -> p j n", p=128)
            items = []
            slots = {}

            def load(gi):
                f0, G = GROUPS[gi]
                s = slot_ctr[0] % 2; slot_ctr[0] += 1
                base = s * SLOT
                Wg_ = W[:, base:base + 4096].rearrange("p (k n) -> p k n", n=512)
                Wu_ = W[:, base + 4096:base + 8192].rearrange("p (k n) -> p k n", n=512)
                Wd_ = W[:, base + 8192:base + 12288].rearrange("p (j n) -> p j n", n=1024)
                S.dma('pool', Wg_[:, :, 0:G * 128], wgl[:, :, f0 * 128:(f0 + G) * 128])
                S.dma('pool', Wu_[:, :, 0:G * 128], wul[:, :, f0 * 128:(f0 + G) * 128])
                S.dma('pool', Wd_[:, 0:G, :], wdl[:, f0:f0 + G, :])
                slots[gi] = (Wg_, Wu_, Wd_)

            def gu(gi, bi, par):
                f0, G = GROUPS[gi]
                t0, n = FFN_BLOCKS[bi]
                Wg_, Wu_, Wd_ = slots[gi]
                for j in range(G):
                    q = gu_ctr[0] % 2; gu_ctr[0] += 1
                    pg = ps[2 * q][:, 0:n]; pu = ps[2 * q + 1][:, 0:n]
                    for k in range(8):
                        S.mm(pg, Wg_[:, k, j * 128:(j + 1) * 128], XM[:, k, t0:t0 + n], start=(k == 0), stop=(k == 7))
                    for k in range(8):
                        S.mm(pu, Wu_[:, k, j * 128:(j + 1) * 128], XM[:, k, t0:t0 + n], start=(k == 0), stop=(k == 7))
                    S.act(SG[:, q, 0:n], pg, AF.Silu)
                    S.tt('dve', ACTB[:, par, j, 0:n], SG[:, q, 0:n], pu, ALU.mult)

            def down(gi, bi, par):
                f0, G = GROUPS[gi]
                t0, n = FFN_BLOCKS[bi]
                Wg_, Wu_, Wd_ = slots[gi]
                for oc in range(8):
                    pd = ps[4 + dn_ctr[0] % 4][:, 0:n]; dn_ctr[0] += 1
                    for j in range(G):
                        S.mm(pd, Wd_[:, j, oc * 128:(oc + 1) * 128], ACTB[:, par, j, 0:n], start=(j == 0), stop=(j == G - 1))
                    accumulate(pd, l, i, oc, t0, n)
                if gi == len(GROUPS) - 1:
                    for (m0, mn) in MIX_BLOCKS:
                        if t0 <= m0 < t0 + n:
                            ln_block(k_next, m0, mn)

            load(0)
            prev = None
            par = 0
            for gi in range(len(GROUPS)):
                if gi + 1 < len(GROUPS):
                    pass
                for bi in range(len(FFN_BLOCKS)):
                    if bi == 0 and gi + 1 < len(GROUPS):
                        pending_load = gi + 1
                    else:
                        pending_load = None
                    gu(gi, bi, par)
                    if prev is not None:
                        down(*prev)
                    if pending_load is not None:
                        load(pending_load)
                    prev = (gi, bi, par)
                    par ^= 1
                    if l == 0:
                        ada_pump(2 if which == 0 else 3)
            down(*prev)
            if l == 0 and which == 1:
                ada_pump(1000)

        hbp = [0]

        def next_hb():
            r = ps[hbp[0] % 6][:, 0:256]; hbp[0] += 1
            return r

        def mixer(l):
            k_next = 3 * l + 2
            S.dma('sp', MSK[:, 0, :], masks[0])
            S.dma('sp', MSK[:, 1, :], masks[1])
            S.dma('sp', STG[:, :].rearrange("p (h t) -> p h t", t=128), sgu_wT[l].rearrange("h s t -> s h t"))
            for h in range(4):
                S.tt('dve', WTP[:, h, :], STG[:, h * 128:(h + 1) * 128], MSK[:, 0, :], ALU.mult)
            for h in range(4):
                for sq in range(16):
                    src = sgu_wT[l, h, 0:8, 0:8].unsqueeze(1).to_broadcast([8, 16, 8])
                    S.dma('sp', STG[sq * 8:(sq + 1) * 8, h * 128:(h + 1) * 128].rearrange("p (a b) -> p a b", b=8), src)
            for h in range(4):
                S.tt('dve', WTS[:, h, :], STG[:, h * 128:(h + 1) * 128], MSK[:, 1, :], ALU.mult)
            for c in range(2):
                for hh in range(2):
                    S.dma('sp', BTP[hh * 64:(hh + 1) * 64, c, :], sgu_b[l, 2 * c + hh:2 * c + hh + 1, :].to_broadcast([64, 128]))
            S.dma('sp', LNG[:, 0, :], sgu_ln[l, 0:1, :].to_broadcast([128, 256]))
            S.dma('sp', LNG[:, 1, :], sgu_ln[l, 1:2, :].to_broadcast([128, 256]))
            S.memset('dve', STG[:, 0:256], 0.0)
            for g in range(4):
                c, hh = divmod(g, 2)
                S.dma('sp', STG[hh * 64:(hh + 1) * 64, c * 128 + hh * 64:c * 128 + hh * 64 + 64], pool_w[l, g])
            S.copy('dve', PWB[:, :, :].rearrange("p c n -> p (c n)"), STG[:, 0:256])
            for c in range(2):
                for kk in range(31):
                    S.act(DG[:, c * 31 + kk, :], CST[:, COFF['idn32']:COFF['idn32'] + 32], AF.Identity,
                          scale=cc('dconv_w', (l * 2 + c) * 31 + kk), bias=0.0)
            wi = w_in[l].rearrange("(k p) n -> p k n", p=128)
            wo = w_out[l].rearrange("(k p) n -> p k n", p=128)
            Wp = [W[:, q * PIECE:(q + 1) * PIECE].rearrange("p (k n) -> p k n", n=512) for q in range(6)]
            for q in range(4):
                S.dma('pool', Wp[q], wi[:, :, q * 512:(q + 1) * 512])
            for q in range(2):
                S.dma('pool', Wp[4 + q], wo[:, :, q * 512:(q + 1) * 512])

            def wcol(oc):
                q, r = divmod(oc, 4)
                return lambda k: Wp[q][:, k, r * 128:(r + 1) * 128]

            def proj(oc, t0, n):
                pt = next_hb()[:, 0:n]
                wc = wcol(oc)
                for k in range(8):
                    S.mm(pt, wc(k), XM[:, k, t0:t0 + n], start=(k == 0), stop=(k == 7))
                return pt

            def front(bi, t0, n):
                sample = t0 >= NPR
                NS, T = (16, 8) if sample else (1, n)
                v3 = lambda ap: ap.rearrange("p (s t) -> p s t", t=T)

                def ext(buf, c, P):
                    L = P + T
                    return buf[:, c, 0:NS * L].rearrange("p (s t) -> p s t", t=L)
                if sample:
                    for c in range(2):
                        S.dma('pool', ext(ED, c, 30)[:, :, 0:30], st_dconv[l, :, c])
                        S.dma('sp', o_dconv_s[l, :, c, :, 0:22], st_dconv[l, :, c, :, 8:30])
                elif bi == 0:
                    for c in range(2):
                        S.memset('pool', ED[:, c, 0:30], 0.0)
                for c in range(2):
                    E = ext(ED, c, 30)
                    pt = proj(14 + c, t0, n)
                    S.act(H[:, c, 0:n], pt, AF.Sigmoid)
                    pt = proj(12 + c, t0, n)
                    S.tt('dve', E[:, :, 30:30 + T], v3(H[:, c, 0:n]), v3(pt), ALU.mult)
                    if sample:
                        S.tt('dve', XS[:, 0:n], H[:, c, 0:n], pt, ALU.mult)
                        S.dma('sp', o_dconv_s[l, :, c, :, 22:30], XS[:, 0:n].rearrange("p (s t) -> p s t", t=8))
                    elif bi == 7:
                        S.tt('dve', XS[:, 64:94], H[:, c, n - 30:n], pt[:, n - 30:n], ALU.mult)
                        S.dma('sp', o_dconv_p[l, :, c], XS[:, 64:94])
                for c in range(2):
                    E = ext(ED, c, 30)
                    pc = next_hb()[:, 0:n]
                    for kk in range(31):
                        for g in range(4):
                            S.mm(v3(pc[32 * g:32 * g + 32, :]), DG[32 * g:32 * g + 32, c * 31 + kk, :],
                                 E[32 * g:32 * g + 32, :, kk:kk + T], start=(kk == 0), stop=(kk == 30),
                                 tile_position=(32 * g, 32 * g))
                    S.act(H[:, c, 0:n], pc, AF.Identity, bias=cc('dconv_b', l * 2 + c), scale=1.0)
                BE = 'dve' if 'bdve' in FL else 'pool'
                if sample:
                    for c in range(2):
                        S.dma('sp', ext(EB, c, 15)[:, :, 0:15], st_pool[l, :, c])
                elif bi == 0:
                    for c in range(2):
                        S.memset(BE, EB[:, c, 0:15], 0.0)
                for c in range(2):
                    pt = proj(4 + c, t0, n)
                    S.copy('act', ext(EB, c, 15)[:, :, 15:15 + T], v3(pt))
                L = 15 + T
                for c in range(2):
                    E = ext(EB, c, 15)
                    A = SA[:, 0:NS * L].rearrange("p (s t) -> p s t", t=L)
                    Bv = SBB[:, 0:NS * L].rearrange("p (s t) -> p s t", t=L)
                    xb_new = E[:, :, 15:L]
                    pbv = v3(PB[:, c, 0:n])
                    S.tt(BE, A[:, :, 1:L], E[:, :, 1:L], E[:, :, 0:L - 1], ALU.add)
                    if c == 0:
                        S.tt(BE, Bv[64:128, :, 3:L], A[64:128, :, 3:L], A[64:128, :, 1:L - 2], ALU.add)
                        S.stt('dve', pbv[0:64], A[0:64, :, 15:L], 0.5, xb_new[0:64], ALU.mult, ALU.subtract)
                        S.stt('dve', pbv[64:128], Bv[64:128, :, 15:L], 0.25, xb_new[64:128], ALU.mult, ALU.subtract)
                    else:
                        S.tt(BE, Bv[:, :, 3:L], A[:, :, 3:L], A[:, :, 1:L - 2], ALU.add)
                        S.tt(BE, A[:, :, 7:L], Bv[:, :, 7:L], Bv[:, :, 3:L - 4], ALU.add)
                        S.tt(BE, Bv[64:128, :, 15:L], A[64:128, :, 15:L], A[64:128, :, 7:L - 8], ALU.add)
                        S.stt('dve', pbv[0:64], A[0:64, :, 15:L], 0.125, xb_new[0:64], ALU.mult, ALU.subtract)
                        S.stt('dve', pbv[64:128], Bv[64:128, :, 15:L], 0.0625, xb_new[64:128], ALU.mult, ALU.subtract)
                    if (not sample) and bi == 0:
                        ic = CST[:, COFF['invcnt'] + c * 16:COFF['invcnt'] + (c + 1) * 16]
                        tq = XS[:, 64:80]
                        for (p0, p1, src) in ((0, 64, A), (64, 128, Bv)):
                            S.tt(BE, tq[p0:p1], src[p0:p1, 0, 15:31], ic[p0:p1], ALU.mult)
                            S.tt(BE, PB[p0:p1, c, 0:16], tq[p0:p1], E[p0:p1, 0, 15:31], ALU.subtract)
                    pt = next_hb()[:, 0:n]
                    S.mm(pt, PWB[:, c, :], PB[:, c, 0:n], start=True, stop=True)
                    S.act(Y[:, 2 + c, 0:n], pt, AF.Identity, scale=cc('pool_scale', l * 2 + c), bias=0.0)
                if sample:
                    for c in range(2):
                        S.dma('sp', ext(EZ, c, 2)[:, :, 0:2], st_sconv[l, :, c])
                elif bi == 0:
                    for c in range(2):
                        S.memset('pool', EZ[:, c, 0:2], 0.0)
                for c in range(2):
                    E = ext(EZ, c, 2)
                    pt = proj(8 + c, t0, n)
                    S.copy('act', E[:, :, 2:2 + T], v3(pt))
                    pt = proj(10 + c, t0, n)
                    S.tt('dve', E[:, :, 2:2 + T], E[:, :, 2:2 + T], v3(pt), ALU.mult)
                    acc = v3(Y[:, 4 + c, 0:n])
                    wcol_ = lambda kk: cc('sconv_w', (l * 2 + c) * 3 + kk)
                    S.ts('dve', acc, E[:, :, 2:2 + T], wcol_(2), None, ALU.mult)
                    S.stt('dve', acc, E[:, :, 1:1 + T], wcol_(1), acc, ALU.mult, ALU.add)
                    S.stt('dve', acc, E[:, :, 0:T], wcol_(0), acc, ALU.mult, ALU.add)
                    pt = proj(6 + c, t0, n)
                    S.tt('dve', Y[:, 4 + c, 0:n], Y[:, 4 + c, 0:n], pt, ALU.mult)
                for c in range(2):
                    pt = proj(c, t0, n)
                    S.act(U[:, c, 0:n], pt, AF.Gelu)
                for ti in range(n // 128):
                    tt0 = t0 + ti * 128
                    if ti == 0:
                        VG, VB, BST = VG0, VB0, BST0
                    else:
                        VG = Y[:, 7, :]
                        VB = Y[:, 6, 0:128].bitcast(BF16)
                        BST = Y[:, 6, 128:144]
                    pv = next_hb()
                    for k in range(8):
                        S.mm(pv, XM[:, k, tt0:tt0 + 128], Wp[0][:, k, 256:512], start=(k == 0), stop=(k == 7))
                    S.act(VG[:, :], pv, AF.Gelu)
                    S.generic('dve', lambda e, b_=BST, v_=VG: e.bn_stats(b_[:, 0:6], v_[:, :]), [VG[:, :]], [BST[:, 0:6]])
                    S.generic('dve', lambda e, b_=BST: e.bn_aggr(b_[:, 8:10], b_[:, 0:6]), [BST[:, 0:6]], [BST[:, 8:10]])
                    S.act(BST[:, 9:10], BST[:, 9:10], AF.Sqrt, bias=EPS[:, 0:1], scale=1.0)
                    S.recip(BST[:, 9:10], BST[:, 9:10])
                    S.stt('dve', BST[:, 10:11], BST[:, 8:9], -1.0, BST[:, 9:10], ALU.mult, ALU.mult)
                    S.act(VG[:, :], VG[:, :], AF.Identity, bias=BST[:, 10:11], scale=BST[:, 9:10])
                    S.tt('dve' if 'lngdve' in FL else 'pool', VG[:, :], VG[:, :], LNG[:, 0, :], ALU.mult)
                    S.tt('dve', VG[:, :], VG[:, :], LNG[:, 1, :], ALU.add)
                    S.copy('act', VB[:, :], VG[:, :])
                    if tt0 == NPR - 128:
                        S.dma('sp', o_sgu_p[l], VG[:, :])
                    if sample:
                        S.dma('sp', o_sgu_s[l], VG[:, :])
                    pm = next_hb()
                    WT = WTS if sample else WTP
                    for c in range(2):
                        for hh in range(2):
                            h = 2 * c + hh
                            S.mm(pm[hh * 64:(hh + 1) * 64, c * 128:(c + 1) * 128], VB[:, h * 64:(h + 1) * 64], WT[:, h, :],
                                 start=True, stop=True)
                    tmp = Y[:, 0:2, ti * 128:(ti + 1) * 128]
                    if not sample:
                        S.tt('dve', tmp, pm.rearrange("p (c t) -> p c t", t=128), BTP[:, :, :], ALU.add)
                    else:
                        for c in range(2):
                            S.tt('dve', tmp[:, c, :].rearrange("p (s t) -> p s t", t=8),
                                 pm[:, c * 128:(c + 1) * 128].rearrange("p (s t) -> p s t", t=8),
                                 BTP[:, c, 0:8].unsqueeze(1).to_broadcast([128, 16, 8]), ALU.add)
                    S.tt('dve', Y[:, 0:2, ti * 128:(ti + 1) * 128], tmp, U[:, 0:2, ti * 128:(ti + 1) * 128], ALU.mult)
                for c in range(2):
                    S.copy('act', HB[:, 0, c, 0:n], H[:, c, 0:n])
                    S.act(HB[:, 1, c, 0:n], H[:, c, 0:n], AF.Square)
                pm_ = next_hb()[:, 0:n]
                pq_ = next_hb()[:, 0:n]
                for c in range(2):
                    S.mm(pm_, ONESB[:, :], HB[:, 0, c, 0:n], start=(c == 0), stop=(c == 1))
                for c in range(2):
                    S.mm(pq_, ONESB[:, :], HB[:, 1, c, 0:n], start=(c == 0), stop=(c == 1))
                mean = MR[:, 0, 0:n]; rstd = MR[:, 1, 0:n]
                S.copy('act', mean, pm_)
                S.act(rstd, pm_, AF.Square)
                S.tt('dve', rstd, pq_, rstd, ALU.subtract)
                S.act(rstd, rstd, AF.Ln, bias=EPS[:, 0:1], scale=1.0)
                S.act(rstd, rstd, AF.Exp, scale=-0.5)
                for c in range(2):
                    hc_ = H[:, c, 0:n]
                    S.tt('dve', hc_, hc_, mean, ALU.subtract)
                    S.tt('dve', hc_, hc_, rstd, ALU.mult)
                    S.act(Y[:, 6 + c, 0:n], hc_, AF.Silu, bias=cc('conv_ln_b', l * 2 + c), scale=cc('conv_ln_g', l * 2 + c))
                for c in range(2):
                    if sample:
                        S.dma('sp', o_pool_s[l, :, c], ext(EB, c, 15)[:, :, T:T + 15])
                        S.dma('sp', o_sconv_s[l, :, c], ext(EZ, c, 2)[:, :, T:T + 2])
                    elif bi == 7:
                        S.dma('sp', o_pool_p[l, :, c], EB[:, c, T:T + 15])
                        S.dma('sp', o_sconv_p[l, :, c], EZ[:, c, T:T + 2])
                    else:
                        S.copy('pool' if 'carrypool' in FL else 'act', XS[:, 0:15], EB[:, c, T:T + 15])
                        S.copy('pool' if 'carrypool' in FL else 'act', EB[:, c, 0:15], XS[:, 0:15])
                        S.copy('pool' if 'carrypool' in FL else 'act', XS[:, 16:18], EZ[:, c, T:T + 2])
                        S.copy('pool' if 'carrypool' in FL else 'act', EZ[:, c, 0:2], XS[:, 16:18])
                        S.copy('pool' if 'carrypool' in FL else 'act', XS[:, 32:62], ED[:, c, T:T + 30])
                        S.copy('pool' if 'carrypool' in FL else 'act', ED[:, c, 0:30], XS[:, 32:62])

            def rms(bi, t0, n):
                for c in range(8):
                    if 'rmsdve' in FL and c % 2 == 1:
                        S.tt('dve', SQ[:, c, 0:n], Y[:, c, 0:n], Y[:, c, 0:n], ALU.mult)
                    else:
                        S.act(SQ[:, c, 0:n], Y[:, c, 0:n], AF.Square)
                for g in range(4):
                    pr = next_hb()[:, 0:n]
                    for c in range(2):
                        S.mm(pr, ONESB[:, :], SQ[:, 2 * g + c, 0:n], start=(c == 0), stop=(c == 1))
                    rg = MR[:, g % 2, 0:n]
                    S.act(rg, pr, AF.Ln, bias=EPS[:, 1:2], scale=1.0)
                    S.act(rg, rg, AF.Exp, scale=-0.5)
                    for c in (2 * g, 2 * g + 1):
                        S.stt('dve', XM[:, c, t0:t0 + n], Y[:, c, 0:n], cc('out_norm_g', l * 8 + c), rg,
                              ALU.mult, ALU.mult)

            def back(bi, t0, n):
                for oc in range(8):
                    pt = next_hb()[:, 0:n]
                    q, r = divmod(oc, 4)
                    for k in range(8):
                        S.mm(pt, Wp[4 + q][:, k, r * 128:(r + 1) * 128], XM[:, k, t0:t0 + n], start=(k == 0), stop=(k == 7))
                    accumulate(pt, l, 1, oc, t0, n)
                ln_block(k_next, t0, n)

            nb = len(MIX_BLOCKS)
            for bi, (t0, n) in enumerate(MIX_BLOCKS):
                front(bi, t0, n)
                if bi > 0:
                    back(bi - 1, *MIX_BLOCKS[bi - 1])
                rms(bi, t0, n)
            back(nb - 1, *MIX_BLOCKS[nb - 1])

        def finish():
            for (t0_, n_) in MIX_BLOCKS:
                for c in range(8):
                    S.dma('sp', yT[c][:, t0_:t0_ + n_], X[:, c, t0_:t0_ + n_])
            if stop is not None:
                S.dma('sp', o_ada, ADA[:, :, :, :].rearrange("p l j s -> p (l j s)"))
            S.emit(es)

        stages = []
        if stop in ('setup', 'ada'):
            finish()
            return nc
        import os
        _sel = os.environ.get('LNBLK')
        _blks = MIX_BLOCKS if _sel is None else [MIX_BLOCKS[int(q)] for q in _sel.split(',')]
        stages.append(('ln0', lambda: [ln_block(0, t0, n) for (t0, n) in _blks]))
        for l in range(nlayers):
            stages.append(('ffn1_%d' % l, lambda l=l: ffn(l, 0)))
            stages.append(('mixer_%d' % l, lambda l=l: mixer(l)))
            stages.append(('ffn2_%d' % l, lambda l=l: ffn(l, 1)))
        for name, fn in stages:
            fn()
            if stop == name:
                break
        finish()
    return nc

_PROG_CACHE = {}


def _fm(a, rows):
    return np.ascontiguousarray(np.asarray(a, np.float32).reshape(rows, 128).T)


def _const_tables():
    inv = np.zeros((128, 2, 16), np.float32)
    wins = (2, 4, 8, 16)
    for c in range(2):
        for p in range(128):
            w = wins[2 * c + p // 64]
            for pos in range(16):
                inv[p, c, pos] = 1.0 / min(w, pos + 1)
    m0 = np.triu(np.ones((128, 128), np.float32))
    m1 = np.kron(np.eye(16, dtype=np.float32), np.triu(np.ones((8, 8), np.float32)))
    return inv.reshape(128, 32), np.stack([m0, m1]).astype(np.float32)


def _make_in_maps(x_prompt, x_sample, state_pool, state_sconv, state_dconv, c_prompt, c_sample,
           ln_in_g, ln_in_b, w_ada, b_ada, ffn1_w_gate, ffn1_w_up, ffn1_w_down, w_in,
           sgu_ln_g, sgu_ln_b, sgu_w, sgu_b, pool_w, pool_scale, sconv_w, dconv_w, dconv_b,
           conv_ln_g, conv_ln_b, out_norm_g, w_out, ffn2_w_gate, ffn2_w_up, ffn2_w_down,
           post_ln_g, post_ln_b):
    f32 = lambda a: np.ascontiguousarray(np.asarray(a, dtype=np.float32))
    x_prompt = f32(x_prompt); x_sample = f32(x_sample)
    inv, masks = _const_tables()
    consts = np.zeros((128, NCONST), np.float32)

    def put(name, arr):
        consts[:, COFF[name]:COFF[name] + arr.shape[1]] = arr
    put('ln_in_g', _fm(ln_in_g, 8)); put('ln_in_b', _fm(ln_in_b, 8))
    put('post_g', _fm(post_ln_g, 48)); put('post_b', _fm(post_ln_b, 48))
    put('out_norm_g', _fm(out_norm_g, 16)); put('pool_scale', _fm(pool_scale, 4))
    put('sconv_w', f32(sconv_w).reshape(2, 3, 2, 128).transpose(3, 0, 2, 1).reshape(128, 12))
    put('dconv_w', f32(dconv_w).reshape(2, 31, 2, 128).transpose(3, 0, 2, 1).reshape(128, 124))
    put('dconv_b', _fm(dconv_b, 4)); put('conv_ln_g', _fm(conv_ln_g, 4)); put('conv_ln_b', _fm(conv_ln_b, 4))
    put('b_ada', _fm(b_ada, 144))
    put('invcnt', inv)
    put('idn32', np.tile(np.eye(32, dtype=np.float32), (4, 1)))

    shared = {
        "consts": consts,
        "w_ada": f32(w_ada),
        "ffn1_w_gate": f32(ffn1_w_gate), "ffn1_w_up": f32(ffn1_w_up), "ffn1_w_down": f32(ffn1_w_down),
        "ffn2_w_gate": f32(ffn2_w_gate), "ffn2_w_up": f32(ffn2_w_up), "ffn2_w_down": f32(ffn2_w_down),
        "w_in": f32(w_in), "w_out": f32(w_out),
        "sgu_wT": f32(np.asarray(sgu_w).transpose(0, 1, 3, 2)),
        "sgu_b": f32(sgu_b),
        "sgu_ln": f32(np.stack([np.asarray(sgu_ln_g), np.asarray(sgu_ln_b)], axis=1)),
        "pool_w": f32(pool_w),
        "masks": masks,
    }
    state_pool = np.asarray(state_pool); state_sconv = np.asarray(state_sconv); state_dconv = np.asarray(state_dconv)
    c_prompt = np.asarray(c_prompt); c_sample = np.asarray(c_sample)

    def st_fm(st, i):
        a = st[:, 16 * i:16 * i + 16]
        L_, _, R, _ = a.shape
        a = a.reshape(L_, 16, R, 2, 128).transpose(0, 4, 3, 1, 2)
        return f32(a)

    in_maps = []
    for i in range(8):
        xall = np.concatenate([x_prompt[i], x_sample[16 * i:16 * i + 16].reshape(128, 1024)], axis=0)
        call = np.concatenate([c_prompt[i:i + 1], c_sample[16 * i:16 * i + 16]], axis=0)
        m = dict(shared)
        m["xT"] = f32(xall.T.reshape(8, 128, NT))
        m["cT"] = f32(call.reshape(17, 8, 128).transpose(2, 1, 0))
        m["st_pool"] = st_fm(state_pool, i)
        m["st_sconv"] = st_fm(state_sconv, i)
        m["st_dconv"] = st_fm(state_dconv, i)
        in_maps.append(m)

    return in_maps


def _gather(R):
    y_prompt = np.zeros((8, 2048, 1024), np.float32)
    y_sample = np.zeros((128, 8, 1024), np.float32)
    sgu_p = np.zeros((DEPTH, 8, 128, 256), np.float32)
    sgu_s = np.zeros((DEPTH, 128, 8, 256), np.float32)
    pool_p = np.zeros((DEPTH, 8, 15, 256), np.float32)
    pool_s = np.zeros((DEPTH, 128, 15, 256), np.float32)
    sconv_p = np.zeros((DEPTH, 8, 2, 256), np.float32)
    sconv_s = np.zeros((DEPTH, 128, 2, 256), np.float32)
    dconv_p = np.zeros((DEPTH, 8, 30, 256), np.float32)
    dconv_s = np.zeros((DEPTH, 128, 30, 256), np.float32)
    for i in range(8):
        r = R[i]
        yt = np.asarray(r["yT"]).reshape(1024, NT)
        y_prompt[i] = yt[:, :2048].T
        y_sample[16 * i:16 * i + 16] = yt[:, 2048:].T.reshape(16, 8, 1024)
        sgu_p[:, i] = np.asarray(r["o_sgu_p"])
        sgu_s[:, 16 * i:16 * i + 16] = np.asarray(r["o_sgu_s"]).reshape(DEPTH, 16, 8, 256)
        for (op_, os_, dp, ds, rows) in (("o_pool_p", "o_pool_s", pool_p, pool_s, 15),
                                         ("o_sconv_p", "o_sconv_s", sconv_p, sconv_s, 2),
                                         ("o_dconv_p", "o_dconv_s", dconv_p, dconv_s, 30)):
            a = np.asarray(r[op_])
            dp[:, i] = a.transpose(0, 3, 2, 1).reshape(DEPTH, rows, 256)
            b = np.asarray(r[os_])
            ds[:, 16 * i:16 * i + 16] = b.transpose(0, 3, 4, 2, 1).reshape(DEPTH, 16, rows, 256)
    return (y_prompt, y_sample, sgu_p, sgu_s, pool_p, pool_s, sconv_p, sconv_s, dconv_p, dconv_s)


def kernel(x_prompt, x_sample, state_pool, state_sconv, state_dconv, c_prompt, c_sample,
           ln_in_g, ln_in_b, w_ada, b_ada, ffn1_w_gate, ffn1_w_up, ffn1_w_down, w_in,
           sgu_ln_g, sgu_ln_b, sgu_w, sgu_b, pool_w, pool_scale, sconv_w, dconv_w, dconv_b,
           conv_ln_g, conv_ln_b, out_norm_g, w_out, ffn2_w_gate, ffn2_w_up, ffn2_w_down,
           post_ln_g, post_ln_b):
    in_maps = _make_in_maps(x_prompt, x_sample, state_pool, state_sconv, state_dconv, c_prompt, c_sample,
           ln_in_g, ln_in_b, w_ada, b_ada, ffn1_w_gate, ffn1_w_up, ffn1_w_down, w_in,
           sgu_ln_g, sgu_ln_b, sgu_w, sgu_b, pool_w, pool_scale, sconv_w, dconv_w, dconv_b,
           conv_ln_g, conv_ln_b, out_norm_g, w_out, ffn2_w_gate, ffn2_w_up, ffn2_w_down,
           post_ln_g, post_ln_b)
    if "nc" not in _PROG_CACHE:
        _PROG_CACHE["nc"] = build_program()
    nc = _PROG_CACHE["nc"]
    res = run_bass_kernel_spmd(nc, in_maps, core_ids=list(range(8)))
    return _gather(res.results)
```

```python
from concourse.bass_utils import run_bass_kernel_spmd
import os
import heapq
import numpy as np
import concourse.bass as bass
import concourse.mybir as mybir

F32 = mybir.dt.float32
BF16 = mybir.dt.bfloat16
ALU = mybir.AluOpType
AF = mybir.ActivationFunctionType

CENG = ['pe', 'act', 'dve', 'pool']
DEF_LAT = '0.6'; DEF_LOOK = '20'; DEF_WINDOW = '600'; DEF_SLAT = '0.3'; DEF_TPEN = '1.3'; DEF_DMABW = '150e3'
ALLENG = ['pe', 'act', 'dve', 'pool', 'sp']
_DSZ = {F32: 4, BF16: 2}


def _region(ap):
    name = ap.tensor.name
    steps = ap.ap
    esz = _DSZ.get(ap.dtype, 4)
    sp = str(ap.space)
    if 'DRAM' in sp.upper() or 'HBM' in sp.upper():
        lo = ap.offset
        hi = lo + sum((c - 1) * abs(s) for s, c in steps) + 1
        return (name, 0, 1, lo * esz, hi * esz, None)
    if 'PSUM' in sp.upper():
        return (name, 0, 128, 0, 1 << 20, None)
    pstep, pcnt = steps[0]
    if pstep == 0:
        pstep = 1 << 40
    p0 = ap.offset // pstep if pstep < (1 << 40) else 0
    f0 = ap.offset - p0 * pstep if pstep < (1 << 40) else ap.offset
    free = steps[1:]
    ext = sum((c - 1) * abs(s) for s, c in free) + 1
    rows = None
    if len(free) >= 2:
        s0, c0 = free[0]
        ein = sum((c - 1) * abs(s) for s, c in free[1:]) + 1
        if 1 < c0 <= 16 and s0 > ein:
            rows = tuple(((f0 + r * s0) * esz, (f0 + r * s0 + ein) * esz) for r in range(c0))
    return (name, p0, p0 + pcnt, f0 * esz, (f0 + ext) * esz, rows)


_TSET = {AF.Gelu: 'gelu', AF.Tanh: 'gelu', AF.Silu: 'silu', AF.Sigmoid: 'sigm', AF.Sqrt: 'sqrt', AF.Ln: 'lnexp', AF.Exp: 'lnexp'}


def _nfree(ap):
    n = 1
    for s, c in ap.ap[1:]:
        n *= c
    return n


def _ovl(a, b):
    if not (a[1] < b[2] and b[1] < a[2] and a[3] < b[4] and b[3] < a[4]):
        return False
    ra, rb = a[5], b[5]
    if ra is None and rb is None:
        return True
    ia = ra if ra is not None else ((a[3], a[4]),)
    ib = rb if rb is not None else ((b[3], b[4]),)
    for (x0, x1) in ia:
        for (y0, y1) in ib:
            if x0 < y1 and y0 < x1:
                return True
    return False


def _covers(a, b):
    if not (a[1] <= b[1] and a[2] >= b[2] and a[3] <= b[3] and a[4] >= b[4]):
        return False
    ra, rb = a[5], b[5]
    if ra is None:
        return True
    ib = rb if rb is not None else ((b[3], b[4]),)
    for (y0, y1) in ib:
        if not any(x0 <= y0 and y1 <= x1 for (x0, x1) in ra):
            return False
    return True


class Op:
    __slots__ = ('eng', 'fn', 'pos', 'is_dma', 'dsem', 'dval', 'waits', 'sig', 'vc', 'prewait', 'tag',
                 'idx', 'deps', 'cost', 'xfer', 'nsucc', 'succ', 'start', 'finish', 'raw', 'tset')


class Sched:
    def __init__(self, nc, kdma=8):
        self.nc = nc
        self.ops = []
        self.wr = {}
        self.rd = {}
        self.kdma = kdma

    def _add(self, eng, fn, ins, outs, is_dma=False, cost=0.3, xfer=0.0):
        op = Op()
        op.eng = eng; op.fn = fn; op.is_dma = is_dma; op.sig = False; op.waits = []; op.prewait = None
        op.idx = len(self.ops); op.cost = cost; op.xfer = xfer; op.tset = None
        deps = {}
        raw = set()
        for ap in ins:
            r = _region(ap)
            for (wr_, o) in self.wr.get(r[0], ()):
                if _ovl(r, wr_):
                    deps[o.idx] = o; raw.add(o.idx)
        for ap in outs:
            r = _region(ap)
            for (wr_, o) in self.wr.get(r[0], ()):
                if _ovl(r, wr_):
                    deps[o.idx] = o
            for (rr, o) in self.rd.get(r[0], ()):
                if _ovl(r, rr):
                    deps[o.idx] = o
        for ap in ins:
            r = _region(ap)
            lst = self.rd.setdefault(r[0], [])
            if not is_dma:
                keep = []
                for (rr, o) in lst:
                    if o.eng == eng and (not o.is_dma) and _covers(r, rr):
                        deps[o.idx] = o
                    else:
                        keep.append((rr, o))
                lst[:] = keep
            lst.append((r, op))
        for ap in outs:
            r = _region(ap)
            wl = self.wr.setdefault(r[0], [])
            wl[:] = [(wr_, o) for (wr_, o) in wl if not _covers(r, wr_)]
            wl.append((r, op))
            rl = self.rd.get(r[0])
            if rl:
                rl[:] = [(rr, o) for (rr, o) in rl if not _covers(r, rr)]
        deps.pop(op.idx, None)
        op.deps = list(deps.values())
        op.raw = raw
        self.ops.append(op)
        return op

    def mm(self, out, lhsT, rhs, start=True, stop=True, **kw):
        n = _nfree(rhs)
        cost = max(n, 64) / 2400.0 + 0.012
        if rhs.dtype == F32:
            cost *= 4.0
        if 'tile_position' in kw:
            cost *= 0.27
        return self._add('pe', lambda e: e.matmul(out, lhsT, rhs, start=start, stop=stop, **kw),
                         [lhsT, rhs] + ([] if start else [out]), [out], cost=cost)

    def act(self, out, in_, func, bias=None, scale=None):
        ins = [in_]
        kw = {}
        if bias is not None:
            kw['bias'] = bias
            if not isinstance(bias, (int, float)):
                ins.append(bias)
        if scale is not None:
            kw['scale'] = scale
            if not isinstance(scale, (int, float)):
                ins.append(scale)
        o = self._add('act', lambda e: e.activation(out, in_, func, **kw), ins, [out],
                      cost=0.2 + _nfree(out) / 1200.0)
        o.tset = _TSET.get(func)
        return o

    def _vcost(self, eng, out, mult=1.0):
        n = _nfree(out)
        if eng == 'pool':
            return 0.3 + n * 2.2 / 1200.0
        return 0.12 + mult * n / 960.0

    def tt(self, eng, out, in0, in1, op):
        return self._add(eng, lambda e: e.tensor_tensor(out, in0, in1, op), [in0, in1], [out], cost=self._vcost(eng, out))

    def ts(self, eng, out, in0, s1, s2, op0, op1=None):
        ins = [in0] + [s for s in (s1, s2) if s is not None and not isinstance(s, (int, float))]
        if op1 is None:
            return self._add(eng, lambda e: e.tensor_single_scalar(out, in0, s1, op0), ins, [out], cost=self._vcost(eng, out))
        return self._add(eng, lambda e: e.tensor_scalar(out, in0, s1, s2, op0, op1), ins, [out], cost=self._vcost(eng, out))

    def stt(self, eng, out, in0, scalar, in1, op0, op1):
        ins = [in0, in1] + ([] if isinstance(scalar, (int, float)) else [scalar])
        return self._add(eng, lambda e: e.scalar_tensor_tensor(out, in0, scalar, in1, op0, op1), ins, [out],
                         cost=self._vcost(eng, out))

    def copy(self, eng, out, in_):
        if eng == 'act':
            return self._add('act', lambda e: e.activation(out, in_, AF.Copy), [in_], [out], cost=0.2 + _nfree(out) / 1200.0)
        return self._add(eng, lambda e: e.tensor_copy(out, in_), [in_], [out], cost=self._vcost(eng, out))

    def memset(self, eng, out, val):
        return self._add(eng, lambda e: e.memset(out, val), [], [out], cost=self._vcost(eng, out))

    def recip(self, out, in_):
        return self._add('dve', lambda e: e.reciprocal(out, in_), [in_], [out], cost=self._vcost('dve', out, 4.0))

    def generic(self, eng, fn, ins, outs, cost=0.4):
        return self._add(eng, fn, ins, outs, cost=cost)

    def dma(self, eng, out, in_, **kw):
        nbytes = 1
        for d_ in out.shape:
            nbytes *= d_
        nbytes *= max(_DSZ.get(out.dtype, 4), _DSZ.get(in_.dtype, 4))
        return self._add(eng, lambda e, sem, val: e.dma_start(out=out, in_=in_, **kw).then_inc(sem, 16),
                         [in_], [out], is_dma=True, cost=(1.2 if eng == 'pool' else 0.1), xfer=2.0 + nbytes / float(os.environ.get('SCHED_DMABW', DEF_DMABW)))

    def _schedule(self):
        ops = self.ops
        LAT = float(os.environ.get('SCHED_LAT', DEF_LAT))
        LOOK = int(os.environ.get('SCHED_LOOK', DEF_LOOK))
        SLAT = float(os.environ.get('SCHED_SLAT', DEF_SLAT))
        TPEN = float(os.environ.get('SCHED_TPEN', DEF_TPEN))
        if os.environ.get('NOSCHED'):
            t = 0.0
            for op in ops:
                op.start = t; op.finish = t + 1e-3; t += 1e-3
            return list(ops)
        for op in ops:
            op.nsucc = len(op.deps); op.succ = []
        for op in ops:
            for d in op.deps:
                d.succ.append(op)
        cand = {e: [] for e in ALLENG}
        rtime = {}
        for op in ops:
            if op.nsucc == 0:
                heapq.heappush(cand[op.eng], (op.idx, op)); rtime[op.idx] = 0.0
        free = {e: 0.0 for e in ALLENG}
        dma_free = 0.0
        cur_tset = [None]
        order = []
        nleft = len(ops)
        WINDOW = int(os.environ.get('SCHED_WINDOW', DEF_WINDOW))
        while nleft:
            best = None
            for e in ALLENG:
                h = cand[e]
                if not h:
                    continue
                tfree = free[e]
                pick = None; pick_t = None
                look = heapq.nsmallest(LOOK + 4 if e == 'act' else LOOK, h)
                lo_idx = look[0][0]
                for (idx, o) in look:
                    if idx - lo_idx > WINDOW:
                        break
                    rt = rtime[idx]
                    st = rt if rt > tfree else tfree
                    if e == 'act' and o.tset is not None and o.tset != cur_tset[0]:
                        st += TPEN
                    if pick is None or st < pick_t - 1e-9:
                        pick = o; pick_t = st
                    if e != 'act' and rt <= tfree:
                        break
                if best is None or pick_t < best[0] - 1e-9 or (abs(pick_t - best[0]) <= 1e-9 and pick.idx < best[1].idx):
                    best = (pick_t, pick, e)
            st, op, e = best
            h = cand[e]
            h.remove((op.idx, op)); heapq.heapify(h)
            op.start = st
            if op.is_dma:
                free[e] = st + op.cost
                xs = max(st + op.cost, dma_free)
                dma_free = xs + (op.xfer - 2.0) * 0.5
                op.finish = xs + op.xfer
            else:
                if e == 'act' and op.tset is not None:
                    cur_tset[0] = op.tset
                op.finish = st + op.cost
                free[e] = op.finish
            order.append(op)
            nleft -= 1
            for s in op.succ:
                s.nsucc -= 1
                lat = (0.0 if op.eng == 'pe' else SLAT) if (s.eng == op.eng and not op.is_dma) else LAT
                rt = op.finish + lat
                if rtime.get(s.idx, 0.0) < rt:
                    rtime[s.idx] = rt
                if s.nsucc == 0:
                    heapq.heappush(cand[s.eng], (s.idx, s))
        order.sort(key=lambda o: (o.start, o.idx))
        self.sim_end = max(o.finish for o in order)
        return order

    def _waits(self, order):
        self.streams = {e: [] for e in ALLENG}
        self.cops = {e: [] for e in CENG}
        cpos = {e: 0 for e in CENG}
        vcs = {e: {c: 0 for c in CENG} for e in ALLENG}
        dknown = {e: {} for e in ALLENG}
        ndma = {e: 0 for e in ALLENG}
        for op in order:
            eng = op.eng
            vc = vcs[eng]; dk = dknown[eng]
            best = {}; dmab = {}
            for d in op.deps:
                if d.is_dma:
                    if d.dsem not in dmab or d.dval > dmab[d.dsem].dval:
                        dmab[d.dsem] = d
                else:
                    a = d.eng
                    if a == eng and eng == 'pe':
                        continue
                    if a not in best or d.pos > best[a].pos:
                        best[a] = d
            for key, d in dmab.items():
                if dk.get(key, 0) >= d.dval:
                    continue
                op.waits.append(('dma', key, d.dval))
                dk[key] = d.dval
                for c in CENG:
                    if d.vc[c] > vc[c]:
                        vc[c] = d.vc[c]
            for a, d in best.items():
                if vc[a] >= d.pos:
                    continue
                op.waits.append(('eng', a, d.pos)); d.sig = True
                for c in CENG:
                    if d.vc[c] > vc[c]:
                        vc[c] = d.vc[c]
                if d.pos > vc[a]:
                    vc[a] = d.pos
            if op.is_dma:
                i = ndma[eng]; ndma[eng] += 1
                op.dsem = (eng, i % self.kdma)
                op.dval = 16 * (i // self.kdma + 1)
                if i >= self.kdma:
                    op.prewait = (op.dsem, op.dval - 16)
                    if dk.get(op.dsem, 0) < op.dval - 16:
                        dk[op.dsem] = op.dval - 16
                op.pos = None
                op.vc = dict(vc)
            else:
                cpos[eng] += 1
                op.pos = cpos[eng]
                self.cops[eng].append(op)
                op.vc = dict(vc)
                op.vc[eng] = op.pos
            self.streams[eng].append(op)
        self.ndma = ndma

    def emit(self, es):
        nc = self.nc
        order = self._schedule()
        self._waits(order)
        csem = {e: es.enter_context(nc.semaphore('s_' + e)) for e in CENG}
        dsem = {}
        for e in ALLENG:
            for k in range(min(self.kdma, self.ndma[e])):
                dsem[(e, k)] = es.enter_context(nc.semaphore('d_%s%d' % (e, k)))
        count = {}
        for e in CENG:
            n = 0
            for op in self.cops[e]:
                if op.sig:
                    n += 1
                count[(e, op.pos)] = n
        self.count = count
        streams = self.streams
        kd = self.kdma
        nd = self.ndma

        def run(engname, eng):
            for op in streams[engname]:
                if op.prewait is not None:
                    eng.wait_ge(dsem[op.prewait[0]], op.prewait[1])
                for w in op.waits:
                    if w[0] == 'dma':
                        eng.wait_ge(dsem[w[1]], w[2])
                    else:
                        eng.wait_ge(csem[w[1]], count[(w[1], w[2])])
                if op.is_dma:
                    op.fn(eng, dsem[op.dsem], op.dval)
                else:
                    ins = op.fn(eng)
                    if op.sig:
                        ins.then_inc(csem[engname], 1)
            n = nd[engname]
            for k in range(min(kd, n)):
                cnt = (n - k + kd - 1) // kd
                eng.wait_ge(dsem[(engname, k)], 16 * cnt)

        block = es.enter_context(nc.Block())

        @block.tensor
        def _(e):
            run('pe', e)

        @block.scalar
        def _(e):
            run('act', e)

        @block.vector
        def _(e):
            run('dve', e)

        @block.gpsimd
        def _(e):
            run('pool', e)

        @block.sync
        def _(e):
            run('sp', e)

import contextlib

D = 1024; KC = 8; FF = 2816; NFC = 22; DEPTH = 2
NT = 2176; NPR = 2048; NSM = 128
ALPHA = (2 * DEPTH) ** 0.25
GROUPS = [(0, 4), (4, 4), (8, 2), (10, 4), (14, 4), (18, 4)]
FFN_BLOCKS = [(0, 512), (512, 512), (1024, 512), (1536, 512), (2048, 128)]
MIX_BLOCKS = [(i * 256, 256) for i in range(8)] + [(2048, 128)]
SLOT = 12288
PIECE = 4096

DEFAULT_FLAGS = 'castdve,rescdve,subdve,lngdve,bdve'

def _const_layout():
    off = {}
    n = 0
    def add(name, k):
        nonlocal n
        off[name] = n; n += k
    add('ln_in_g', 8); add('ln_in_b', 8)
    add('post_g', 48); add('post_b', 48)
    add('out_norm_g', 16); add('pool_scale', 4)
    add('sconv_w', 12); add('dconv_w', 124); add('dconv_b', 4)
    add('conv_ln_g', 4); add('conv_ln_b', 4)
    add('b_ada', 144)
    add('invcnt', 32)
    add('idn32', 32)
    return off, n
COFF, NCONST = _const_layout()


def build_program(nlayers=DEPTH, stop=None):
    import os as _os2
    FL = set(_os2.environ.get('KFLAGS', DEFAULT_FLAGS).split(','))
    nc = bass.Bass("TRN2", target_bir_lowering=False)
    dt_in = lambda name, shape: nc.dram_tensor(name, shape, F32, kind="ExternalInput").ap()
    dt_out = lambda name, shape: nc.dram_tensor(name, shape, F32, kind="ExternalOutput").ap()
    xT = dt_in("xT", [8, 128, NT])
    cT = dt_in("cT", [128, 8, 17])
    consts = dt_in("consts", [128, NCONST])
    w_ada = dt_in("w_ada", [DEPTH, D, 9 * D])
    wg = [dt_in("ffn1_w_gate", [DEPTH, D, FF]), dt_in("ffn2_w_gate", [DEPTH, D, FF])]
    wu = [dt_in("ffn1_w_up", [DEPTH, D, FF]), dt_in("ffn2_w_up", [DEPTH, D, FF])]
    wd = [dt_in("ffn1_w_down", [DEPTH, FF, D]), dt_in("ffn2_w_down", [DEPTH, FF, D])]
    w_in = dt_in("w_in", [DEPTH, D, 2048])
    w_out = dt_in("w_out", [DEPTH, D, D])
    sgu_wT = dt_in("sgu_wT", [DEPTH, 4, 128, 128])
    sgu_b = dt_in("sgu_b", [DEPTH, 4, 128])
    sgu_ln = dt_in("sgu_ln", [DEPTH, 2, 256])
    pool_w = dt_in("pool_w", [DEPTH, 4, 64, 64])
    masks = dt_in("masks", [2, 128, 128])
    st_pool = dt_in("st_pool", [DEPTH, 128, 2, 16, 15])
    st_sconv = dt_in("st_sconv", [DEPTH, 128, 2, 16, 2])
    st_dconv = dt_in("st_dconv", [DEPTH, 128, 2, 16, 30])

    yT = dt_out("yT", [8, 128, NT])
    o_sgu_p = dt_out("o_sgu_p", [DEPTH, 128, 256])
    o_sgu_s = dt_out("o_sgu_s", [DEPTH, 128, 256])
    o_pool_p = dt_out("o_pool_p", [DEPTH, 128, 2, 15])
    o_pool_s = dt_out("o_pool_s", [DEPTH, 128, 2, 16, 15])
    o_sconv_p = dt_out("o_sconv_p", [DEPTH, 128, 2, 2])
    o_sconv_s = dt_out("o_sconv_s", [DEPTH, 128, 2, 16, 2])
    o_dconv_p = dt_out("o_dconv_p", [DEPTH, 128, 2, 30])
    o_dconv_s = dt_out("o_dconv_s", [DEPTH, 128, 2, 16, 30])

    o_ada = dt_out("o_ada", [128, DEPTH * 72 * 17]) if stop is not None else None
    es = contextlib.ExitStack()
    with es:
        S = Sched(nc)
        sb = lambda name, shape, dt=F32: es.enter_context(nc.sbuf_tensor(name, shape, dt))
        X = sb("X", [128, 8, NT])
        XM = sb("XM", [128, 8, NT], BF16)
        W = sb("W", [128, 2 * SLOT], BF16)
        SQ = sb("SQ", [128, 8, 256], BF16)
        MR = sb("MR", [128, 2, 256])
        ADA = sb("ADA", [128, DEPTH, 72, 17])
        CST = sb("CST", [128, NCONST])
        BC = sb("BC", [128, DEPTH * 3 + 1, 8])
        AB = sb("AB", [128, DEPTH * 3 + 1, 8])
        EPS = sb("EPS", [128, 2])
        CSB = sb("CSB", [128, 8, 17], BF16)
        ONESA = sb("ONESA", [128, 128], BF16)
        ONESB = sb("ONESB", [128, 128], BF16)
        WTP = sb("WTP", [128, 4, 128], BF16)
        WTS = sb("WTS", [128, 4, 128], BF16)
        BTP = sb("BTP", [128, 2, 128])
        LNG = sb("LNG", [128, 2, 256])
        PWB = sb("PWB", [128, 2, 128], BF16)
        XS = sb("XS", [128, 128])
        DG = sb("DG", [128, 62, 32], BF16)
        NSCR = 6352
        SCR = sb("SCR", [128, NSCR])

        def carve(w0, shape, dt=F32):
            nel = 1
            for d_ in shape:
                nel *= d_
            nw = nel if dt == F32 else nel // 2
            a = SCR[:, w0:w0 + nw]
            if dt != F32:
                a = a.bitcast(dt)
            if len(shape) == 1:
                return a
            if len(shape) == 2:
                return a.rearrange("p (a b) -> p a b", b=shape[1])
            return a.rearrange("p (a b c) -> p a b c", b=shape[1], c=shape[2])
        Y = carve(0, [8, 256]); U = carve(2048, [2, 256]); H = carve(2560, [2, 256])
        VG0 = carve(3072, [256]); VB0 = carve(3328, [256], BF16); PB = carve(3456, [2, 256], BF16)
        SA = carve(3712, [368]); SBB = carve(4080, [368]); EB = carve(4448, [2, 368])
        EZ = carve(5184, [2, 272]); ED = carve(5728, [2, 608], BF16); BST0 = carve(6336, [16])
        SG = carve(0, [2, 512], BF16); ACTB = carve(512, [2, 4, 512], BF16)
        STG = carve(0, [512]); MSK = carve(512, [2, 128])
        HB = SQ[:, 0:4, :].rearrange("p (a c) n -> p a c n", c=2)
        ps = [es.enter_context(nc.psum_tensor("ps%d" % i, [128, 512], F32)) for i in range(8)]
        hb = [ps[i // 2][:, (i % 2) * 256:(i % 2) * 256 + 256] for i in range(16)]

        cc = lambda name, j: CST[:, COFF[name] + j:COFF[name] + j + 1]

        for c in range(8):
            S.dma('sp', X[:, c, :], xT[c])
        S.dma('sp', CST[:, :], consts)
        S.dma('sp', STG[:, 0:136], cT.rearrange("p k s -> p (k s)"))
        S.memset('dve', EPS[:, 0:1], 1e-5)
        S.memset('dve', EPS[:, 1:2], 1e-6)
        S.memset('dve', ONESA[:, :], 1.0 / 1024)
        S.memset('dve', ONESB[:, :], 1.0 / 256)
        S.act(CSB[:, :, :].rearrange("p k s -> p (k s)"), STG[:, 0:136], AF.Silu)

        WA = sb("WA", [128, 2, 8, 128], BF16)
        pcnt = [0]

        def ln_cols(k):
            if k == 0:
                return COFF['ln_in_g'], COFF['ln_in_b']
            return COFF['post_g'] + (k - 1) * 8, COFF['post_b'] + (k - 1) * 8
        nln = 3 * nlayers

        def ada_post(l, i):
            sl = ADA[:, l, (3 * i + 1) * 8:(3 * i + 2) * 8, :]
            S.ts('dve', sl, sl, 1.0, None, ALU.add)
            gl = ADA[:, l, (3 * i + 2) * 8:(3 * i + 3) * 8, :]
            if i == 1:
                S.ts('dve', gl, gl, 1.0, None, ALU.add)
            else:
                S.ts('dve', gl, gl, 1.0, 0.5, ALU.add, ALU.mult)
            k = 3 * l + i
            gcol, bcol = ln_cols(k)
            S.tt('dve', BC[:, k, :], CST[:, bcol:bcol + 8], ADA[:, l, (3 * i + 1) * 8:(3 * i + 2) * 8, 0], ALU.mult)
            S.tt('dve', BC[:, k, :], BC[:, k, :], ADA[:, l, (3 * i) * 8:(3 * i + 1) * 8, 0], ALU.add)

        def ada_chunk(l, j, Wt, jj, banks):
            pt = ps[banks[pcnt[0] % len(banks)]][:, 0:17]; pcnt[0] += 1
            for k in range(8):
                S.mm(pt, Wt[:, k, jj * 128:(jj + 1) * 128], CSB[:, k, :], start=(k == 0), stop=(k == 7))
            S.act(ADA[:, l, j, :], pt, AF.Identity, bias=cc('b_ada', l * 72 + j))

        if stop != 'setup':
            wv0 = w_ada[0].rearrange("(k p) n -> p k n", p=128)
            for n in range(6):
                reg = (n % 6) * PIECE
                Wt = W[:, reg:reg + PIECE].rearrange("p (k n) -> p k n", n=512)
                S.dma('pool', Wt, wv0[:, :, n * 512:(n + 1) * 512])
                for jj in range(4):
                    ada_chunk(0, n * 4 + jj, Wt, jj, list(range(8)))
            for k in range(nln + 1):
                gcol, bcol = ln_cols(k)
                S.ts('dve', AB[:, k, :], CST[:, bcol:bcol + 8], ALPHA, None, ALU.mult)
            ada_post(0, 0)

        ada_todo = [(0, j) for j in range(24, 72)]
        for l in range(1, nlayers):
            for j in range(72):
                ada_todo.append((l, j))
        ada_state = {'i': 0}

        def ada_pump(nmax):
            for _ in range(nmax):
                if ada_state['i'] >= len(ada_todo):
                    return
                l, j = ada_todo[ada_state['i']]
                slot = ada_state['i'] % 2
                ada_state['i'] += 1
                wvl = w_ada[l].rearrange("(k p) n -> p k n", p=128)
                S.dma('pool', WA[:, slot, :, :], wvl[:, :, j * 128:(j + 1) * 128])
                ada_chunk(l, j, WA[:, slot, :, :], 0, [4, 5, 6, 7])
                if j % 24 == 23:
                    ada_post(l, j // 24)

        lnp = [0]

        import os as _os
        _lnstep = int(_os.environ.get('LNSTEP', '9'))

        def ln_block(k, t0, n):
            gcol, bcol = ln_cols(k)
            final = (k == nln)
            sample = (t0 >= NPR)
            tok = slice(t0, t0 + n)
            merge = 'lnmerge' in FL
            Xb = X[:, :, tok]
            if merge:
                S.act(SQ[:, :, 0:n], Xb, AF.Square)
                S.copy('dve', XM[:, :, tok], Xb)
            else:
                for c in range(8):
                    S.act(SQ[:, c, 0:n], X[:, c, tok], AF.Square)
                    S.copy('dve', XM[:, c, tok], X[:, c, tok])
            pm = ps[6][:, 0:n]
            pq = ps[7][:, 0:n]
            for c in range(8):
                S.mm(pm, ONESA[:, :], XM[:, c, tok], start=(c == 0), stop=(c == 7))
            for c in range(8):
                S.mm(pq, ONESA[:, :], SQ[:, c, 0:n], start=(c == 0), stop=(c == 7))
            mean = MR[:, 0, 0:n]; rstd = MR[:, 1, 0:n]
            S.copy('act', mean, pm)
            S.act(rstd, pm, AF.Square)
            S.tt('dve', rstd, pq, rstd, ALU.subtract)
            S.act(rstd, rstd, AF.Ln, bias=EPS[:, 0:1], scale=1.0)
            S.act(rstd, rstd, AF.Exp, scale=-0.5)
            if merge:
                S.tt('dve', Xb, Xb, mean.unsqueeze(1).to_broadcast([128, 8, n]), ALU.subtract)
            for c in range(8):
                xc = X[:, c, tok]
                if not merge:
                    S.tt('dve', xc, xc, mean, ALU.subtract)
                S.stt('dve', xc, xc, CST[:, gcol + c:gcol + c + 1], rstd, ALU.mult, ALU.mult)
                if final:
                    if not merge:
                        S.act(xc, xc, AF.Identity, bias=CST[:, bcol + c:bcol + c + 1], scale=1.0)
                    continue
                l, i = divmod(k, 3)
                if not sample:
                    S.act(XM[:, c, tok], xc, AF.Identity, bias=BC[:, k, c:c + 1],
                          scale=ADA[:, l, (3 * i + 1) * 8 + c, 0:1])
                else:
                    S.act(XS[:, 0:n], xc, AF.Identity, bias=CST[:, bcol + c:bcol + c + 1], scale=1.0)
                    xs3 = XS[:, 0:n].rearrange("p (s t) -> p s t", t=8)
                    scb = ADA[:, l, (3 * i + 1) * 8 + c, 1:17].unsqueeze(2).to_broadcast([128, 16, 8])
                    shb = ADA[:, l, (3 * i) * 8 + c, 1:17].unsqueeze(2).to_broadcast([128, 16, 8])
                    S.tt('dve', xs3, xs3, scb, ALU.mult)
                    S.tt('dve', XM[:, c, tok].rearrange("p (s t) -> p s t", t=8), xs3, shb, ALU.add)
                if not merge:
                    S.ts('dve', xc, xc, ALPHA, AB[:, k, c:c + 1], ALU.mult, ALU.add)
            if merge:
                if final:
                    S.tt('dve', Xb, Xb, CST[:, bcol:bcol + 8].unsqueeze(2).to_broadcast([128, 8, n]), ALU.add)
                else:
                    S.stt('dve', Xb, Xb, ALPHA, AB[:, k, :].unsqueeze(2).to_broadcast([128, 8, n]), ALU.mult, ALU.add)

        def accumulate(pt, l, i, oc, t0, n):
            xc = X[:, oc, t0:t0 + n]
            j = (3 * i + 2) * 8 + oc
            if t0 < NPR:
                S.stt('dve', xc, pt, ADA[:, l, j, 0:1], xc, ALU.mult, ALU.add)
            else:
                gbc = ADA[:, l, j, 1:17].unsqueeze(2).to_broadcast([128, 16, 8])
                xs3 = XS[:, 0:n].rearrange("p (s t) -> p s t", t=8)
                S.tt('dve', xs3, pt.rearrange("p (s t) -> p s t", t=8), gbc, ALU.mult)
                S.tt('dve', xc, xc, XS[:, 0:n], ALU.add)

        slot_ctr = [0]
        gu_ctr = [0]
        dn_ctr = [0]

        def ffn(l, which):
            i = 0 if which == 0 else 2
            k_next = 3 * l + i + 1
            wgl = wg[which][l].rearrange("(k p) n -> p k n", p=128)
            wul = wu[which][l].rearrange("(k p) n -> p k n", p=128)
            wdl = wd[which][l].rearrange("(j p) n -> p j n", p=128)
            items = []
            slots = {}

            def load(gi):
                f0, G = GROUPS[gi]
                s = slot_ctr[0] % 2; slot_ctr[0] += 1
                base = s * SLOT
                Wg_ = W[:, base:base + 4096].rearrange("p (k n) -> p k n", n=512)
                Wu_ = W[:, base + 4096:base + 8192].rearrange("p (k n) -> p k n", n=512)
                Wd_ = W[:, base + 8192:base + 12288].rearrange("p (j n) -> p j n", n=1024)
                S.dma('pool', Wg_[:, :, 0:G * 128], wgl[:, :, f0 * 128:(f0 + G) * 128])
                S.dma('pool', Wu_[:, :, 0:G * 128], wul[:, :, f0 * 128:(f0 + G) * 128])
                S.dma('pool', Wd_[:, 0:G, :], wdl[:, f0:f0 + G, :])
                slots[gi] = (Wg_, Wu_, Wd_)

            def gu(gi, bi, par):
                f0, G = GROUPS[gi]
                t0, n = FFN_BLOCKS[bi]
                Wg_, Wu_, Wd_ = slots[gi]
                for j in range(G):
                    q = gu_ctr[0] % 2; gu_ctr[0] += 1
                    pg = ps[2 * q][:, 0:n]; pu = ps[2 * q + 1][:, 0:n]
                    for k in range(8):
                        S.mm(pg, Wg_[:, k, j * 128:(j + 1) * 128], XM[:, k, t0:t0 + n], start=(k == 0), stop=(k == 7))
                    for k in range(8):
                        S.mm(pu, Wu_[:, k, j * 128:(j + 1) * 128], XM[:, k, t0:t0 + n], start=(k == 0), stop=(k == 7))
                    S.act(SG[:, q, 0:n], pg, AF.Silu)
                    S.tt('dve', ACTB[:, par, j, 0:n], SG[:, q, 0:n], pu, ALU.mult)

            def down(gi, bi, par):
                f0, G = GROUPS[gi]
                t0, n = FFN_BLOCKS[bi]
                Wg_, Wu_, Wd_ = slots[gi]
                for oc in range(8):
                    pd = ps[4 + dn_ctr[0] % 4][:, 0:n]; dn_ctr[0] += 1
                    for j in range(G):
                        S.mm(pd, Wd_[:, j, oc * 128:(oc + 1) * 128], ACTB[:, par, j, 0:n], start=(j == 0), stop=(j == G - 1))
                    accumulate(pd, l, i, oc, t0, n)
                if gi == len(GROUPS) - 1:
                    for (m0, mn) in MIX_BLOCKS:
                        if t0 <= m0 < t0 + n:
                            ln_block(k_next, m0, mn)

            load(0)
            prev = None
            par = 0
            for gi in range(len(GROUPS)):
                if gi + 1 < len(GROUPS):
                    pass
                for bi in range(len(FFN_BLOCKS)):
                    if bi == 0 and gi + 1 < len(GROUPS):
                        pending_load = gi + 1
                    else:
                        pending_load = None
                    gu(gi, bi, par)
                    if prev is not None:
                        down(*prev)
                    if pending_load is not None:
                        load(pending_load)
                    prev = (gi, bi, par)
                    par ^= 1
                    if l == 0:
                        ada_pump(2 if which == 0 else 3)
            down(*prev)
            if l == 0 and which == 1:
                ada_pump(1000)

        hbp = [0]

        def next_hb():
            r = ps[hbp[0] % 6][:, 0:256]; hbp[0] += 1
            return r

        def mixer(l):
            k_next = 3 * l + 2
            S.dma('sp', MSK[:, 0, :], masks[0])
            S.dma('sp', MSK[:, 1, :], masks[1])
            S.dma('sp', STG[:, :].rearrange("p (h t) -> p h t", t=128), sgu_wT[l].rearrange("h s t -> s h t"))
            for h in range(4):
                S.tt('dve', WTP[:, h, :], STG[:, h * 128:(h + 1) * 128], MSK[:, 0, :], ALU.mult)
            for h in range(4):
                for sq in range(16):
                    src = sgu_wT[l, h, 0:8, 0:8].unsqueeze(1).to_broadcast([8, 16, 8])
                    S.dma('sp', STG[sq * 8:(sq + 1) * 8, h * 128:(h + 1) * 128].rearrange("p (a b) -> p a b", b=8), src)
            for h in range(4):
                S.tt('dve', WTS[:, h, :], STG[:, h * 128:(h + 1) * 128], MSK[:, 1, :], ALU.mult)
            for c in range(2):
                for hh in range(2):
                    S.dma('sp', BTP[hh * 64:(hh + 1) * 64, c, :], sgu_b[l, 2 * c + hh:2 * c + hh + 1, :].to_broadcast([64, 128]))
            S.dma('sp', LNG[:, 0, :], sgu_ln[l, 0:1, :].to_broadcast([128, 256]))
            S.dma('sp', LNG[:, 1, :], sgu_ln[l, 1:2, :].to_broadcast([128, 256]))
            S.memset('dve', STG[:, 0:256], 0.0)
            for g in range(4):
                c, hh = divmod(g, 2)
                S.dma('sp', STG[hh * 64:(hh + 1) * 64, c * 128 + hh * 64:c * 128 + hh * 64 + 64], pool_w[l, g])
            S.copy('dve', PWB[:, :, :].rearrange("p c n -> p (c n)"), STG[:, 0:256])
            for c in range(2):
                for kk in range(31):
                    S.act(DG[:, c * 31 + kk, :], CST[:, COFF['idn32']:COFF['idn32'] + 32], AF.Identity,
                          scale=cc('dconv_w', (l * 2 + c) * 31 + kk), bias=0.0)
            wi = w_in[l].rearrange("(k p) n -> p k n", p=128)
            wo = w_out[l].rearrange("(k p) n -> p k n", p=128)
            Wp = [W[:, q * PIECE:(q + 1) * PIECE].rearrange("p (k n) -> p k n", n=512) for q in range(6)]
            for q in range(4):
                S.dma('pool', Wp[q], wi[:, :, q * 512:(q + 1) * 512])
            for q in range(2):
                S.dma('pool', Wp[4 + q], wo[:, :, q * 512:(q + 1) * 512])

            def wcol(oc):
                q, r = divmod(oc, 4)
                return lambda k: Wp[q][:, k, r * 128:(r + 1) * 128]

            def proj(oc, t0, n):
                pt = next_hb()[:, 0:n]
                wc = wcol(oc)
                for k in range(8):
                    S.mm(pt, wc(k), XM[:, k, t0:t0 + n], start=(k == 0), stop=(k == 7))
                return pt

            def front(bi, t0, n):
                sample = t0 >= NPR
                NS, T = (16, 8) if sample else (1, n)
                v3 = lambda ap: ap.rearrange("p (s t) -> p s t", t=T)

                def ext(buf, c, P):
                    L = P + T
                    return buf[:, c, 0:NS * L].rearrange("p (s t) -> p s t", t=L)
                if sample:
                    for c in range(2):
                        S.dma('pool', ext(ED, c, 30)[:, :, 0:30], st_dconv[l, :, c])
                        S.dma('sp', o_dconv_s[l, :, c, :, 0:22], st_dconv[l, :, c, :, 8:30])
                elif bi == 0:
                    for c in range(2):
                        S.memset('pool', ED[:, c, 0:30], 0.0)
                for c in range(2):
                    E = ext(ED, c, 30)
                    pt = proj(14 + c, t0, n)
                    S.act(H[:, c, 0:n], pt, AF.Sigmoid)
                    pt = proj(12 + c, t0, n)
                    S.tt('dve', E[:, :, 30:30 + T], v3(H[:, c, 0:n]), v3(pt), ALU.mult)
                    if sample:
                        S.tt('dve', XS[:, 0:n], H[:, c, 0:n], pt, ALU.mult)
                        S.dma('sp', o_dconv_s[l, :, c, :, 22:30], XS[:, 0:n].rearrange("p (s t) -> p s t", t=8))
                    elif bi == 7:
                        S.tt('dve', XS[:, 64:94], H[:, c, n - 30:n], pt[:, n - 30:n], ALU.mult)
                        S.dma('sp', o_dconv_p[l, :, c], XS[:, 64:94])
                for c in range(2):
                    E = ext(ED, c, 30)
                    pc = next_hb()[:, 0:n]
                    for kk in range(31):
                        for g in range(4):
                            S.mm(v3(pc[32 * g:32 * g + 32, :]), DG[32 * g:32 * g + 32, c * 31 + kk, :],
                                 E[32 * g:32 * g + 32, :, kk:kk + T], start=(kk == 0), stop=(kk == 30),
                                 tile_position=(32 * g, 32 * g))
                    S.act(H[:, c, 0:n], pc, AF.Identity, bias=cc('dconv_b', l * 2 + c), scale=1.0)
                BE = 'dve' if 'bdve' in FL else 'pool'
                if sample:
                    for c in range(2):
                        S.dma('sp', ext(EB, c, 15)[:, :, 0:15], st_pool[l, :, c])
                elif bi == 0:
                    for c in range(2):
                        S.memset(BE, EB[:, c, 0:15], 0.0)
                for c in range(2):
                    pt = proj(4 + c, t0, n)
                    S.copy('act', ext(EB, c, 15)[:, :, 15:15 + T], v3(pt))
                L = 15 + T
                for c in range(2):
                    E = ext(EB, c, 15)
                    A = SA[:, 0:NS * L].rearrange("p (s t) -> p s t", t=L)
                    Bv = SBB[:, 0:NS * L].rearrange("p (s t) -> p s t", t=L)
                    xb_new = E[:, :, 15:L]
                    pbv = v3(PB[:, c, 0:n])
                    S.tt(BE, A[:, :, 1:L], E[:, :, 1:L], E[:, :, 0:L - 1], ALU.add)
                    if c == 0:
                        S.tt(BE, Bv[64:128, :, 3:L], A[64:128, :, 3:L], A[64:128, :, 1:L - 2], ALU.add)
                        S.stt('dve', pbv[0:64], A[0:64, :, 15:L], 0.5, xb_new[0:64], ALU.mult, ALU.subtract)
                        S.stt('dve', pbv[64:128], Bv[64:128, :, 15:L], 0.25, xb_new[64:128], ALU.mult, ALU.subtract)
                    else:
                        S.tt(BE, Bv[:, :, 3:L], A[:, :, 3:L], A[:, :, 1:L - 2], ALU.add)
                        S.tt(BE, A[:, :, 7:L], Bv[:, :, 7:L], Bv[:, :, 3:L - 4], ALU.add)
                        S.tt(BE, Bv[64:128, :, 15:L], A[64:128, :, 15:L], A[64:128, :, 7:L - 8], ALU.add)
                        S.stt('dve', pbv[0:64], A[0:64, :, 15:L], 0.125, xb_new[0:64], ALU.mult, ALU.subtract)
                        S.stt('dve', pbv[64:128], Bv[64:128, :, 15:L], 0.0625, xb_new[64:128], ALU.mult, ALU.subtract)
                    if (not sample) and bi == 0:
                        ic = CST[:, COFF['invcnt'] + c * 16:COFF['invcnt'] + (c + 1) * 16]
                        tq = XS[:, 64:80]
                        for (p0, p1, src) in ((0, 64, A), (64, 128, Bv)):
                            S.tt(BE, tq[p0:p1], src[p0:p1, 0, 15:31], ic[p0:p1], ALU.mult)
                            S.tt(BE, PB[p0:p1, c, 0:16], tq[p0:p1], E[p0:p1, 0, 15:31], ALU.subtract)
                    pt = next_hb()[:, 0:n]
                    S.mm(pt, PWB[:, c, :], PB[:, c, 0:n], start=True, stop=True)
                    S.act(Y[:, 2 + c, 0:n], pt, AF.Identity, scale=cc('pool_scale', l * 2 + c), bias=0.0)
                if sample:
                    for c in range(2):
                        S.dma('sp', ext(EZ, c, 2)[:, :, 0:2], st_sconv[l, :, c])
                elif bi == 0:
                    for c in range(2):
                        S.memset('pool', EZ[:, c, 0:2], 0.0)
                for c in range(2):
                    E = ext(EZ, c, 2)
                    pt = proj(8 + c, t0, n)
                    S.copy('act', E[:, :, 2:2 + T], v3(pt))
                    pt = proj(10 + c, t0, n)
                    S.tt('dve', E[:, :, 2:2 + T], E[:, :, 2:2 + T], v3(pt), ALU.mult)
                    acc = v3(Y[:, 4 + c, 0:n])
                    wcol_ = lambda kk: cc('sconv_w', (l * 2 + c) * 3 + kk)
                    S.ts('dve', acc, E[:, :, 2:2 + T], wcol_(2), None, ALU.mult)
                    S.stt('dve', acc, E[:, :, 1:1 + T], wcol_(1), acc, ALU.mult, ALU.add)
                    S.stt('dve', acc, E[:, :, 0:T], wcol_(0), acc, ALU.mult, ALU.add)
                    pt = proj(6 + c, t0, n)
                    S.tt('dve', Y[:, 4 + c, 0:n], Y[:, 4 + c, 0:n], pt, ALU.mult)
                for c in range(2):
                    pt = proj(c, t0, n)
                    S.act(U[:, c, 0:n], pt, AF.Gelu)
                for ti in range(n // 128):
                    tt0 = t0 + ti * 128
                    if ti == 0:
                        VG, VB, BST = VG0, VB0, BST0
                    else:
                        VG = Y[:, 7, :]
                        VB = Y[:, 6, 0:128].bitcast(BF16)
                        BST = Y[:, 6, 128:144]
                    pv = next_hb()
                    for k in range(8):
                        S.mm(pv, XM[:, k, tt0:tt0 + 128], Wp[0][:, k, 256:512], start=(k == 0), stop=(k == 7))
                    S.act(VG[:, :], pv, AF.Gelu)
                    S.generic('dve', lambda e, b_=BST, v_=VG: e.bn_stats(b_[:, 0:6], v_[:, :]), [VG[:, :]], [BST[:, 0:6]])
                    S.generic('dve', lambda e, b_=BST: e.bn_aggr(b_[:, 8:10], b_[:, 0:6]), [BST[:, 0:6]], [BST[:, 8:10]])
                    S.act(BST[:, 9:10], BST[:, 9:10], AF.Sqrt, bias=EPS[:, 0:1], scale=1.0)
                    S.recip(BST[:, 9:10], BST[:, 9:10])
                    S.stt('dve', BST[:, 10:11], BST[:, 8:9], -1.0, BST[:, 9:10], ALU.mult, ALU.mult)
                    S.act(VG[:, :], VG[:, :], AF.Identity, bias=BST[:, 10:11], scale=BST[:, 9:10])
                    S.tt('dve' if 'lngdve' in FL else 'pool', VG[:, :], VG[:, :], LNG[:, 0, :], ALU.mult)
                    S.tt('dve', VG[:, :], VG[:, :], LNG[:, 1, :], ALU.add)
                    S.copy('act', VB[:, :], VG[:, :])
                    if tt0 == NPR - 128:
                        S.dma('sp', o_sgu_p[l], VG[:, :])
                    if sample:
                        S.dma('sp', o_sgu_s[l], VG[:, :])
                    pm = next_hb()
                    WT = WTS if sample else WTP
                    for c in range(2):
                        for hh in range(2):
                            h = 2 * c + hh
                            S.mm(pm[hh * 64:(hh + 1) * 64, c * 128:(c + 1) * 128], VB[:, h * 64:(h + 1) * 64], WT[:, h, :],
                                 start=True, stop=True)
                    tmp = Y[:, 0:2, ti * 128:(ti + 1) * 128]
                    if not sample:
                        S.tt('dve', tmp, pm.rearrange("p (c t) -> p c t", t=128), BTP[:, :, :], ALU.add)
                    else:
                        for c in range(2):
                            S.tt('dve', tmp[:, c, :].rearrange("p (s t) -> p s t", t=8),
                                 pm[:, c * 128:(c + 1) * 128].rearrange("p (s t) -> p s t", t=8),
                                 BTP[:, c, 0:8].unsqueeze(1).to_broadcast([128, 16, 8]), ALU.add)
                    S.tt('dve', Y[:, 0:2, ti * 128:(ti + 1) * 128], tmp, U[:, 0:2, ti * 128:(ti + 1) * 128], ALU.mult)
                for c in range(2):
                    S.copy('act', HB[:, 0, c, 0:n], H[:, c, 0:n])
                    S.act(HB[:, 1, c, 0:n], H[:, c, 0:n], AF.Square)
                pm_ = next_hb()[:, 0:n]
                pq_ = next_hb()[:, 0:n]
                for c in range(2):
                    S.mm(pm_, ONESB[:, :], HB[:, 0, c, 0:n], start=(c == 0), stop=(c == 1))
                for c in range(2):
                    S.mm(pq_, ONESB[:, :], HB[:, 1, c, 0:n], start=(c == 0), stop=(c == 1))
                mean = MR[:, 0, 0:n]; rstd = MR[:, 1, 0:n]
                S.copy('act', mean, pm_)
                S.act(rstd, pm_, AF.Square)
                S.tt('dve', rstd, pq_, rstd, ALU.subtract)
                S.act(rstd, rstd, AF.Ln, bias=EPS[:, 0:1], scale=1.0)
                S.act(rstd, rstd, AF.Exp, scale=-0.5)
                for c in range(2):
                    hc_ = H[:, c, 0:n]
                    S.tt('dve', hc_, hc_, mean, ALU.subtract)
                    S.tt('dve', hc_, hc_, rstd, ALU.mult)
                    S.act(Y[:, 6 + c, 0:n], hc_, AF.Silu, bias=cc('conv_ln_b', l * 2 + c), scale=cc('conv_ln_g', l * 2 + c))
                for c in range(2):
                    if sample:
                        S.dma('sp', o_pool_s[l, :, c], ext(EB, c, 15)[:, :, T:T + 15])
                        S.dma('sp', o_sconv_s[l, :, c], ext(EZ, c, 2)[:, :, T:T + 2])
                    elif bi == 7:
                        S.dma('sp', o_pool_p[l, :, c], EB[:, c, T:T + 15])
                        S.dma('sp', o_sconv_p[l, :, c], EZ[:, c, T:T + 2])
                    else:
                        S.copy('pool' if 'carrypool' in FL else 'act', XS[:, 0:15], EB[:, c, T:T + 15])
                        S.copy('pool' if 'carrypool' in FL else 'act', EB[:, c, 0:15], XS[:, 0:15])
                        S.copy('pool' if 'carrypool' in FL else 'act', XS[:, 16:18], EZ[:, c, T:T + 2])
                        S.copy('pool' if 'carrypool' in FL else 'act', EZ[:, c, 0:2], XS[:, 16:18])
                        S.copy('pool' if 'carrypool' in FL else 'act', XS[:, 32:62], ED[:, c, T:T + 30])
                        S.copy('pool' if 'carrypool' in FL else 'act', ED[:, c, 0:30], XS[:, 32:62])

            def rms(bi, t0, n):
                for c in range(8):
                    if 'rmsdve' in FL and c % 2 == 1:
                        S.tt('dve', SQ[:, c, 0:n], Y[:, c, 0:n], Y[:, c, 0:n], ALU.mult)
                    else:
                        S.act(SQ[:, c, 0:n], Y[:, c, 0:n], AF.Square)
                for g in range(4):
                    pr = next_hb()[:, 0:n]
                    for c in range(2):
                        S.mm(pr, ONESB[:, :], SQ[:, 2 * g + c, 0:n], start=(c == 0), stop=(c == 1))
                    rg = MR[:, g % 2, 0:n]
                    S.act(rg, pr, AF.Ln, bias=EPS[:, 1:2], scale=1.0)
                    S.act(rg, rg, AF.Exp, scale=-0.5)
                    for c in (2 * g, 2 * g + 1):
                        S.stt('dve', XM[:, c, t0:t0 + n], Y[:, c, 0:n], cc('out_norm_g', l * 8 + c), rg,
                              ALU.mult, ALU.mult)

            def back(bi, t0, n):
                for oc in range(8):
                    pt = next_hb()[:, 0:n]
                    q, r = divmod(oc, 4)
                    for k in range(8):
                        S.mm(pt, Wp[4 + q][:, k, r * 128:(r + 1) * 128], XM[:, k, t0:t0 + n], start=(k == 0), stop=(k == 7))
                    accumulate(pt, l, 1, oc, t0, n)
                ln_block(k_next, t0, n)

            nb = len(MIX_BLOCKS)
            for bi, (t0, n) in enumerate(MIX_BLOCKS):
                front(bi, t0, n)
                if bi > 0:
                    back(bi - 1, *MIX_BLOCKS[bi - 1])
                rms(bi, t0, n)
            back(nb - 1, *MIX_BLOCKS[nb - 1])

        def finish():
            for (t0_, n_) in MIX_BLOCKS:
                for c in range(8):
                    S.dma('sp', yT[c][:, t0_:t0_ + n_], X[:, c, t0_:t0_ + n_])
            if stop is not None:
                S.dma('sp', o_ada, ADA[:, :, :, :].rearrange("p l j s -> p (l j s)"))
            S.emit(es)

        stages = []
        if stop in ('setup', 'ada'):
            finish()
            return nc
        import os
        _sel = os.environ.get('LNBLK')
        _blks = MIX_BLOCKS if _sel is None else [MIX_BLOCKS[int(q)] for q in _sel.split(',')]
        stages.append(('ln0', lambda: [ln_block(0, t0, n) for (t0, n) in _blks]))
        for l in range(nlayers):
            stages.append(('ffn1_%d' % l, lambda l=l: ffn(l, 0)))
            stages.append(('mixer_%d' % l, lambda l=l: mixer(l)))
            stages.append(('ffn2_%d' % l, lambda l=l: ffn(l, 1)))
        for name, fn in stages:
            fn()
            if stop == name:
                break
        finish()
    return nc

_PROG_CACHE = {}


def _fm(a, rows):
    return np.ascontiguousarray(np.asarray(a, np.float32).reshape(rows, 128).T)


def _const_tables():
    inv = np.zeros((128, 2, 16), np.float32)
    wins = (2, 4, 8, 16)
    for c in range(2):
        for p in range(128):
            w = wins[2 * c + p // 64]
            for pos in range(16):
                inv[p, c, pos] = 1.0 / min(w, pos + 1)
    m0 = np.triu(np.ones((128, 128), np.float32))
    m1 = np.kron(np.eye(16, dtype=np.float32), np.triu(np.ones((8, 8), np.float32)))
    return inv.reshape(128, 32), np.stack([m0, m1]).astype(np.float32)


def _make_in_maps(x_prompt, x_sample, state_pool, state_sconv, state_dconv, c_prompt, c_sample,
           ln_in_g, ln_in_b, w_ada, b_ada, ffn1_w_gate, ffn1_w_up, ffn1_w_down, w_in,
           sgu_ln_g, sgu_ln_b, sgu_w, sgu_b, pool_w, pool_scale, sconv_w, dconv_w, dconv_b,
           conv_ln_g, conv_ln_b, out_norm_g, w_out, ffn2_w_gate, ffn2_w_up, ffn2_w_down,
           post_ln_g, post_ln_b):
    f32 = lambda a: np.ascontiguousarray(np.asarray(a, dtype=np.float32))
    x_prompt = f32(x_prompt); x_sample = f32(x_sample)
    inv, masks = _const_tables()
    consts = np.zeros((128, NCONST), np.float32)

    def put(name, arr):
        consts[:, COFF[name]:COFF[name] + arr.shape[1]] = arr
    put('ln_in_g', _fm(ln_in_g, 8)); put('ln_in_b', _fm(ln_in_b, 8))
    put('post_g', _fm(post_ln_g, 48)); put('post_b', _fm(post_ln_b, 48))
    put('out_norm_g', _fm(out_norm_g, 16)); put('pool_scale', _fm(pool_scale, 4))
    put('sconv_w', f32(sconv_w).reshape(2, 3, 2, 128).transpose(3, 0, 2, 1).reshape(128, 12))
    put('dconv_w', f32(dconv_w).reshape(2, 31, 2, 128).transpose(3, 0, 2, 1).reshape(128, 124))
    put('dconv_b', _fm(dconv_b, 4)); put('conv_ln_g', _fm(conv_ln_g, 4)); put('conv_ln_b', _fm(conv_ln_b, 4))
    put('b_ada', _fm(b_ada, 144))
    put('invcnt', inv)
    put('idn32', np.tile(np.eye(32, dtype=np.float32), (4, 1)))

    shared = {
        "consts": consts,
        "w_ada": f32(w_ada),
        "ffn1_w_gate": f32(ffn1_w_gate), "ffn1_w_up": f32(ffn1_w_up), "ffn1_w_down": f32(ffn1_w_down),
        "ffn2_w_gate": f32(ffn2_w_gate), "ffn2_w_up": f32(ffn2_w_up), "ffn2_w_down": f32(ffn2_w_down),
        "w_in": f32(w_in), "w_out": f32(w_out),
        "sgu_wT": f32(np.asarray(sgu_w).transpose(0, 1, 3, 2)),
        "sgu_b": f32(sgu_b),
        "sgu_ln": f32(np.stack([np.asarray(sgu_ln_g), np.asarray(sgu_ln_b)], axis=1)),
        "pool_w": f32(pool_w),
        "masks": masks,
    }
    state_pool = np.asarray(state_pool); state_sconv = np.asarray(state_sconv); state_dconv = np.asarray(state_dconv)
    c_prompt = np.asarray(c_prompt); c_sample = np.asarray(c_sample)

    def st_fm(st, i):
        a = st[:, 16 * i:16 * i + 16]
        L_, _, R, _ = a.shape
        a = a.reshape(L_, 16, R, 2, 128).transpose(0, 4, 3, 1, 2)
        return f32(a)

    in_maps = []
    for i in range(8):
        xall = np.concatenate([x_prompt[i], x_sample[16 * i:16 * i + 16].reshape(128, 1024)], axis=0)
        call = np.concatenate([c_prompt[i:i + 1], c_sample[16 * i:16 * i + 16]], axis=0)
        m = dict(shared)
        m["xT"] = f32(xall.T.reshape(8, 128, NT))
        m["cT"] = f32(call.reshape(17, 8, 128).transpose(2, 1, 0))
        m["st_pool"] = st_fm(state_pool, i)
        m["st_sconv"] = st_fm(state_sconv, i)
        m["st_dconv"] = st_fm(state_dconv, i)
        in_maps.append(m)

    return in_maps


def _gather(R):
    y_prompt = np.zeros((8, 2048, 1024), np.float32)
    y_sample = np.zeros((128, 8, 1024), np.float32)
    sgu_p = np.zeros((DEPTH, 8, 128, 256), np.float32)
    sgu_s = np.zeros((DEPTH, 128, 8, 256), np.float32)
    pool_p = np.zeros((DEPTH, 8, 15, 256), np.float32)
    pool_s = np.zeros((DEPTH, 128, 15, 256), np.float32)
    sconv_p = np.zeros((DEPTH, 8, 2, 256), np.float32)
    sconv_s = np.zeros((DEPTH, 128, 2, 256), np.float32)
    dconv_p = np.zeros((DEPTH, 8, 30, 256), np.float32)
    dconv_s = np.zeros((DEPTH, 128, 30, 256), np.float32)
    for i in range(8):
        r = R[i]
        yt = np.asarray(r["yT"]).reshape(1024, NT)
        y_prompt[i] = yt[:, :2048].T
        y_sample[16 * i:16 * i + 16] = yt[:, 2048:].T.reshape(16, 8, 1024)
        sgu_p[:, i] = np.asarray(r["o_sgu_p"])
        sgu_s[:, 16 * i:16 * i + 16] = np.asarray(r["o_sgu_s"]).reshape(DEPTH, 16, 8, 256)
        for (op_, os_, dp, ds, rows) in (("o_pool_p", "o_pool_s", pool_p, pool_s, 15),
                                         ("o_sconv_p", "o_sconv_s", sconv_p, sconv_s, 2),
                                         ("o_dconv_p", "o_dconv_s", dconv_p, dconv_s, 30)):
            a = np.asarray(r[op_])
            dp[:, i] = a.transpose(0, 3, 2, 1).reshape(DEPTH, rows, 256)
            b = np.asarray(r[os_])
            ds[:, 16 * i:16 * i + 16] = b.transpose(0, 3, 4, 2, 1).reshape(DEPTH, 16, rows, 256)
    return (y_prompt, y_sample, sgu_p, sgu_s, pool_p, pool_s, sconv_p, sconv_s, dconv_p, dconv_s)


def kernel(x_prompt, x_sample, state_pool, state_sconv, state_dconv, c_prompt, c_sample,
           ln_in_g, ln_in_b, w_ada, b_ada, ffn1_w_gate, ffn1_w_up, ffn1_w_down, w_in,
           sgu_ln_g, sgu_ln_b, sgu_w, sgu_b, pool_w, pool_scale, sconv_w, dconv_w, dconv_b,
           conv_ln_g, conv_ln_b, out_norm_g, w_out, ffn2_w_gate, ffn2_w_up, ffn2_w_down,
           post_ln_g, post_ln_b):
    in_maps = _make_in_maps(x_prompt, x_sample, state_pool, state_sconv, state_dconv, c_prompt, c_sample,
           ln_in_g, ln_in_b, w_ada, b_ada, ffn1_w_gate, ffn1_w_up, ffn1_w_down, w_in,
           sgu_ln_g, sgu_ln_b, sgu_w, sgu_b, pool_w, pool_scale, sconv_w, dconv_w, dconv_b,
           conv_ln_g, conv_ln_b, out_norm_g, w_out, ffn2_w_gate, ffn2_w_up, ffn2_w_down,
           post_ln_g, post_ln_b)
    if "nc" not in _PROG_CACHE:
        _PROG_CACHE["nc"] = build_program()
    nc = _PROG_CACHE["nc"]
    res = run_bass_kernel_spmd(nc, in_maps, core_ids=list(range(8)))
    return _gather(res.results)
```
